# Optimizing a Trainium2 kernel written in Bass

```python
import jax
import jax.numpy as jnp
from jax import lax
import numpy as np

D_MODEL = 4096
BATCH = 4
SEQ = 4096
DEPTH = 2

HEAD_DIM = 128
ROPE_DIM = HEAD_DIM // 4
ROPE_THETA = 500000.0
NORM_EPS = 1e-6
NEG_INF = -1e30

NSA_HEADS = D_MODEL // (2 * HEAD_DIM)
NSA_KV_HEADS = 4
NSA_GROUP = NSA_HEADS // NSA_KV_HEADS
CMP_BLOCK = 32
CMP_STRIDE = 16
SEL_BLOCK = 64
SEL_TOPN = 16
WINDOW = 512
NSA_QCHUNK = 32
FORCED_SCORE = 1e4

RWKV_WIDTH = D_MODEL // 2
RWKV_HEAD = 64
RWKV_HEADS = RWKV_WIDTH // RWKV_HEAD
LORA_DECAY = max(32, int(round(1.8 * RWKV_WIDTH ** 0.5 / 32)) * 32)
LORA_AAA = max(32, int(round(1.8 * RWKV_WIDTH ** 0.5 / 32)) * 32)
LORA_GATE = max(32, int(round(0.6 * RWKV_WIDTH ** 0.8 / 32)) * 32)
GN_EPS = 64e-5

DSA_HEADS = D_MODEL // HEAD_DIM
DSA_KV_HEADS = 4
DSA_GROUP = DSA_HEADS // DSA_KV_HEADS
IDX_HEADS = 32
IDX_DIM = 128
DSA_TOPK_MAX = 256
DSA_QCHUNK = 128

MEM_TOKENS = 256
MEM_HEADS = 4
MEM_WIDTH = MEM_HEADS * HEAD_DIM

D_FF = -(-(8 * D_MODEL) // (3 * 256)) * 256

NSA_Q = NSA_HEADS * HEAD_DIM
NSA_KV = NSA_KV_HEADS * HEAD_DIM
NSA_SPLITS = (NSA_Q, NSA_KV, NSA_KV, NSA_KV, NSA_KV, NSA_KV, NSA_KV, 3 * NSA_HEADS)
NSA_COLS = sum(NSA_SPLITS)
RWKV_SPLITS = (RWKV_WIDTH, RWKV_WIDTH, RWKV_WIDTH, LORA_DECAY, LORA_AAA, LORA_GATE)
RWKV_COLS = sum(RWKV_SPLITS)
EVEN_IN = NSA_COLS + RWKV_COLS
EVEN_OUT = NSA_Q + RWKV_WIDTH
DSA_SPLITS = (DSA_HEADS * HEAD_DIM, DSA_KV_HEADS * HEAD_DIM, DSA_KV_HEADS * HEAD_DIM,
              IDX_HEADS * IDX_DIM, IDX_DIM, IDX_HEADS)
ODD_IN = sum(DSA_SPLITS)
ODD_OUT = DSA_HEADS * HEAD_DIM

kernel_name = "hybrid_nsa_rwkv7_dsa_memory_trunk"


def rms_norm(x, g, eps=NORM_EPS):
    xf = x.astype(jnp.float32)
    y = xf * lax.rsqrt(jnp.mean(xf * xf, axis=-1, keepdims=True) + eps)
    return (y * g.astype(jnp.float32)).astype(x.dtype)


def split_cols(y, sizes):
    bounds = [int(b) for b in np.cumsum(sizes)[:-1]]
    return jnp.split(y, bounds, axis=-1)


def rope_tables(pos):
    inv = ROPE_THETA ** (-jnp.arange(0, ROPE_DIM, 2, dtype=jnp.float32) / ROPE_DIM)
    ang = pos.astype(jnp.float32)[..., None] * inv
    return jnp.cos(ang)[:, :, None, :], jnp.sin(ang)[:, :, None, :]


def partial_rope(x, cos, sin):
    half = ROPE_DIM // 2
    x1 = x[..., :half].astype(jnp.float32)
    x2 = x[..., half:ROPE_DIM].astype(jnp.float32)
    rot = jnp.concatenate([x1 * cos - x2 * sin, x2 * cos + x1 * sin], axis=-1)
    return jnp.concatenate([rot.astype(x.dtype), x[..., ROPE_DIM:]], axis=-1)


def compress_blocks(kv, pe, w1, w2):
    B, T, H, D = kv.shape
    ch = kv.reshape(B, T // CMP_STRIDE, CMP_STRIDE, H, D)
    blocks = jnp.concatenate([ch[:, :-1], ch[:, 1:]], axis=2) + pe[None, None, :, None, :]
    n_cmp = blocks.shape[1]
    flat = blocks.transpose(0, 1, 3, 2, 4).reshape(B, n_cmp, H, CMP_BLOCK * D)
    return jax.nn.gelu(flat @ w1) @ w2


def cmp_to_sel_matrix(n_cmp, n_sel):
    cs = CMP_STRIDE * np.arange(n_cmp)[:, None]
    ss = SEL_BLOCK * np.arange(n_sel)[None, :]
    ov = np.clip(np.minimum(cs + CMP_BLOCK, ss + SEL_BLOCK) - np.maximum(cs, ss), 0, None)
    return jnp.asarray(ov / CMP_BLOCK, dtype=jnp.float32)


def nsa_attention(pa, cos, sin, cmp_cos, cmp_sin, q_norm, kc_norm, ks_norm, kw_norm,
                  pe_k, pe_v, ck_w1, ck_w2, cv_w1, cv_w2):
    B, T, _ = pa.shape
    f32 = jnp.float32
    scale = HEAD_DIM ** -0.5
    q, kc, vc, ks, vs, kw, vw, gates = split_cols(pa, NSA_SPLITS)
    kv_shape = (B, T, NSA_KV_HEADS, HEAD_DIM)
    q = partial_rope(rms_norm(q.reshape(B, T, NSA_HEADS, HEAD_DIM), q_norm), cos, sin)
    qg = q.reshape(B, T, NSA_KV_HEADS, NSA_GROUP, HEAD_DIM)
    ks = partial_rope(rms_norm(ks.reshape(kv_shape), ks_norm), cos, sin)
    kw = partial_rope(rms_norm(kw.reshape(kv_shape), kw_norm), cos, sin)
    vs = vs.reshape(kv_shape)
    vw = vw.reshape(kv_shape)
    t = jnp.arange(T)

    kc = compress_blocks(kc.reshape(kv_shape), pe_k, ck_w1, ck_w2)
    kc = partial_rope(rms_norm(kc, kc_norm), cmp_cos, cmp_sin)
    vc = compress_blocks(vc.reshape(kv_shape), pe_v, cv_w1, cv_w2)
    n_cmp = kc.shape[1]
    cmp_end = CMP_STRIDE * jnp.arange(n_cmp) + CMP_BLOCK - 1
    cmp_ok = cmp_end[None, :] <= t[:, None]
    s_c = jnp.einsum('btkgd,bckd->bkgtc', qg, kc).astype(f32) * scale
    p_c = jax.nn.softmax(jnp.where(cmp_ok, s_c, NEG_INF), axis=-1)
    p_c = p_c * (t >= CMP_BLOCK - 1)[:, None]
    o_c = jnp.einsum('bkgtc,bckd->btkgd', p_c.astype(vc.dtype), vc)

    n_sel = T // SEL_BLOCK
    imp = jnp.einsum('bkgtc,cj->btkj', p_c, cmp_to_sel_matrix(n_cmp, n_sel))
    jt = (t // SEL_BLOCK)[:, None]
    jj = jnp.arange(n_sel)[None, :]
    forced = (jj == 0) | (jj == jt) | (jj == jt - 1)
    imp = jnp.where(forced[None, :, None, :], FORCED_SCORE, imp)
    imp = jnp.where((jj <= jt)[None, :, None, :], imp, -jnp.inf)
    _, sel_idx = lax.top_k(imp, min(SEL_TOPN, n_sel))

    ks_blk = ks.reshape(B, n_sel, SEL_BLOCK, NSA_KV_HEADS, HEAD_DIM).transpose(0, 3, 1, 2, 4)
    vs_blk = vs.reshape(B, n_sel, SEL_BLOCK, NSA_KV_HEADS, HEAD_DIM).transpose(0, 3, 1, 2, 4)
    pad = ((0, 0), (WINDOW, 0), (0, 0), (0, 0))
    kw_pad = jnp.pad(kw, pad)
    vw_pad = jnp.pad(vw, pad)
    b_ix = jnp.arange(B)[:, None, None, None]
    h_ix = jnp.arange(NSA_KV_HEADS)[None, None, :, None]
    blk_off = jnp.arange(SEL_BLOCK)
    win_off = jnp.arange(WINDOW + NSA_QCHUNK)

    def chunk(c):
        t0 = c * NSA_QCHUNK
        tq = t0 + jnp.arange(NSA_QCHUNK)
        q_c = lax.dynamic_slice_in_dim(qg, t0, NSA_QCHUNK, axis=1)
        idx = lax.dynamic_slice_in_dim(sel_idx, t0, NSA_QCHUNK, axis=1)
        k_sel = ks_blk[b_ix, h_ix, idx]
        v_sel = vs_blk[b_ix, h_ix, idx]
        kpos = idx[..., None] * SEL_BLOCK + blk_off
        ok = (kpos <= tq[None, :, None, None, None])[:, :, :, None]
        s = jnp.einsum('bqkgd,bqknsd->bqkgns', q_c, k_sel).astype(f32) * scale
        p = jax.nn.softmax(jnp.where(ok, s, NEG_INF), axis=(-2, -1))
        o_s = jnp.einsum('bqkgns,bqknsd->bqkgd', p.astype(v_sel.dtype), v_sel)
        k_w = lax.dynamic_slice_in_dim(kw_pad, t0, WINDOW + NSA_QCHUNK, axis=1)
        v_w = lax.dynamic_slice_in_dim(vw_pad, t0, WINDOW + NSA_QCHUNK, axis=1)
        spos = t0 - WINDOW + win_off
        diff = tq[:, None] - spos[None, :]
        okw = (spos[None, :] >= 0) & (diff >= 0) & (diff < WINDOW)
        s = jnp.einsum('bqkgd,bskd->bqkgs', q_c, k_w).astype(f32) * scale
        p = jax.nn.softmax(jnp.where(okw[None, :, None, None, :], s, NEG_INF), axis=-1)
        o_w = jnp.einsum('bqkgs,bskd->bqkgd', p.astype(v_w.dtype), v_w)
        return o_s, o_w

    o_s, o_w = lax.map(chunk, jnp.arange(T // NSA_QCHUNK))
    heads = (B, T, NSA_HEADS, HEAD_DIM)
    o_s = jnp.moveaxis(o_s, 0, 1).reshape(heads)
    o_w = jnp.moveaxis(o_w, 0, 1).reshape(heads)
    o_c = o_c.reshape(heads)
    g = jax.nn.sigmoid(gates.reshape(B, T, NSA_HEADS, 3).astype(f32)).astype(pa.dtype)
    o = g[..., 0:1] * o_c + g[..., 1:2] * o_s + g[..., 2:3] * o_w
    return o.reshape(B, T, NSA_Q)


def rwkv7_time_mix(pb, mu, w0, w2, a0, a2, g2, kk_gain, k_a, r_k, gn_g, gn_b):
    B, T, _ = pb.shape
    f32 = jnp.float32
    prev = jnp.pad(pb, ((0, 0), (1, 0), (0, 0)))[:, :T]
    xs = pb + (prev - pb) * mu
    r, k, v, wd, ad, gd = split_cols(xs, RWKV_SPLITS)
    w_log = -jax.nn.softplus(-(w0 + jnp.tanh(wd) @ w2).astype(f32)) - 0.5
    a = jax.nn.sigmoid((a0 + ad @ a2).astype(f32))
    g = jax.nn.sigmoid(gd) @ g2
    heads = lambda z: z.reshape(B, T, RWKV_HEADS, RWKV_HEAD).astype(f32)
    kk = heads(k * kk_gain)
    kk = kk / jnp.maximum(jnp.linalg.norm(kk, axis=-1, keepdims=True), 1e-12)
    k = k * (1.0 + (a - 1.0) * k_a)
    r_h, k_h, v_h, a_h = heads(r), heads(k), heads(v), heads(a)
    decay = jnp.exp(-jnp.exp(heads(w_log)))

    def step(S, inp):
        r_t, w_t, k_t, v_t, kk_t, b_t = inp
        sa = jnp.einsum('bhij,bhj->bhi', S, -kk_t)
        S = (S * w_t[:, :, None, :] + sa[..., None] * b_t[:, :, None, :]
             + v_t[..., None] * k_t[:, :, None, :])
        return S, jnp.einsum('bhij,bhj->bhi', S, r_t)

    tm = lambda z: jnp.moveaxis(z, 1, 0)
    S0 = jnp.zeros((B, RWKV_HEADS, RWKV_HEAD, RWKV_HEAD), f32)
    _, y = lax.scan(step, S0, (tm(r_h), tm(decay), tm(k_h), tm(v_h), tm(kk), tm(kk * a_h)))
    y = jnp.moveaxis(y, 0, 1)
    mean = jnp.mean(y, axis=-1, keepdims=True)
    var = jnp.mean(jnp.square(y - mean), axis=-1, keepdims=True)
    y = ((y - mean) * lax.rsqrt(var + GN_EPS)).reshape(B, T, RWKV_WIDTH) * gn_g + gn_b
    bonus = jnp.sum(r_h * k_h * r_k, axis=-1, keepdims=True) * v_h
    y = y + bonus.reshape(B, T, RWKV_WIDTH)
    return (y * g).astype(pb.dtype)


def nsa_rwkv_mixer(h, cos, sin, cmp_cos, cmp_sin, w_in, w_out, nsa_w, rwkv_w):
    proj = h @ w_in
    o_a = nsa_attention(proj[..., :NSA_COLS], cos, sin, cmp_cos, cmp_sin, *nsa_w)
    o_b = rwkv7_time_mix(proj[..., NSA_COLS:], *rwkv_w)
    return jnp.concatenate([o_a, o_b], axis=-1) @ w_out


def dsa_attention(po, cos, sin, q_norm, k_norm, ki_norm):
    B, T, _ = po.shape
    f32 = jnp.float32
    scale = HEAD_DIM ** -0.5
    q, k, v, qi, ki, wi = split_cols(po, DSA_SPLITS)
    q = partial_rope(rms_norm(q.reshape(B, T, DSA_HEADS, HEAD_DIM), q_norm), cos, sin)
    qg = q.reshape(B, T, DSA_KV_HEADS, DSA_GROUP, HEAD_DIM)
    k = partial_rope(rms_norm(k.reshape(B, T, DSA_KV_HEADS, HEAD_DIM), k_norm), cos, sin)
    v = v.reshape(B, T, DSA_KV_HEADS, HEAD_DIM)
    qi = partial_rope(qi.reshape(B, T, IDX_HEADS, IDX_DIM), cos, sin)
    ki = partial_rope(rms_norm(ki[:, :, None, :], ki_norm), cos, sin)[:, :, 0]
    wi = wi.astype(f32) * IDX_HEADS ** -0.5
    topk = min(DSA_TOPK_MAX, T // 4)
    b_ix = jnp.arange(B)[:, None, None]
    key_pos = jnp.arange(T)

    def chunk(c):
        t0 = c * DSA_QCHUNK
        tq = t0 + jnp.arange(DSA_QCHUNK)
        qi_c = lax.dynamic_slice_in_dim(qi, t0, DSA_QCHUNK, axis=1)
        wi_c = lax.dynamic_slice_in_dim(wi, t0, DSA_QCHUNK, axis=1)
        logits = jnp.einsum('bqhd,bsd->bqhs', qi_c, ki).astype(f32) * IDX_DIM ** -0.5
        score = jnp.einsum('bqh,bqhs->bqs', wi_c, jax.nn.relu(logits))
        score = jnp.where(key_pos[None, None, :] <= tq[None, :, None], score, -jnp.inf)
        _, sel = lax.top_k(score, topk)
        k_sel = k[b_ix, sel]
        v_sel = v[b_ix, sel]
        q_c = lax.dynamic_slice_in_dim(qg, t0, DSA_QCHUNK, axis=1)
        s = jnp.einsum('bqkgd,bqnkd->bqkgn', q_c, k_sel).astype(f32) * scale
        ok = (sel <= tq[None, :, None])[:, :, None, None, :]
        p = jax.nn.softmax(jnp.where(ok, s, NEG_INF), axis=-1)
        return jnp.einsum('bqkgn,bqnkd->bqkgd', p.astype(v_sel.dtype), v_sel)

    o = lax.map(chunk, jnp.arange(T // DSA_QCHUNK))
    return jnp.moveaxis(o, 0, 1).reshape(B, T, ODD_OUT)


def dsa_mixer(h, cos, sin, w_in, w_out, q_norm, k_norm, ki_norm):
    return dsa_attention(h @ w_in, cos, sin, q_norm, k_norm, ki_norm) @ w_out


def memory_kv(mem, mem_norm, w_kv, k_norm):
    B, M, _ = mem.shape
    mk, mv = split_cols(rms_norm(mem, mem_norm) @ w_kv, (MEM_WIDTH, MEM_WIDTH))
    mk = rms_norm(mk.reshape(B, M, MEM_HEADS, HEAD_DIM), k_norm)
    return mk, mv.reshape(B, M, MEM_HEADS, HEAD_DIM)


def memory_xattn(h, mem_k, mem_v, wq, q_norm, wo):
    B, T, _ = h.shape
    q = rms_norm((h @ wq).reshape(B, T, MEM_HEADS, HEAD_DIM), q_norm)
    s = jnp.einsum('bthd,bmhd->bhtm', q, mem_k).astype(jnp.float32) * HEAD_DIM ** -0.5
    p = jax.nn.softmax(s, axis=-1).astype(mem_v.dtype)
    o = jnp.einsum('bhtm,bmhd->bthd', p, mem_v).reshape(B, T, MEM_WIDTH)
    return o @ wo


def swiglu(h, w1, w3, w2):
    return (jax.nn.silu(h @ w1) * (h @ w3)) @ w2


def setup_inputs(seed: int = 0) -> dict:
    key = jax.random.key(seed)
    keys = iter(jax.random.split(key, 64))
    f32 = jnp.float32

    def dense(shape):
        return jax.random.normal(next(keys), shape, f32) * shape[0] ** -0.5

    def gain(shape):
        return 1.0 + 0.02 * jax.random.normal(next(keys), shape, f32)

    def small(shape, s):
        return s * jax.random.normal(next(keys), shape, f32)

    x = jax.random.normal(next(keys), (BATCH, SEQ, D_MODEL), f32)
    mem = jax.random.normal(next(keys), (BATCH, MEM_TOKENS, D_MODEL), f32)
    steps = jax.random.randint(next(keys), (BATCH, SEQ), 1, 3, dtype=jnp.int32)
    positions = jnp.cumsum(steps, axis=1, dtype=jnp.int32) - 1
    return {
        "x": x,
        "mem": mem,
        "positions": positions,
        "mem_norm": gain((D_MODEL,)),
        "mem_w_kv": dense((D_MODEL, 2 * MEM_WIDTH)),
        "mem_k_norm": gain((HEAD_DIM,)),
        "l0_mix_norm": gain((D_MODEL,)),
        "l0_w_in": dense((D_MODEL, EVEN_IN)),
        "l0_w_out": dense((EVEN_OUT, D_MODEL)),
        "nsa_q_norm": gain((HEAD_DIM,)),
        "nsa_kc_norm": gain((HEAD_DIM,)),
        "nsa_ks_norm": gain((HEAD_DIM,)),
        "nsa_kw_norm": gain((HEAD_DIM,)),
        "nsa_pe_k": small((CMP_BLOCK, HEAD_DIM), 0.02),
        "nsa_pe_v": small((CMP_BLOCK, HEAD_DIM), 0.02),
        "nsa_ck_w1": dense((CMP_BLOCK * HEAD_DIM, HEAD_DIM)),
        "nsa_ck_w2": dense((HEAD_DIM, HEAD_DIM)),
        "nsa_cv_w1": dense((CMP_BLOCK * HEAD_DIM, HEAD_DIM)),
        "nsa_cv_w2": dense((HEAD_DIM, HEAD_DIM)),
        "rwkv_mu": jax.random.uniform(next(keys), (RWKV_COLS,), f32),
        "rwkv_w0": jax.random.uniform(next(keys), (RWKV_WIDTH,), f32, -3.0, 1.0),
        "rwkv_w2": dense((LORA_DECAY, RWKV_WIDTH)),
        "rwkv_a0": small((RWKV_WIDTH,), 0.5),
        "rwkv_a2": dense((LORA_AAA, RWKV_WIDTH)),
        "rwkv_g2": dense((LORA_GATE, RWKV_WIDTH)),
        "rwkv_kk": gain((RWKV_WIDTH,)),
        "rwkv_ka": gain((RWKV_WIDTH,)),
        "rwkv_rk": small((RWKV_HEADS, RWKV_HEAD), 0.1),
        "rwkv_gn_g": gain((RWKV_WIDTH,)),
        "rwkv_gn_b": small((RWKV_WIDTH,), 0.02),
        "l0_xattn_norm": gain((D_MODEL,)),
        "l0_mem_wq": dense((D_MODEL, MEM_WIDTH)),
        "l0_mem_q_norm": gain((HEAD_DIM,)),
        "l0_mem_wo": dense((MEM_WIDTH, D_MODEL)),
        "l0_ffn_norm": gain((D_MODEL,)),
        "l0_w1": dense((D_MODEL, D_FF)),
        "l0_w3": dense((D_MODEL, D_FF)),
        "l0_w2": dense((D_FF, D_MODEL)),
        "l1_mix_norm": gain((D_MODEL,)),
        "l1_w_in": dense((D_MODEL, ODD_IN)),
        "l1_w_out": dense((ODD_OUT, D_MODEL)),
        "dsa_q_norm": gain((HEAD_DIM,)),
        "dsa_k_norm": gain((HEAD_DIM,)),
        "dsa_ki_norm": gain((IDX_DIM,)),
        "l1_xattn_norm": gain((D_MODEL,)),
        "l1_mem_wq": dense((D_MODEL, MEM_WIDTH)),
        "l1_mem_q_norm": gain((HEAD_DIM,)),
        "l1_mem_wo": dense((MEM_WIDTH, D_MODEL)),
        "l1_ffn_norm": gain((D_MODEL,)),
        "l1_w1": dense((D_MODEL, D_FF)),
        "l1_w3": dense((D_MODEL, D_FF)),
        "l1_w2": dense((D_FF, D_MODEL)),
    }


def reference(x, mem, positions, mem_norm, mem_w_kv, mem_k_norm,
              l0_mix_norm, l0_w_in, l0_w_out,
              nsa_q_norm, nsa_kc_norm, nsa_ks_norm, nsa_kw_norm, nsa_pe_k, nsa_pe_v,
              nsa_ck_w1, nsa_ck_w2, nsa_cv_w1, nsa_cv_w2,
              rwkv_mu, rwkv_w0, rwkv_w2, rwkv_a0, rwkv_a2, rwkv_g2, rwkv_kk, rwkv_ka,
              rwkv_rk, rwkv_gn_g, rwkv_gn_b,
              l0_xattn_norm, l0_mem_wq, l0_mem_q_norm, l0_mem_wo,
              l0_ffn_norm, l0_w1, l0_w3, l0_w2,
              l1_mix_norm, l1_w_in, l1_w_out, dsa_q_norm, dsa_k_norm, dsa_ki_norm,
              l1_xattn_norm, l1_mem_wq, l1_mem_q_norm, l1_mem_wo,
              l1_ffn_norm, l1_w1, l1_w3, l1_w2):
    T = x.shape[1]
    cos, sin = rope_tables(positions)
    n_cmp = T // CMP_STRIDE - 1
    cmp_pos = positions[:, CMP_STRIDE * np.arange(n_cmp) + CMP_BLOCK - 1]
    cmp_cos, cmp_sin = rope_tables(cmp_pos)
    mem_k, mem_v = memory_kv(mem, mem_norm, mem_w_kv, mem_k_norm)

    nsa_w = (nsa_q_norm, nsa_kc_norm, nsa_ks_norm, nsa_kw_norm, nsa_pe_k, nsa_pe_v,
             nsa_ck_w1, nsa_ck_w2, nsa_cv_w1, nsa_cv_w2)
    rwkv_w = (rwkv_mu, rwkv_w0, rwkv_w2, rwkv_a0, rwkv_a2, rwkv_g2, rwkv_kk, rwkv_ka,
              rwkv_rk, rwkv_gn_g, rwkv_gn_b)
    xattn_w = ((l0_xattn_norm, l0_mem_wq, l0_mem_q_norm, l0_mem_wo),
               (l1_xattn_norm, l1_mem_wq, l1_mem_q_norm, l1_mem_wo))
    ffn_w = ((l0_ffn_norm, l0_w1, l0_w3, l0_w2),
             (l1_ffn_norm, l1_w1, l1_w3, l1_w2))

    for layer in range(DEPTH):
        if layer % 2 == 0:
            x = x + nsa_rwkv_mixer(rms_norm(x, l0_mix_norm), cos, sin, cmp_cos, cmp_sin,
                                   l0_w_in, l0_w_out, nsa_w, rwkv_w)
        else:
            x = x + dsa_mixer(rms_norm(x, l1_mix_norm), cos, sin, l1_w_in, l1_w_out,
                              dsa_q_norm, dsa_k_norm, dsa_ki_norm)
        xn, wq, qn, wo = xattn_w[layer]
        x = x + memory_xattn(rms_norm(x, xn), mem_k, mem_v, wq, qn, wo)
        fn, w1, w3, w2 = ffn_w[layer]
        x = x + swiglu(rms_norm(x, fn), w1, w3, w2)
    return x
```

```python
import contextlib, time, math
import numpy as np
import concourse.bass as bass
import concourse.mybir as mybir
from concourse.bass_utils import run_bass_kernel_spmd

F32 = mybir.dt.float32; BF16 = mybir.dt.bfloat16; I32 = mybir.dt.int32; U32 = mybir.dt.uint32
AF = mybir.ActivationFunctionType; ALU = mybir.AluOpType; AX = mybir.AxisListType


class Prog:
    COMPUTE = ("pe", "act", "dve", "pool")
    NDMA = 8

    def __init__(self, nc):
        self.nc = nc
        self.es = contextlib.ExitStack()
        self.lists = {e: [] for e in ("pe", "act", "dve", "pool", "sp")}
        self.cnt = {e: 0 for e in self.COMPUTE}
        self.sem = {e: self.es.enter_context(nc.semaphore("s_" + e)) for e in self.COMPUTE}
        self.dsem = {q: [self.es.enter_context(nc.semaphore(f"d_{q}{i}")) for i in range(self.NDMA)]
                     for q in ("sp", "pool", "act")}
        self.dval = {q: [0] * self.NDMA for q in ("sp", "pool", "act")}
        self.dnext = {q: 0 for q in ("sp", "pool", "act")}
        self.waited = {e: {} for e in self.lists}
        self.lastw = {}
        self.readers = {}
        self.semobj = {}
        self.ninst = 0

    def sbuf(self, name, shape, dt):
        return self.es.enter_context(self.nc.sbuf_tensor(name, list(shape), dt))

    def psum(self, name, shape, dt=F32):
        return self.es.enter_context(self.nc.psum_tensor(name, list(shape), dt))

    def dram(self, name, shape, dt, kind="Internal"):
        return self.nc.dram_tensor(name, list(shape), dt, kind=kind).ap()

    def _key(self, sem):
        k = id(sem)
        self.semobj[k] = sem
        return k

    def _deps(self, eng, reads, writes):
        deps = {}
        def add(tok):
            if tok is None:
                return
            k, v, src = tok
            if src == "pe" and eng == "pe":
                return
            if deps.get(k, 0) < v:
                deps[k] = v
        for r in reads:
            for tok in self.lastw.get(r, {}).values():
                add(tok)
        for w in writes:
            for tok in self.lastw.get(w, {}).values():
                add(tok)
            for tok in self.readers.get(w, {}).values():
                add(tok)
        out = []
        wd = self.waited[eng]
        for k, v in deps.items():
            if wd.get(k, 0) < v:
                wd[k] = v
                out.append((self.semobj[k], v))
        return out

    def _commit(self, tok, reads, writes):
        for w in writes:
            self.lastw.setdefault(w, {})[tok[0]] = tok
            self.readers[w] = {}
        for r in reads:
            d = self.readers.setdefault(r, {})
            if tok[0] not in d or d[tok[0]][1] < tok[1]:
                d[tok[0]] = tok

    def op(self, eng, fn, reads=(), writes=()):
        waits = self._deps(eng, reads, writes)
        sem = self.sem[eng]
        self.cnt[eng] += 1
        tok = (self._key(sem), self.cnt[eng], eng)
        def emit(e, fn=fn, waits=waits, sem=sem):
            for s, v in waits:
                e.wait_ge(s, v)
            fn(e).then_inc(sem, 1)
        self.lists[eng].append(emit)
        self._commit(tok, reads, writes)
        self.ninst += 1

    def dma(self, q, out, in_, reads=(), writes=(), **kw):
        eng = q
        waits = self._deps(eng, reads, writes)
        i = self.dnext[q]
        self.dnext[q] = (i + 1) % self.NDMA
        sem = self.dsem[q][i]
        k = self._key(sem)
        prev = self.dval[q][i]
        if prev and self.waited[eng].get(k, 0) < prev:
            self.waited[eng][k] = prev
            waits = waits + [(sem, prev)]
        self.dval[q][i] = prev + 16
        tok = (k, prev + 16, "dma")
        def emit(e, waits=waits, sem=sem):
            for s, v in waits:
                e.wait_ge(s, v)
            e.dma_start(out=out, in_=in_, **kw).then_inc(sem, 16)
        self.lists[eng].append(emit)
        self._commit(tok, reads, writes)
        self.ninst += 1

    def finish(self):
        waits = []
        for q in self.dsem:
            for i, s in enumerate(self.dsem[q]):
                if self.dval[q][i]:
                    waits.append((s, self.dval[q][i]))
        for e in self.COMPUTE:
            if self.cnt[e]:
                waits.append((self.sem[e], self.cnt[e]))
        def emit(e):
            for s, v in waits:
                e.wait_ge(s, v)
        self.lists["sp"].append(emit)
        L = self.lists
        with self.nc.Block() as block:
            @block.sync
            def _(e):
                for f in L["sp"]:
                    f(e)
            @block.tensor
            def _(e):
                for f in L["pe"]:
                    f(e)
            @block.scalar
            def _(e):
                for f in L["act"]:
                    f(e)
            @block.vector
            def _(e):
                for f in L["dve"]:
                    f(e)
            @block.gpsimd
            def _(e):
                for f in L["pool"]:
                    f(e)
        self.es.close()


def new_nc():
    return bass.Bass("TRN2", target_bir_lowering=False)

D_MODEL = 4096; BATCH = 4; SEQ = 4096; HD = 128
NSA_HEADS = 16; NSA_KVH = 4; RW = 2048; RH = 64; RHEADS = 32
L_DEC = 96; L_AAA = 96; L_GATE = 256
D_FF = 11008
NSA_Q = 2048; NSA_KV = 512
NSA_COLS = 2048 + 6 * 512 + 48
RWKV_COLS = 3 * 2048 + 96 + 96 + 256
EVEN_IN = NSA_COLS + RWKV_COLS
ODD_IN = 4096 + 512 + 512 + 4096 + 128 + 32
EPS = 1e-6
NEG = -30000.0


def col_tiles(sections):
    out = []
    for s, n in sections:
        o = 0
        while o < n:
            w = min(128, n - o)
            out.append((s + o, w))
            o += w
    return out


EVEN_SECTIONS = [(0, 2048)] + [(2048 + 512 * i, 512) for i in range(6)] + [(5120, 48)] + \
    [(5168 + 2048 * i, 2048) for i in range(3)] + [(11312, 96), (11408, 96), (11504, 256)]
ODD_SECTIONS = [(0, 4096), (4096, 512), (4608, 512), (5120, 4096), (9216, 128), (9344, 32)]


class TL:
    TG = 512

    def __init__(self, P):
        self.P = P
        self.actbuf = P.sbuf("actbuf", [128, 86 * 256], BF16)
        self.wbuf = [P.sbuf(f"wbuf{i}", [128, 86 * 128], BF16) for i in range(2)]
        self.ps = [P.psum(f"ps{i}", [128, 512]) for i in range(8)]
        self.xf = [P.sbuf(f"xf{i}", [128, 512], F32) for i in range(2)]
        self.sq = [P.sbuf(f"sq{i}", [128, 512], F32) for i in range(2)]
        self.rstd = P.sbuf("rstd", [128, 512], F32)
        self.ob = [P.sbuf(f"ob{i}", [128, 512], F32) for i in range(4)]
        self.rb = [P.sbuf(f"rb{i}", [128, 512], F32) for i in range(4)]
        self.ones = P.sbuf("ones_f", [128, 128], F32)
        self.onesb = P.sbuf("ones_b", [128, 128], BF16)
        P.op("dve", lambda e: e.memset(self.ones[:], 1.0), writes=["ones"])
        P.op("dve", lambda e: e.memset(self.onesb[:], 1.0), writes=["onesb"])
        self.wi = 0; self.pi = 0; self.oi = 0; self.ri = 0

    def act_view(self, KC, TG):
        return self.actbuf[:, :KC * TG].rearrange("p (c t) -> p c t", c=KC)

    def next_ps(self):
        i = self.pi % 8; self.pi += 1
        return self.ps[i], ("ps", i)

    def load_gain(self, name, gvec, K):
        P = self.P
        g = P.sbuf("g_" + name, [128, K // 128], F32)
        P.dma("sp", g[:], gvec.rearrange("(c p) -> p c", p=128), writes=["g_" + name], allow_slow_non_contiguous=True)
        return g

    def norm_act(self, xT, xkey, t0, TG, gain, gkey, K=4096):
        P = self.P; KC = K // 128
        act = self.act_view(KC, TG)
        sps, skey = self.next_ps()
        for c in range(KC):
            s = c % 2
            P.dma("sp", self.xf[s][:, :TG], xT[c * 128:(c + 1) * 128, t0:t0 + TG], reads=[xkey], writes=[("xf", s)])
            P.op("act", lambda e, s=s: e.activation(out=self.sq[s][:, :TG], in_=self.xf[s][:, :TG], func=AF.Square),
                 reads=[("xf", s)], writes=[("sq", s)])
            P.op("pe", lambda e, s=s, c=c: e.matmul(sps[:, :TG], lhsT=self.ones[:], rhs=self.sq[s][:, :TG],
                                                    start=(c == 0), stop=(c == KC - 1)),
                 reads=[("sq", s), "ones"], writes=[skey])
        P.op("dve", lambda e: e.tensor_scalar(out=self.rstd[:, :TG], in0=sps[:, :TG], scalar1=1.0 / K, scalar2=EPS,
                                              op0=ALU.mult, op1=ALU.add), reads=[skey], writes=["rstd"])
        P.op("act", lambda e: e.activation(out=self.rstd[:, :TG], in_=self.rstd[:, :TG], func=AF.Sqrt), reads=["rstd"], writes=["rstd"])
        P.op("dve", lambda e: e.reciprocal(out=self.rstd[:, :TG], in_=self.rstd[:, :TG]), reads=["rstd"], writes=["rstd"])
        for c in range(KC):
            s = c % 2
            P.dma("sp", self.xf[s][:, :TG], xT[c * 128:(c + 1) * 128, t0:t0 + TG], reads=[xkey], writes=[("xf", s)])
            P.op("dve", lambda e, s=s, c=c: e.scalar_tensor_tensor(out=act[:, c, :], in0=self.xf[s][:, :TG], scalar=gain[:, c:c + 1],
                                                                  in1=self.rstd[:, :TG], op0=ALU.mult, op1=ALU.mult),
                 reads=[("xf", s), gkey, "rstd"], writes=["actbuf"])

    def cast_act(self, srcT, skey, t0, TG, K):
        KC = K // 128
        act = self.act_view(KC, TG)
        self.P.dma("pool", act, srcT[:, t0:t0 + TG].rearrange("(c p) t -> p c t", p=128), reads=[skey], writes=["actbuf"])

    def gemm(self, jobs, K, TG, evac, act=None, akey="actbuf"):
        P = self.P; KC = K // 128
        if act is None:
            act = self.act_view(KC, TG)
        WG = 256 if KC <= 32 else 128
        groups = []
        for (W, c0, w) in jobs:
            if groups and groups[-1][0] is W and groups[-1][1] + groups[-1][2] == c0 and groups[-1][2] + w <= WG:
                groups[-1][2] += w; groups[-1][3].append((c0, w))
            else:
                groups.append([W, c0, w, [(c0, w)]])
        ji = 0
        for (W, g0, gw, tl) in groups:
            s = self.wi % 2; self.wi += 1
            wkey = ("wbuf", s)
            wv = self.wbuf[s][:, :KC * gw].rearrange("p (c n) -> p c n", c=KC)
            P.dma("pool", wv, W[:, g0:g0 + gw].rearrange("(c p) n -> p c n", p=128), writes=[wkey])
            for (c0, w) in tl:
                ps, pkey = self.next_ps()
                for c in range(KC):
                    P.op("pe", lambda e, c=c, wv=wv, o=c0 - g0, w=w, ps=ps: e.matmul(
                        ps[:w, :TG], lhsT=wv[:, c, o:o + w], rhs=act[:, c, :TG], start=(c == 0), stop=(c == KC - 1)),
                        reads=[wkey, akey], writes=[pkey])
                evac(ji, c0, w, ps, pkey)
                ji += 1

    def evac_store(self, dstT, dkey, t0, TG, resT=None, rkey=None):
        P = self.P
        def evac(ji, c0, w, ps, pkey):
            o = self.oi % 4; self.oi += 1
            if resT is None:
                if o % 2:
                    P.op("act", lambda e: e.copy(out=self.ob[o][:w, :TG], in_=ps[:w, :TG]), reads=[pkey], writes=[("ob", o)])
                else:
                    P.op("dve", lambda e: e.tensor_copy(out=self.ob[o][:w, :TG], in_=ps[:w, :TG]), reads=[pkey], writes=[("ob", o)])
            else:
                r = self.ri % 4; self.ri += 1
                P.dma("sp", self.rb[r][:w, :TG], resT[c0:c0 + w, t0:t0 + TG], reads=[rkey], writes=[("rb", r)])
                P.op("dve", lambda e: e.tensor_tensor(out=self.ob[o][:w, :TG], in0=ps[:w, :TG], in1=self.rb[r][:w, :TG], op=ALU.add),
                     reads=[pkey, ("rb", r)], writes=[("ob", o)])
            P.dma("sp", dstT[c0:c0 + w, t0:t0 + TG], self.ob[o][:w, :TG], reads=[("ob", o)], writes=[dkey])
        return evac


def build_token_local(do_c, do_a, n_in, sections, TN=2048):
    nc = new_nc()
    P = Prog(nc)
    tl = TL(P)
    TG = 512
    din = lambda name, shape, dt=F32: nc.dram_tensor(name, list(shape), dt, kind="ExternalInput").ap()
    dout = lambda name, shape, dt=F32: nc.dram_tensor(name, list(shape), dt, kind="ExternalOutput").ap()
    xT = din("xT", [4096, TN])
    scale = HD ** -0.5
    if do_c:
        oT = din("oT", [4096, TN]); w_out = din("w_out", [4096, 4096])
        memT = din("memT", [4096, 256]); mem_norm = din("mem_norm", [4096]); mem_w_kv = din("mem_w_kv", [4096, 1024])
        mem_k_norm = din("mem_k_norm", [128]); xattn_norm = din("xattn_norm", [4096]); wq = din("wq", [4096, 512])
        q_norm = din("q_norm", [128]); wo = din("wo", [512, 4096]); ffn_norm = din("ffn_norm", [4096])
        w1 = din("w1", [4096, D_FF]); w3 = din("w3", [4096, D_FF]); w2 = din("w2", [D_FF, 4096])
        ident_d = din("ident_d", [128, 128])
        x1T = P.dram("x1T", [4096, TN], F32); x2T = P.dram("x2T", [4096, TN], F32); uT = P.dram("uT", [D_FF, TN], BF16)
        x3T = dout("x3T", [4096, TN])
        ident = P.sbuf("ident", [128, 128], F32)
        P.dma("sp", ident[:], ident_d, writes=["ident"])
        gk = P.sbuf("gk", [128, 1], F32); gq = P.sbuf("gq", [128, 1], F32)
        P.dma("sp", gk[:], mem_k_norm.rearrange("(p o) -> p o", o=1), writes=["gk"])
        P.dma("sp", gq[:], q_norm.rearrange("(p o) -> p o", o=1), writes=["gq"])
        g_mem = tl.load_gain("mem", mem_norm, 4096); g_xa = tl.load_gain("xa", xattn_norm, 4096); g_ffn = tl.load_gain("ffn", ffn_norm, 4096)
        mkT = P.sbuf("mkT", [128, 4, 256], BF16); mv = P.sbuf("mv", [128, 2, 4, 128], BF16)
        qf = P.sbuf("qf", [128, 512], F32); qn = P.sbuf("qn", [128, 512], BF16); rq = P.sbuf("rq", [128, 512], F32)
        Eb = P.sbuf("Eb", [128, 2, 512], BF16); rinv = P.sbuf("rinv", [128, 512], F32)
        o_act = P.sbuf("o_act", [128, 4, 512], BF16)
        sil = [P.sbuf(f"sil{i}", [128, 512], F32) for i in range(2)]
        ub = [P.sbuf(f"ub{i}", [128, 512], BF16) for i in range(2)]

        def head_rms(src_ps, pkey, ncol, gcol, gkey, dst, dkey):
            P.op("dve", lambda e: e.tensor_copy(out=qf[:, :ncol], in_=src_ps[:, :ncol]), reads=[pkey], writes=["qf"])
            P.op("act", lambda e: e.activation(out=rq[:, :ncol], in_=qf[:, :ncol], func=AF.Square), reads=["qf"], writes=["rq"])
            sp2, sk2 = tl.next_ps()
            P.op("pe", lambda e: e.matmul(sp2[:, :ncol], lhsT=tl.ones[:], rhs=rq[:, :ncol], start=True, stop=True), reads=["rq", "ones"], writes=[sk2])
            P.op("dve", lambda e: e.tensor_scalar(out=rq[:, :ncol], in0=sp2[:, :ncol], scalar1=1.0 / 128, scalar2=EPS, op0=ALU.mult, op1=ALU.add),
                 reads=[sk2], writes=["rq"])
            P.op("act", lambda e: e.activation(out=rq[:, :ncol], in_=rq[:, :ncol], func=AF.Sqrt), reads=["rq"], writes=["rq"])
            P.op("dve", lambda e: e.reciprocal(out=rq[:, :ncol], in_=rq[:, :ncol]), reads=["rq"], writes=["rq"])
            P.op("dve", lambda e: e.scalar_tensor_tensor(out=dst, in0=qf[:, :ncol], scalar=gcol[:, 0:1], in1=rq[:, :ncol], op0=ALU.mult, op1=ALU.mult),
                 reads=["qf", "rq", gkey], writes=[dkey])

        tl.norm_act(memT, "memT", 0, 256, g_mem, "g_mem")
        def evac_mem(ji, c0, w, ps, pkey):
            h = ji % 4
            if ji < 4:
                head_rms(ps, pkey, 256, gk, "gk", mkT[:, h, :], "mkT")
            else:
                P.op("dve", lambda e: e.tensor_copy(out=qf[:, :256], in_=ps[:, :256]), reads=[pkey], writes=["qf"])
                for mt in range(2):
                    tp, tk = tl.next_ps()
                    P.op("pe", lambda e, mt=mt, tp=tp: e.transpose(out=tp[:, :128], in_=qf[:, mt * 128:(mt + 1) * 128], identity=ident[:]),
                         reads=["qf", "ident"], writes=[tk])
                    P.op("act", lambda e, mt=mt, tp=tp, h=h: e.copy(out=mv[:, mt, h, :], in_=tp[:, :128]), reads=[tk], writes=["mv"])
        tl.gemm([(mem_w_kv, c0, w) for (c0, w) in col_tiles([(0, 1024)])], 4096, 256, evac_mem)

        for t0 in range(0, TN, TG):
            tl.cast_act(oT, "oT", t0, TG, 4096)
            tl.gemm([(w_out, c0, w) for (c0, w) in col_tiles([(0, 4096)])], 4096, TG, tl.evac_store(x1T, "x1T", t0, TG, xT, "xT"))
            tl.norm_act(x1T, "x1T", t0, TG, g_xa, "g_xa")
            def evac_q(ji, c0, w, ps, pkey):
                h = ji
                head_rms(ps, pkey, 512, gq, "gq", qn[:], "qn")
                for mt in range(2):
                    sp_, sk_ = tl.next_ps()
                    P.op("pe", lambda e, mt=mt, sp_=sp_: e.matmul(sp_[:], lhsT=mkT[:, h, mt * 128:(mt + 1) * 128], rhs=qn[:], start=True, stop=True),
                         reads=["mkT", "qn"], writes=[sk_])
                    P.op("act", lambda e, mt=mt, sp_=sp_: e.activation(out=Eb[:, mt, :], in_=sp_[:], func=AF.Exp, scale=scale), reads=[sk_], writes=[("Eb", mt)])
                op_, ok_ = tl.next_ps(); rp_, rk_ = tl.next_ps()
                for mt in range(2):
                    P.op("pe", lambda e, mt=mt: e.matmul(op_[:], lhsT=mv[:, mt, h, :], rhs=Eb[:, mt, :], start=(mt == 0), stop=(mt == 1)),
                         reads=["mv", ("Eb", mt)], writes=[ok_])
                for mt in range(2):
                    P.op("pe", lambda e, mt=mt: e.matmul(rp_[:], lhsT=tl.onesb[:], rhs=Eb[:, mt, :], start=(mt == 0), stop=(mt == 1)),
                         reads=["onesb", ("Eb", mt)], writes=[rk_])
                P.op("dve", lambda e: e.reciprocal(out=rinv[:], in_=rp_[:]), reads=[rk_], writes=["rinv"])
                P.op("dve", lambda e: e.tensor_tensor(out=o_act[:, h, :], in0=op_[:], in1=rinv[:], op=ALU.mult), reads=[ok_, "rinv"], writes=["o_act"])
            tl.gemm([(wq, c0, w) for (c0, w) in col_tiles([(0, 512)])], 4096, TG, evac_q)
            tl.gemm([(wo, c0, w) for (c0, w) in col_tiles([(0, 4096)])], 512, TG, tl.evac_store(x2T, "x2T", t0, TG, x1T, "x1T"),
                    act=o_act, akey="o_act")
            tl.norm_act(x2T, "x2T", t0, TG, g_ffn, "g_ffn")
            jobs = []
            ft = col_tiles([(0, D_FF)])
            for i in range(0, len(ft), 2):
                pair = ft[i:i + 2]
                jobs += [(w1, c0, w) for (c0, w) in pair] + [(w3, c0, w) for (c0, w) in pair]
            held = {}
            def evac_ffn(ji, c0, w, ps, pkey, t0=t0):
                r = ji % 4
                if r < 2:
                    held[r] = (ps, pkey)
                    return
                pa, ka = held[r - 2]
                s = r - 2
                P.op("act", lambda e: e.activation(out=sil[s][:w, :], in_=pa[:w, :], func=AF.Silu), reads=[ka], writes=[("sil", s)])
                P.op("dve", lambda e: e.tensor_tensor(out=ub[s][:w, :], in0=ps[:w, :], in1=sil[s][:w, :], op=ALU.mult),
                     reads=[pkey, ("sil", s)], writes=[("ub", s)])
                P.dma("sp", uT[c0:c0 + w, t0:t0 + TG], ub[s][:w, :], reads=[("ub", s)], writes=["uT"])
            tl.gemm(jobs, 4096, TG, evac_ffn)
            for hf in range(2):
                tt = t0 + hf * 256
                P.dma("sp", tl.act_view(86, 256), uT[:, tt:tt + 256].rearrange("(c p) t -> p c t", p=128), reads=["uT"], writes=["actbuf"])
                tl.gemm([(w2, c0, w) for (c0, w) in col_tiles([(0, 4096)])], D_FF, 256, tl.evac_store(x3T, "x3T", tt, 256, x2T, "x2T"))
        xn, xnkey = x3T, "x3T"
    else:
        xn, xnkey = xT, "xT"
    if do_a:
        mix_norm = din("mix_norm", [4096]); w_in = din("w_in", [4096, n_in])
        projT = dout("projT", [n_in, TN])
        g_mix = tl.load_gain("mix", mix_norm, 4096)
        for t0 in range(0, TN, TG):
            tl.norm_act(xn, xnkey, t0, TG, g_mix, "g_mix")
            tl.gemm([(w_in, c0, w) for (c0, w) in col_tiles(sections)], 4096, TG, tl.evac_store(projT, "projT", t0, TG))
    P.finish()
    return nc, P


def const_RT():
    RT = np.zeros((128, 128), np.float32)
    for m in range(16):
        RT[m + 16, m] = -1.0
        RT[m, m + 16] = 1.0
    return RT


def const_inv():
    inv = np.zeros((128, 1), np.float32)
    inv[:32, 0] = np.tile((500000.0 ** (-np.arange(0, 32, 2, dtype=np.float32) / 32)).astype(np.float32), 2)
    return inv


class SeqCommon:
    def __init__(self, P, nc, T, nps=3):
        self.P = P; self.nc = nc; self.T = T; self.nps = nps
        din = lambda name, shape, dt=F32: nc.dram_tensor(name, list(shape), dt, kind="ExternalInput").ap()
        self.din = din
        self.ones = P.sbuf("ones_f", [128, 128], F32)
        self.onesb = P.sbuf("ones_b", [128, 128], BF16)
        P.op("dve", lambda e: e.memset(self.ones[:], 1.0), writes=["ones"])
        P.op("dve", lambda e: e.memset(self.onesb[:], 1.0), writes=["onesb"])
        self.RT = P.sbuf("RT", [128, 128], F32); self.inv = P.sbuf("inv", [128, 1], F32); self.ident = P.sbuf("ident", [128, 128], F32)
        P.dma("sp", self.RT[:], din("RT_d", [128, 128]), writes=["RT"])
        P.dma("sp", self.inv[:], din("inv_d", [128, 1]), writes=["inv"])
        P.dma("sp", self.ident[:], din("ident_d", [128, 128]), writes=["ident"])
        self.identb = P.sbuf("identb", [128, 128], BF16)
        P.op("dve", lambda e: e.tensor_copy(out=self.identb[:], in_=self.ident[:]), reads=["ident"], writes=["identb"])
        self.ps = [P.psum(f"ps{i}", [128, 512]) for i in range(nps)]
        self.pi = 0
        self.xf = [P.sbuf(f"pxf{i}", [128, 512], F32) for i in range(2)]
        self.t1 = P.sbuf("pt1", [128, 512], F32); self.t2 = P.sbuf("pt2", [128, 512], F32); self.t3 = P.sbuf("pt3", [128, 512], F32)
        self.xi = 0

    def next_ps(self):
        i = self.pi % self.nps; self.pi += 1
        return self.ps[i], ("ps", i)

    def rope_tables(self, pos_rep, n, name):
        P = self.P
        C = P.sbuf("C_" + name, [128, n], F32); S = P.sbuf("S_" + name, [128, n], F32)
        if not hasattr(self, "rt_pi"):
            self.rt_pi = P.sbuf("rt_pi", [128, 512], I32); self.rt_pf = P.sbuf("rt_pf", [128, 512], F32); self.rt_tf = P.sbuf("rt_tf", [128, 512], F32)
        pi_, pf, tf = self.rt_pi, self.rt_pf, self.rt_tf
        for b0 in range(0, n, 512):
            w = min(512, n - b0)
            P.dma("sp", pi_[:, :w], pos_rep[:, b0:b0 + w], writes=["rt_pi"])
            P.op("dve", lambda e, w=w: e.tensor_copy(out=pf[:, :w], in_=pi_[:, :w]), reads=["rt_pi"], writes=["rt_pf"])
            for X, k, ph in ((C, "C" + name, 0.5 * math.pi), (S, "S" + name, 0.0)):
                Xs = X[:, b0:b0 + w]
                P.op("dve", lambda e, Xs=Xs, ph=ph, w=w: e.tensor_scalar(out=Xs, in0=pf[:, :w], scalar1=self.inv[:, 0:1], scalar2=ph, op0=ALU.mult, op1=ALU.add),
                     reads=["rt_pf", "inv"], writes=[k])
                P.op("dve", lambda e, Xs=Xs, w=w: e.tensor_scalar(out=pi_[:, :w], in0=Xs, scalar1=1.0 / (2 * math.pi), scalar2=None, op0=ALU.mult),
                     reads=[k], writes=["rt_pi"])
                P.op("dve", lambda e, w=w: e.tensor_copy(out=tf[:, :w], in_=pi_[:, :w]), reads=["rt_pi"], writes=["rt_tf"])
                P.op("dve", lambda e, Xs=Xs, w=w: e.scalar_tensor_tensor(out=Xs, in0=tf[:, :w], scalar=-2 * math.pi, in1=Xs, op0=ALU.mult, op1=ALU.add),
                     reads=["rt_tf", k], writes=[k])
                P.op("act", lambda e, Xs=Xs: e.activation(out=Xs, in_=Xs, func=AF.Sin), reads=[k], writes=[k])
        return C, S

    def prep_block(self, src, skey, ncol, dst, dkey, gcol=None, gkey=None, C=None, S=None, ckey=None, norm=True, src_sbuf=False):
        P = self.P
        if src_sbuf:
            xf, xkey = src, skey
        else:
            s = self.xi % 2; self.xi += 1
            xf, xkey = self.xf[s][:, :ncol], ("pxf", s)
            P.dma("sp", xf, src, reads=[skey], writes=[xkey])
        t1, t2, t3 = self.t1[:, :ncol], self.t2[:, :ncol], self.t3[:, :ncol]
        cur, ckey_cur = xf, xkey
        if norm:
            P.op("act", lambda e: e.activation(out=t1, in_=xf, func=AF.Square), reads=[xkey], writes=["pt1"])
            sp_, sk_ = self.next_ps()
            P.op("pe", lambda e: e.matmul(sp_[:, :ncol], lhsT=self.ones[:], rhs=t1, start=True, stop=True), reads=["pt1", "ones"], writes=[sk_])
            P.op("dve", lambda e: e.tensor_scalar(out=t1, in0=sp_[:, :ncol], scalar1=1.0 / 128, scalar2=EPS, op0=ALU.mult, op1=ALU.add), reads=[sk_], writes=["pt1"])
            P.op("act", lambda e: e.activation(out=t1, in_=t1, func=AF.Sqrt), reads=["pt1"], writes=["pt1"])
            P.op("dve", lambda e: e.reciprocal(out=t1, in_=t1), reads=["pt1"], writes=["pt1"])
            P.op("dve", lambda e: e.scalar_tensor_tensor(out=t2, in0=xf, scalar=gcol[:, 0:1], in1=t1, op0=ALU.mult, op1=ALU.mult),
                 reads=[xkey, "pt1", gkey], writes=["pt2"])
            cur, ckey_cur = t2, "pt2"
        if C is None:
            P.op("dve", lambda e: e.tensor_copy(out=dst, in_=cur), reads=[ckey_cur], writes=[dkey])
            return
        rp_, rk_ = self.next_ps()
        P.op("pe", lambda e: e.matmul(rp_[:, :ncol], lhsT=self.RT[:], rhs=cur, start=True, stop=True), reads=[ckey_cur, "RT"], writes=[rk_])
        P.op("dve", lambda e: e.tensor_tensor(out=t3, in0=rp_[:, :ncol], in1=S, op=ALU.mult), reads=[rk_, ckey[1]], writes=["pt3"])
        P.op("dve", lambda e: e.tensor_tensor(out=t1, in0=cur, in1=C, op=ALU.mult), reads=[ckey_cur, ckey[0]], writes=["pt1"])
        P.op("dve", lambda e: e.tensor_tensor(out=dst, in0=t1, in1=t3, op=ALU.add), reads=["pt1", "pt3"], writes=[dkey])

    def load_col(self, name, vec):
        g = self.P.sbuf("gc_" + name, [128, 1], F32)
        self.P.dma("sp", g[:], vec.rearrange("(p o) -> p o", o=1), writes=["gc_" + name])
        return g, "gc_" + name


def build_dsa_index(NQ=2048, T=4096, TOPK=256):
    nc = new_nc(); P = Prog(nc); sc = SeqCommon(P, nc, T); din = sc.din
    qiT = din("qiT", [32, 128, NQ]); wi = din("wi", [NQ, 32]); posq = din("posq", [128, NQ], I32); posk = din("posk", [128, T], I32)
    kiT = din("kiT", [128, T]); ki_norm = din("ki_norm", [128]); cbias_d = din("cbias", [2, 128, 512])
    biasQ = nc.dram_tensor("biasQ", [NQ, T], F32, kind="ExternalOutput").ap()
    Ck, Sk = sc.rope_tables(posk, T, "k"); Cq, Sq = sc.rope_tables(posq, NQ, "q")
    gki, gkey = sc.load_col("ki", ki_norm)
    kib = P.sbuf("kib", [128, T], BF16)
    for b in range(T // 512):
        sl = slice(b * 512, (b + 1) * 512)
        sc.prep_block(kiT[:, sl], "kiT", 512, kib[:, sl], "kib", gki, gkey, Ck[:, sl], Sk[:, sl], ("Ck", "Sk"))
    cb = P.sbuf("cb", [128, 2, 512], F32)
    P.dma("sp", cb[:], cbias_d.rearrange("a p s -> p a s"), writes=["cb"])
    qib = P.sbuf("qib", [128, 32, 128], BF16)
    wt = P.sbuf("wt", [128, 32], F32); wabs = P.sbuf("wabs", [128, 32], F32); wsgn = P.sbuf("wsgn", [128, 32], F32)
    acc = P.sbuf("acc", [128, T], F32); wk = P.sbuf("wk", [128, T], F32)
    tmp = [P.sbuf(f"itmp{i}", [128, 512], F32) for i in range(3)]
    m8 = P.sbuf("m8", [128, TOPK], F32); thr = P.sbuf("thr", [128, 1], F32)
    ti = 0
    for i in range(NQ // 128):
        qs = slice(i * 128, (i + 1) * 128)
        P.dma("sp", wt[:], wi[qs, :], writes=["wt"])
        P.op("act", lambda e: e.activation(out=wabs[:], in_=wt[:], func=AF.Abs), reads=["wt"], writes=["wabs"])
        P.op("act", lambda e: e.activation(out=wsgn[:], in_=wt[:], func=AF.Sign), reads=["wt"], writes=["wsgn"])
        for h in range(32):
            sc.prep_block(qiT[h, :, qs], "qiT", 128, qib[:, h, :], "qib", None, None, Cq[:, qs], Sq[:, qs], ("Cq", "Sq"), norm=False)
        nkb = i // 2 + 1
        n = nkb * 512
        for kb in range(nkb):
            ks = slice(kb * 512, (kb + 1) * 512)
            for h in range(32):
                ps, pk = sc.next_ps()
                P.op("pe", lambda e, h=h, ps=ps, ks=ks: e.matmul(ps[:], lhsT=qib[:, h, :], rhs=kib[:, ks], start=True, stop=True),
                     reads=["qib", "kib"], writes=[pk])
                tt = ti % 3; ti += 1
                P.op("act", lambda e, h=h, ps=ps, tt=tt: e.activation(out=tmp[tt][:], in_=ps[:], func=AF.Relu, scale=wabs[:, h:h + 1]),
                     reads=[pk, "wabs"], writes=[("itmp", tt)])
                if h == 0:
                    P.op("dve", lambda e, tt=tt, ks=ks: e.tensor_scalar(out=acc[:, ks], in0=tmp[tt][:], scalar1=wsgn[:, 0:1], scalar2=None, op0=ALU.mult),
                         reads=[("itmp", tt), "wsgn"], writes=["acc"])
                else:
                    P.op("dve", lambda e, tt=tt, ks=ks, h=h: e.scalar_tensor_tensor(out=acc[:, ks], in0=tmp[tt][:], scalar=wsgn[:, h:h + 1], in1=acc[:, ks],
                                                                                 op0=ALU.mult, op1=ALU.add),
                         reads=[("itmp", tt), "wsgn", "acc"], writes=["acc"])
        ls = slice(n - 512, n)
        P.op("dve", lambda e, ls=ls, i=i: e.tensor_tensor(out=acc[:, ls], in0=acc[:, ls], in1=cb[:, i % 2, :], op=ALU.add), reads=["acc", "cb"], writes=["acc"])
        P.op("act", lambda e, n=n: e.copy(out=wk[:, :n], in_=acc[:, :n]), reads=["acc"], writes=["wk"])
        for r in range(TOPK // 8):
            P.op("dve", lambda e, r=r, n=n: e.max(out=m8[:, r * 8:(r + 1) * 8], in_=wk[:, :n]), reads=["wk"], writes=["m8"])
            if r < TOPK // 8 - 1:
                P.op("dve", lambda e, r=r, n=n: e.match_replace(out=wk[:, :n], in_to_replace=m8[:, r * 8:(r + 1) * 8], in_values=wk[:, :n], imm_value=-1e30),
                     reads=["wk", "m8"], writes=["wk"])
        P.op("dve", lambda e: e.tensor_reduce(out=thr[:], in_=m8[:, TOPK - 8:TOPK], axis=AX.X, op=ALU.min), reads=["m8"], writes=["thr"])
        P.op("dve", lambda e: e.tensor_scalar(out=thr[:], in0=thr[:], scalar1=-1e29, scalar2=None, op0=ALU.max), reads=["thr"], writes=["thr"])
        P.op("dve", lambda e, n=n: e.tensor_scalar(out=wk[:, :n], in0=acc[:, :n], scalar1=thr[:, 0:1], scalar2=None, op0=ALU.is_ge), reads=["acc", "thr"], writes=["wk"])
        P.op("dve", lambda e, n=n: e.tensor_scalar(out=wk[:, :n], in0=wk[:, :n], scalar1=-1.0, scalar2=-NEG, op0=ALU.add, op1=ALU.mult), reads=["wk"], writes=["wk"])
        if n < T:
            P.op("pool", lambda e, n=n: e.memset(wk[:, n:], NEG), reads=[], writes=["wk"])
        P.dma("sp", biasQ[qs, :], wk[:], reads=["wk"], writes=["biasQ"])
    P.finish()
    return nc, P


def dsa_index_consts(r):
    cb = np.zeros((2, 128, 512), np.float32)
    c = np.arange(512)[None, :]; p = np.arange(128)[:, None]
    for par in range(2):
        cb[par] = np.where(c <= 256 * par + 2 * p + r, 0.0, -1e30)
    return cb


class Attn:
    def __init__(self, sc):
        self.sc = sc; P = sc.P
        self.E = [P.sbuf(f"attE{i}", [128, 512], BF16) for i in range(3)]
        self.ei = 0
        self.Ops = [P.psum(f"attO{i}", [128, 2, 256]) for i in range(2)]
        self.Sps = [P.psum(f"attS{i}", [128, 512]) for i in range(2)]
        self.si = 0
        self.rinv = P.sbuf("att_rinv", [128, 4], F32)

    def run(self, q4, qkey, kts, scale, on_E=None):
        P = self.sc.P
        n = len(kts)
        for i, kt in enumerate(kts):
            ns = kt["ns"]
            si = self.si % 2; self.si += 1
            Sp = self.Sps[si]; skey = ("attS", si)
            extra = kt.get("bias", [])
            P.op("pe", lambda e, Sp=Sp, kt=kt, ns=ns, extra=extra: e.matmul(Sp[:ns, :].rearrange("p (h t) -> p h t", h=4), lhsT=kt["k"], rhs=q4,
                                                                       start=True, stop=(len(extra) == 0)),
                 reads=[kt["kkey"], qkey], writes=[skey])
            for j, (bl, br, bkeys) in enumerate(extra):
                P.op("pe", lambda e, Sp=Sp, bl=bl, br=br, ns=ns, j=j, extra=extra: e.matmul(Sp[:ns, :], lhsT=bl, rhs=br, start=False, stop=(j == len(extra) - 1)),
                     reads=list(bkeys), writes=[skey])
            ei = self.ei % 3; self.ei += 1
            E = self.E[ei]; ekey = ("attE", ei)
            P.op("act", lambda e, E=E, Sp=Sp, ns=ns: e.activation(out=E[:ns, :], in_=Sp[:ns, :], func=AF.Exp, scale=scale), reads=[skey], writes=[ekey])
            for h in range(4):
                P.op("pe", lambda e, E=E, h=h, kt=kt, ns=ns, i=i: e.matmul(self.Ops[h // 2][:, h % 2, 0:129], lhsT=E[:ns, h * 128:(h + 1) * 128], rhs=kt["v"],
                                                                          start=(i == 0 and h % 2 == 0), stop=(i == n - 1), skip_group_check=True),
                     reads=[ekey, kt["vkey"]], writes=[("attO", h // 2)])
            if on_E is not None:
                on_E(i, E, ekey, ns)

    def finish(self, h):
        P = self.sc.P
        O = self.Ops[h // 2]
        P.op("dve", lambda e: e.tensor_scalar(out=self.rinv[:, h:h + 1], in0=O[:, h % 2, 128:129], scalar1=1e-30, scalar2=None, op0=ALU.add),
             reads=[("attO", h // 2)], writes=["att_rinv"])
        P.op("dve", lambda e: e.reciprocal(out=self.rinv[:, h:h + 1], in_=self.rinv[:, h:h + 1]), reads=["att_rinv"], writes=["att_rinv"])
        return O[:, h % 2, 0:128], self.rinv[:, h:h + 1], ("attO", h // 2)


def load_v_aug(P, v_dram, vkey, T, name):
    va = P.sbuf("vaug_" + name, [128, T // 128, 129], BF16)
    P.op("pool", lambda e: e.memset(va[:, :, 128:129], 1.0), writes=["vaug_" + name])
    P.dma("pool", va[:, :, 0:128], v_dram.rearrange("(n p) d -> p n d", p=128), reads=[vkey], writes=["vaug_" + name])
    return va


def build_dsa_attn(T=4096):
    nc = new_nc(); P = Prog(nc); sc = SeqCommon(P, nc, T); din = sc.din
    qT = din("qT", [16, 128, T]); kT = din("kT", [2, 128, T]); v = din("v", [2, T, 128]); biasT = din("biasT", [T, T])
    pos = din("pos", [128, T], I32); q_norm = din("q_norm", [128]); k_norm = din("k_norm", [128])
    o = nc.dram_tensor("o", [T, 2048], F32, kind="ExternalOutput").ap()
    C, S = sc.rope_tables(pos, T, "k")
    gq, gqk = sc.load_col("q", q_norm); gk, gkk = sc.load_col("k", k_norm)
    at = Attn(sc)
    kb = P.sbuf("kb", [128, T], BF16); qb = P.sbuf("qb", [128, 4, T], BF16)
    b4 = [P.sbuf(f"b4_{i}", [128, 4, 128], BF16) for i in range(2)]
    osb = [P.sbuf(f"osb{i}", [128, 4, 128], F32) for i in range(2)]
    scale = HD ** -0.5
    bi = 0; oi = 0
    NT = T // 128
    for kh in range(2):
        for b in range(T // 512):
            sl = slice(b * 512, (b + 1) * 512)
            sc.prep_block(kT[kh, :, sl], "kT", 512, kb[:, sl], "kb", gk, gkk, C[:, sl], S[:, sl], ("Ck", "Sk"))
        va = load_v_aug(P, v[kh], "v", T, f"{kh}")
        for g in range(2):
            for j in range(4):
                h = kh * 8 + g * 4 + j
                for b in range(T // 512):
                    sl = slice(b * 512, (b + 1) * 512)
                    sc.prep_block(qT[h, :, sl], "qT", 512, qb[:, j, sl], "qb", gq, gqk, C[:, sl], S[:, sl], ("Ck", "Sk"))
            for qi in range(NT):
                qs = slice(qi * 128, (qi + 1) * 128)
                kts = []
                for kt in range(qi + 1):
                    ks = slice(kt * 128, (kt + 1) * 128)
                    s = bi % 2; bi += 1
                    P.dma("pool", b4[s][:, 0, :], biasT[ks, qs], writes=[("b4", s)])
                    P.op("pool", lambda e, s=s: e.tensor_copy(out=b4[s][:, 1, :], in_=b4[s][:, 0, :]), reads=[("b4", s)], writes=[("b4", s)])
                    P.op("pool", lambda e, s=s: e.tensor_copy(out=b4[s][:, 2:4, :], in_=b4[s][:, 0:2, :]), reads=[("b4", s)], writes=[("b4", s)])
                    kts = [dict(k=kb[:, ks], kkey="kb", v=va[:, kt, :], vkey=f"vaug_{kh}", ns=128,
                                bias=[(sc.identb[:], b4[s][:].rearrange("p h t -> p (h t)"), ("identb", ("b4", s)))])]
                    at_run_tile(at, qb[:, :, qs], "qb", kts[0], scale, first=(kt == 0), last=(kt == qi))
                so = oi % 2; oi += 1
                for j in range(4):
                    O, rinv, okey = at.finish(j)
                    P.op("dve", lambda e, O=O, rinv=rinv, j=j, so=so: e.tensor_scalar(out=osb[so][:, j, :], in0=O, scalar1=rinv, scalar2=None, op0=ALU.mult),
                         reads=[okey, "att_rinv"], writes=[("osb", so)])
                c0 = (kh * 8 + g * 4) * 128
                P.dma("sp", o[qs, c0:c0 + 512], osb[so][:].rearrange("p h d -> p (h d)"), reads=[("osb", so)], writes=["o"])
    P.finish()
    return nc, P


def at_run_tile(at, q4, qkey, kt, scale, first, last, on_E=None):
    P = at.sc.P
    ns = kt["ns"]
    si = at.si % 2; at.si += 1
    Sp = at.Sps[si]; skey = ("attS", si)
    extra = kt.get("bias", [])
    P.op("pe", lambda e: e.matmul(Sp[:ns, :].rearrange("p (h t) -> p h t", h=4), lhsT=kt["k"], rhs=q4, start=True, stop=(len(extra) == 0)),
         reads=[kt["kkey"], qkey], writes=[skey])
    for j, (bl, br, bkeys) in enumerate(extra):
        P.op("pe", lambda e, bl=bl, br=br, j=j: e.matmul(Sp[:ns, :], lhsT=bl, rhs=br, start=False, stop=(j == len(extra) - 1)),
             reads=list(bkeys), writes=[skey])
    ei = at.ei % 3; at.ei += 1
    E = at.E[ei]; ekey = ("attE", ei)
    P.op("act", lambda e: e.activation(out=E[:ns, :], in_=Sp[:ns, :], func=AF.Exp, scale=scale), reads=[skey], writes=[ekey])
    for h in range(4):
        P.op("pe", lambda e, h=h: e.matmul(at.Ops[h // 2][:, h % 2, 0:129], lhsT=E[:ns, h * 128:(h + 1) * 128], rhs=kt["v"], start=(first and h % 2 == 0), stop=last, skip_group_check=True),
             reads=[ekey, kt["vkey"]], writes=[("attO", h // 2)])
    if on_E is not None:
        on_E(E, ekey, ns)


def nsa_consts(T=4096):
    n_cmp = T // 16 - 1; n_sel = T // 64; NT = T // 128
    cs = 16 * np.arange(n_cmp)[:, None]; ss = 64 * np.arange(n_sel)[None, :]
    ov = np.clip(np.minimum(cs + 32, ss + 64) - np.maximum(cs, ss), 0, None) / 32.0
    msel = np.zeros((256, 64), np.float32); msel[:n_cmp] = ov
    c = np.arange(256)[:, None]; t = np.arange(T)[None, :]
    cmpb = np.where((16 * c + 31 <= t) & (c < n_cmp), 0.0, NEG).astype(np.float32)
    keep = np.ones((NT, 128, 64), np.float32); addc = np.zeros((NT, 128, 64), np.float32)
    tt = np.arange(T).reshape(NT, 128, 1); jt = tt // 64; jj = np.arange(64)[None, None, :]
    for cond, val in (((jj == 0), 1.0e4), ((jj == jt), 1.1e4), ((jj == jt - 1), 1.2e4), ((jj > jt), -1e30)):
        cond = np.broadcast_to(cond, keep.shape)
        keep = np.where(cond, 0.0, keep); addc = np.where(cond, val, addc)
    esel = np.zeros((NT, 64, 128), np.float32)
    for kt in range(NT):
        esel[kt, 2 * kt, :64] = 1.0; esel[kt, 2 * kt + 1, 64:] = 1.0
    s = np.arange(128)[:, None]; tl = np.arange(128)[None, :]
    tri = np.where(s <= tl, 0.0, NEG).astype(np.float32); atri = np.where(s > tl, 0.0, NEG).astype(np.float32)
    return dict(msel=msel.reshape(2, 128, 64), cmpb=cmpb.reshape(2, 128, T), keep=keep.astype(np.float32), addc=addc.astype(np.float32), esel=esel,
                tri4=np.tile(tri, (1, 4)), atri4=np.tile(atri, (1, 4)))


NSA_BR = (0, 1, 2)


def emit_nsa(P, nc, sc, at, T, C, S):
    din = sc.din
    NT = T // 128; NC = T // 16 - 1
    qT = din("n_qT", [8, 128, T]); kcT = din("n_kcT", [2, 128, T]); vcT = din("n_vcT", [2, 128, T])
    ksT = din("n_ksT", [2, 128, T]); kwT = din("n_kwT", [2, 128, T]); vs = din("n_vs", [2, T, 128]); vw = din("n_vw", [2, T, 128])
    gates = din("n_gates", [T, 24]); cpos = din("n_cpos", [128, 256], I32)
    norms = {k: sc.load_col("n" + k, din("n_" + k + "_norm", [128])) for k in ("q", "kc", "ks", "kw")}
    peT = {k: din("n_peT_" + k, [128, 32]) for k in "kv"}; w1 = {k: din("n_w1_" + k, [4096, 128]) for k in "kv"}; w2 = {k: din("n_w2_" + k, [128, 128]) for k in "kv"}
    msel_d = din("n_msel", [2, 128, 64]); cmpb_d = din("n_cmpb", [2, 128, T]); keep_d = din("n_keep", [NT, 128, 64]); addc_d = din("n_addc", [NT, 128, 64])
    esel_d = din("n_esel", [NT, 64, 128]); tri4_d = din("n_tri4", [128, 512]); atri4_d = din("n_atri4", [128, 512])
    o_a = nc.dram_tensor("o_a", [T, 1024], F32, kind="ExternalOutput").ap()
    Cc, Sc = sc.rope_tables(cpos, 256, "c")
    mselb = P.sbuf("mselb", [128, 2, 64], BF16); P.dma("pool", mselb[:], msel_d.rearrange("a p j -> p a j"), writes=["mselb"])
    keep = P.sbuf("keep", [128, NT, 64], F32); P.dma("sp", keep[:], keep_d.rearrange("n p j -> p n j"), writes=["keep"])
    addc = P.sbuf("addc", [128, NT, 64], F32); P.dma("sp", addc[:], addc_d.rearrange("n p j -> p n j"), writes=["addc"])
    eselb = P.sbuf("eselb", [64, NT, 128], BF16); P.dma("pool", eselb[:], esel_d.rearrange("n j s -> j n s"), writes=["eselb"])
    tri4 = P.sbuf("tri4", [128, 512], BF16); P.dma("pool", tri4[:], tri4_d, writes=["tri4"])
    atri4 = P.sbuf("atri4", [128, 512], BF16); P.dma("pool", atri4[:], atri4_d, writes=["atri4"])
    w1b = P.sbuf("w1b", [128, 32, 128], BF16); w2b = P.sbuf("w2b", [128, 128], BF16); peb = P.sbuf("peb", [128, 32], BF16)
    pecol = P.sbuf("pecol", [128, 1], F32)
    xcb = P.sbuf("xcb", [128, T], BF16)
    gx = P.sbuf("gx", [128, 256], F32); gt = P.sbuf("gt", [128, 256], F32); gT = P.sbuf("gT", [128, 256], BF16)
    kc2 = P.sbuf("kc2", [128, 256], F32)
    kcc = P.sbuf("kcc", [128, 256], BF16); vca = P.sbuf("vca", [128, 2, 129], BF16)
    ksb = P.sbuf("ksb", [128, T], BF16); kwb = P.sbuf("kwb", [128, T], BF16); qb = P.sbuf("qb", [128, 4, T], BF16)
    cb4 = [P.sbuf(f"cb4_{i}", [128, 4, 128], BF16) for i in range(2)]
    gsb = P.sbuf("gsb", [128, 24], F32); fac = P.sbuf("fac", [128, 4], F32)
    imps = P.psum("imps", [128, 4, 64])
    imp = P.sbuf("imp", [128, 64], F32); wk = P.sbuf("nwk", [128, 64], F32); m8 = P.sbuf("nm8", [128, 16], F32); thr = P.sbuf("nthr", [128, 1], F32)
    selbT4 = P.sbuf("selbT4", [64, 4, 128], BF16)
    oacc = [P.sbuf(f"oacc{i}", [128, 4, 128], F32) for i in range(2)]
    scale = HD ** -0.5
    GC = 2 * math.sqrt(2 / math.pi)
    oi = 0; ci = 0
    for kh in range(2):
        for kind, srcT in (("k", kcT), ("v", vcT)):
            P.dma("pool", w1b[:], w1[kind].rearrange("(p d) m -> d p m", d=128), writes=["w1b"])
            P.dma("pool", w2b[:], w2[kind], writes=["w2b"])
            P.dma("pool", peb[:], peT[kind], writes=["peb"])
            P.dma("pool", xcb[:], srcT[kh], writes=["xcb"])
            hp, hk = sc.next_ps(); bp, bk = sc.next_ps()
            for p in range(32):
                P.op("pe", lambda e, p=p, hp=hp: e.matmul(hp[:, :NC], lhsT=w1b[:, p, :], rhs=xcb[:, p:p + 16 * (NC - 1) + 1:16], start=(p == 0), stop=(p == 31)),
                     reads=["w1b", "xcb"], writes=[hk])
            for p in range(32):
                P.op("pe", lambda e, p=p, bp=bp: e.matmul(bp[:, 0:1], lhsT=w1b[:, p, :], rhs=peb[:, p:p + 1], start=(p == 0), stop=(p == 31)),
                     reads=["w1b", "peb"], writes=[bk])
            P.op("dve", lambda e, bp=bp: e.tensor_copy(out=pecol[:], in_=bp[:, 0:1]), reads=[bk], writes=["pecol"])
            P.op("dve", lambda e: e.memset(gT[:], 0.0), writes=["gT"])
            P.op("dve", lambda e, hp=hp: e.tensor_scalar(out=gx[:, :NC], in0=hp[:, :NC], scalar1=pecol[:, 0:1], scalar2=None, op0=ALU.add), reads=[hk, "pecol"], writes=["gx"])
            P.op("dve", lambda e: e.tensor_tensor(out=gt[:, :NC], in0=gx[:, :NC], in1=gx[:, :NC], op=ALU.mult), reads=["gx"], writes=["gt"])
            P.op("dve", lambda e: e.tensor_scalar(out=gt[:, :NC], in0=gt[:, :NC], scalar1=0.044715, scalar2=1.0, op0=ALU.mult, op1=ALU.add), reads=["gt"], writes=["gt"])
            P.op("dve", lambda e: e.tensor_tensor(out=gt[:, :NC], in0=gt[:, :NC], in1=gx[:, :NC], op=ALU.mult), reads=["gt", "gx"], writes=["gt"])
            P.op("act", lambda e: e.activation(out=gt[:, :NC], in_=gt[:, :NC], func=AF.Sigmoid, scale=GC), reads=["gt"], writes=["gt"])
            P.op("dve", lambda e: e.tensor_tensor(out=gT[:, :NC], in0=gt[:, :NC], in1=gx[:, :NC], op=ALU.mult), reads=["gt", "gx"], writes=["gT"])
            if kind == "k":
                op_, ok_ = sc.next_ps()
                P.op("pe", lambda e, op_=op_: e.matmul(op_[:, :256], lhsT=w2b[:], rhs=gT[:], start=True, stop=True), reads=["w2b", "gT"], writes=[ok_])
                P.op("dve", lambda e, op_=op_: e.tensor_copy(out=kc2[:], in_=op_[:, :256]), reads=[ok_], writes=["kc2"])
                g_, gk_ = norms["kc"]
                sc.prep_block(kc2[:], "kc2", 256, kcc[:], "kcc", g_, gk_, Cc[:], Sc[:], ("Cc", "Sc"), src_sbuf=True)
            else:
                P.op("pool", lambda e: e.memset(vca[:, :, 128:129], 1.0), writes=["vca"])
                for ct in range(2):
                    op_, ok_ = sc.next_ps()
                    P.op("pe", lambda e, ct=ct, op_=op_: e.matmul(op_[:, :128], lhsT=gT[:, ct * 128:(ct + 1) * 128], rhs=w2b[:], start=True, stop=True),
                         reads=["w2b", "gT"], writes=[ok_])
                    P.op("dve", lambda e, ct=ct, op_=op_: e.tensor_copy(out=vca[:, ct, 0:128], in_=op_[:, :128]), reads=[ok_], writes=["vca"])
        for b in range(T // 512):
            sl = slice(b * 512, (b + 1) * 512)
            sc.prep_block(ksT[kh, :, sl], "ksT", 512, ksb[:, sl], "ksb", *norms["ks"], C[:, sl], S[:, sl], ("Ck", "Sk"))
            sc.prep_block(kwT[kh, :, sl], "kwT", 512, kwb[:, sl], "kwb", *norms["kw"], C[:, sl], S[:, sl], ("Ck", "Sk"))
        vsa = load_v_aug(P, vs[kh], "vs", T, f"s{kh}"); vwa = load_v_aug(P, vw[kh], "vw", T, f"w{kh}")
        for j in range(4):
            for b in range(T // 512):
                sl = slice(b * 512, (b + 1) * 512)
                sc.prep_block(qT[kh * 4 + j, :, sl], "qT", 512, qb[:, j, sl], "qb", *norms["q"], C[:, sl], S[:, sl], ("Ck", "Sk"))
        for qi in range(NT):
            qs = slice(qi * 128, (qi + 1) * 128)
            q4 = qb[:, :, qs]
            so = oi % 2; oi += 1
            P.dma("sp", gsb[:, 0:12], gates[qs, kh * 12:(kh + 1) * 12], writes=["gsb"])
            P.op("act", lambda e: e.activation(out=gsb[:, 0:12], in_=gsb[:, 0:12], func=AF.Sigmoid), reads=["gsb"], writes=["gsb"])

            def combine(br, first, so=so):
                if br not in NSA_BR:
                    return
                first = (br == min(NSA_BR))
                for g in range(4):
                    O, rinv, okey = at.finish(g)
                    P.op("dve", lambda e, g=g, rinv=rinv: e.tensor_tensor(out=fac[:, g:g + 1], in0=rinv, in1=gsb[:, 3 * g + br:3 * g + br + 1], op=ALU.mult),
                         reads=["att_rinv", "gsb"], writes=["fac"])
                    if first:
                        P.op("dve", lambda e, g=g, O=O: e.tensor_scalar(out=oacc[so][:, g, :], in0=O, scalar1=fac[:, g:g + 1], scalar2=None, op0=ALU.mult),
                             reads=[okey, "fac"], writes=[("oacc", so)])
                    else:
                        P.op("dve", lambda e, g=g, O=O: e.scalar_tensor_tensor(out=oacc[so][:, g, :], in0=O, scalar=fac[:, g:g + 1], in1=oacc[so][:, g, :],
                                                                             op0=ALU.mult, op1=ALU.add),
                             reads=[okey, "fac", ("oacc", so)], writes=[("oacc", so)])

            nct = 1 if 8 * qi + 7 <= 128 else 2
            for ct in range(nct):
                s = ci % 2; ci += 1
                P.dma("pool", cb4[s][:, 0, :], cmpb_d[ct, :, qs], writes=[("cb4", s)])
                P.op("pool", lambda e, s=s: e.tensor_copy(out=cb4[s][:, 1, :], in_=cb4[s][:, 0, :]), reads=[("cb4", s)], writes=[("cb4", s)])
                P.op("pool", lambda e, s=s: e.tensor_copy(out=cb4[s][:, 2:4, :], in_=cb4[s][:, 0:2, :]), reads=[("cb4", s)], writes=[("cb4", s)])
                def on_E(E, ekey, ns, ct=ct):
                    for g in range(4):
                        P.op("pe", lambda e, g=g: e.matmul(imps[:, g, :], lhsT=E[:, g * 128:(g + 1) * 128], rhs=mselb[:, ct, :],
                                                           start=(ct == 0 and g == 0), stop=True, skip_group_check=True),
                             reads=[ekey, "mselb"], writes=["imps"])
                at_run_tile(at, q4, "qb", dict(k=kcc[:, ct * 128:(ct + 1) * 128], kkey="kcc", v=vca[:, ct, :], vkey="vca", ns=128,
                                               bias=[(sc.identb[:], cb4[s][:].rearrange("p h t -> p (h t)"), ("identb", ("cb4", s)))]),
                            scale, first=(ct == 0), last=(ct == nct - 1), on_E=on_E)
            combine(0, True)
            for g in range(4):
                if g == 0:
                    P.op("dve", lambda e: e.tensor_scalar(out=imp[:], in0=imps[:, 0, :], scalar1=at.rinv[:, 0:1], scalar2=None, op0=ALU.mult),
                         reads=["imps", "att_rinv"], writes=["imp"])
                else:
                    P.op("dve", lambda e, g=g: e.scalar_tensor_tensor(out=imp[:], in0=imps[:, g, :], scalar=at.rinv[:, g:g + 1], in1=imp[:], op0=ALU.mult, op1=ALU.add),
                         reads=["imps", "att_rinv", "imp"], writes=["imp"])
            P.op("dve", lambda e, qi=qi: e.tensor_tensor(out=imp[:], in0=imp[:], in1=keep[:, qi, :], op=ALU.mult), reads=["imp", "keep"], writes=["imp"])
            P.op("dve", lambda e, qi=qi: e.tensor_tensor(out=imp[:], in0=imp[:], in1=addc[:, qi, :], op=ALU.add), reads=["imp", "addc"], writes=["imp"])
            P.op("dve", lambda e: e.max(out=m8[:, 0:8], in_=imp[:]), reads=["imp"], writes=["nm8"])
            P.op("dve", lambda e: e.match_replace(out=wk[:], in_to_replace=m8[:, 0:8], in_values=imp[:], imm_value=-1e30), reads=["imp", "nm8"], writes=["nwk"])
            P.op("dve", lambda e: e.max(out=m8[:, 8:16], in_=wk[:]), reads=["nwk"], writes=["nm8"])
            P.op("dve", lambda e: e.tensor_reduce(out=thr[:], in_=m8[:, 8:16], axis=AX.X, op=ALU.min), reads=["nm8"], writes=["nthr"])
            P.op("dve", lambda e: e.tensor_scalar(out=wk[:], in0=imp[:], scalar1=thr[:, 0:1], scalar2=None, op0=ALU.is_ge), reads=["imp", "nthr"], writes=["nwk"])
            P.op("dve", lambda e: e.tensor_scalar(out=wk[:], in0=wk[:], scalar1=-1.0, scalar2=-NEG, op0=ALU.add, op1=ALU.mult), reads=["nwk"], writes=["nwk"])
            tp, tk = sc.next_ps()
            P.op("pe", lambda e, tp=tp: e.transpose(out=tp[:64, :128], in_=wk[:], identity=sc.ident[:]), reads=["nwk", "ident"], writes=[tk])
            P.op("dve", lambda e, tp=tp: e.tensor_copy(out=selbT4[:, 0, :], in_=tp[:64, :128]), reads=[tk], writes=["selbT4"])
            P.op("dve", lambda e: e.tensor_copy(out=selbT4[:, 1, :], in_=selbT4[:, 0, :]), reads=["selbT4"], writes=["selbT4"])
            P.op("dve", lambda e: e.tensor_copy(out=selbT4[:, 2:4, :], in_=selbT4[:, 0:2, :]), reads=["selbT4"], writes=["selbT4"])
            for kt in range(qi + 1):
                ks_ = slice(kt * 128, (kt + 1) * 128)
                bias = [(eselb[:, kt, :], selbT4[:].rearrange("p h t -> p (h t)"), ("eselb", "selbT4"))]
                if kt == qi:
                    bias.append((sc.identb[:], tri4[:], ("identb", "tri4")))
                at_run_tile(at, q4, "qb", dict(k=ksb[:, ks_], kkey="ksb", v=vsa[:, kt, :], vkey=f"vaug_s{kh}", ns=128, bias=bias), scale,
                            first=(kt == 0), last=(kt == qi))
            combine(1, False)
            k0 = max(0, qi - 4)
            for kt in range(k0, qi + 1):
                ks_ = slice(kt * 128, (kt + 1) * 128)
                bias = []
                if kt == qi:
                    bias.append((sc.identb[:], tri4[:], ("identb", "tri4")))
                if kt == qi - 4:
                    bias.append((sc.identb[:], atri4[:], ("identb", "atri4")))
                at_run_tile(at, q4, "qb", dict(k=kwb[:, ks_], kkey="kwb", v=vwa[:, kt, :], vkey=f"vaug_w{kh}", ns=128, bias=bias), scale,
                            first=(kt == k0), last=(kt == qi))
            combine(2, False)
            P.dma("sp", o_a[qs, kh * 512:(kh + 1) * 512], oacc[so][:].rearrange("p h d -> p (h d)"), reads=[("oacc", so)], writes=["o_a"])


GN_EPS = 64e-5
C_DEC = -math.exp(-0.5)


def rwkv_consts():
    s = np.arange(64)[:, None]; t = np.arange(64)[None, :]
    msu = (s < t).astype(np.float32); msl = (t < s).astype(np.float32); mui = (s <= t).astype(np.float32); idn = np.eye(64, dtype=np.float32)
    return np.stack([np.tile(m, (1, 16)) for m in (msu, msl, mui, idn)]).astype(np.float32)


def rwkv_inputs(inp, pb, r, T=SEQ):
    fs = slice(r * 1024, (r + 1) * 1024)
    mu = inp["rwkv_mu"]
    rkv = np.stack([np.ascontiguousarray(pb[:, i * 2048:(i + 1) * 2048][:, fs]) for i in range(3)])
    mu_rkv = np.stack([_rep(mu[i * 2048:(i + 1) * 2048][fs], 64) for i in range(3)])
    mu_l = np.zeros((128, 4), np.float32)
    mu_l[:96, 0] = mu[6144:6240]; mu_l[:96, 1] = mu[6240:6336]; mu_l[:, 2] = mu[6336:6464]; mu_l[:, 3] = mu[6464:6592]
    rep = np.stack([_rep(inp[k].reshape(-1)[fs], 64) for k in ("rwkv_w0", "rwkv_a0", "rwkv_kk", "rwkv_ka", "rwkv_gn_g", "rwkv_gn_b", "rwkv_rk")])
    d = {"rw_rkv": rkv, "rw_wdT": np.ascontiguousarray(pb[:, 6144:6240].T), "rw_adT": np.ascontiguousarray(pb[:, 6240:6336].T),
         "rw_gdT": np.ascontiguousarray(pb[:, 6336:6592].T), "rw_mu_rkv": mu_rkv, "rw_mu_l": mu_l, "rw_rep": rep,
         "rw_w2": np.ascontiguousarray(inp["rwkv_w2"][:, fs]), "rw_a2": np.ascontiguousarray(inp["rwkv_a2"][:, fs]),
         "rw_g2": np.ascontiguousarray(inp["rwkv_g2"][:, fs]), "rw_cst": rwkv_consts()}
    d.update(_consts())
    return d


class WT:
    def __init__(self, P, name, shape=(64, 1024)):
        self.t = P.sbuf(name, list(shape), F32); self.key = name
        self.ap = self.t[:]
        self.v3 = self.t[:].rearrange("p (h j) -> p h j", h=16)


def emit_rwkv(P, nc, sc, T):
    din = sc.din
    NCH = T // 64
    rkv = din("rw_rkv", [3, T, 1024]); wdT = din("rw_wdT", [96, T]); adT = din("rw_adT", [96, T]); gdT = din("rw_gdT", [256, T])
    mu_rkv_d = din("rw_mu_rkv", [3, 64, 1024]); mu_l_d = din("rw_mu_l", [128, 4]); rep_d = din("rw_rep", [7, 64, 1024])
    w2d = din("rw_w2", [96, 1024]); a2d = din("rw_a2", [96, 1024]); g2d = din("rw_g2", [256, 1024]); cst_d = din("rw_cst", [4, 64, 1024])
    o_b = nc.dram_tensor("o_b", [T, 1024], F32, kind="ExternalOutput").ap()

    def persistent(name, src):
        w = WT(P, name)
        P.dma("sp", w.ap, src, writes=[w.key])
        return w
    MU = [persistent(f"rw_mu{i}", mu_rkv_d[i]) for i in range(3)]
    W0, A0, KKG, KA_, GNG, GNB, RK = [persistent(f"rw_rep{i}", rep_d[i]) for i in range(7)]
    MSU, MSL, MUI, IDN = [persistent(f"rw_c{i}", cst_d[i]) for i in range(4)]
    mul = P.sbuf("rw_mul", [128, 4], F32); P.dma("sp", mul[:], mu_l_d, writes=["rw_mul"])
    w2sb = P.sbuf("rw_w2sb", [128, 1024], F32); P.dma("sp", w2sb[:96, :], w2d, writes=["rw_w2sb"])
    a2sb = P.sbuf("rw_a2sb", [128, 1024], F32); P.dma("sp", a2sb[:96, :], a2d, writes=["rw_a2sb"])
    g2sb = P.sbuf("rw_g2sb", [128, 2, 1024], F32); P.dma("sp", g2sb[:], g2d.rearrange("(c p) f -> p c f", p=128), writes=["rw_g2sb"])
    ST = WT(P, "rw_ST")
    P.op("dve", lambda e: e.memset(ST.ap, 0.0), writes=[ST.key])
    lraw = P.sbuf("rw_lraw", [128, 4, 65], F32); lxs = P.sbuf("rw_lxs", [128, 4, 64], F32); ld = P.sbuf("rw_ld", [128, 4, 64], F32)
    st16 = [P.sbuf(f"rw_st{i}", [64, 16], F32) for i in range(4)]
    pool = [WT(P, f"rwt{i}") for i in range(25)]
    PB = [P.psum(f"rw_pb{i}", [64, 1024]) for i in range(3)]
    pbi = [0]

    def get():
        return pool.pop(0)

    def put(*ws):
        for w in ws:
            pool.append(w)

    def nextpb():
        i = pbi[0] % 3; pbi[0] += 1
        return PB[i], ("rw_pb", i)

    def tt(o, a, b, op, eng="dve", b_ap=None, bkey=None):
        bap = b.ap if b_ap is None else b_ap
        P.op(eng, lambda e: e.tensor_tensor(out=o.ap, in0=a.ap, in1=bap, op=op), reads=[a.key, bkey or b.key], writes=[o.key])

    def tt_ps(o, ps, pk, b, op):
        P.op("dve", lambda e: e.tensor_tensor(out=o.ap, in0=ps[:], in1=b.ap, op=op), reads=[pk, b.key], writes=[o.key])

    def cp_ps(o, ps, pk, scale=1.0, eng="act"):
        if eng == "act":
            P.op("act", lambda e: e.activation(out=o.ap, in_=ps[:], func=AF.Copy, scale=scale), reads=[pk], writes=[o.key])
        else:
            P.op("dve", lambda e: e.tensor_scalar(out=o.ap, in0=ps[:], scalar1=scale, scalar2=None, op0=ALU.mult), reads=[pk], writes=[o.key])

    def actf(o, a_ap, akey, func, scale=1.0):
        P.op("act", lambda e: e.activation(out=o.ap, in_=a_ap, func=func, scale=scale), reads=[akey], writes=[o.key])

    def bc(ap16):
        return ap16.unsqueeze(2).broadcast_to([64, 16, 64])

    def tt_bc(o, a, s16, skey, op):
        P.op("dve", lambda e: e.tensor_tensor(out=o.v3, in0=a.v3, in1=bc(s16), op=op), reads=[a.key, skey], writes=[o.key])

    def rsum(s16, skey, a):
        P.op("dve", lambda e: e.tensor_reduce(out=s16, in_=a.v3, axis=AX.X, op=ALU.add), reads=[a.key], writes=[skey])

    def mm16(terms):
        ps, pk = nextpb()
        pv = ps[:].rearrange("p (h j) -> p h j", h=16)
        n = len(terms)
        for h in range(16):
            for ti, (L, R) in enumerate(terms):
                P.op("pe", lambda e, h=h, L=L, R=R, ti=ti: e.matmul(pv[:, h, :], lhsT=L.v3[:, h, :], rhs=R.v3[:, h, :], start=(ti == 0), stop=(ti == n - 1),
                                                                    skip_group_check=True),
                     reads=[L.key, R.key], writes=[pk])
        return ps, pk

    def tr16(o, a):
        ps, pk = nextpb()
        pv = ps[:].rearrange("p (h j) -> p h j", h=16)
        for h in range(16):
            P.op("pe", lambda e, h=h: e.transpose(out=pv[:, h, :], in_=a.v3[:, h, :], identity=sc.ident[:64, :64]), reads=[a.key, "ident"], writes=[pk])
        cp_ps(o, ps, pk)

    lsrc = [(wdT, 0, 96), (adT, 0, 96), (gdT, 0, 128), (gdT, 128, 128)]
    for c in range(NCH):
        t0 = c * 64
        xs = []
        for i in range(3):
            cur = get(); prv = get()
            P.dma("sp", cur.ap, rkv[i, t0:t0 + 64, :], writes=[cur.key])
            if c == 0:
                P.op("pool", lambda e, prv=prv: e.memset(prv.t[0:1, :], 0.0), writes=[prv.key])
                P.dma("sp", prv.t[1:64, :], rkv[i, 0:63, :], writes=[prv.key])
            else:
                P.dma("sp", prv.ap, rkv[i, t0 - 1:t0 + 63, :], writes=[prv.key])
            tt(prv, prv, cur, ALU.subtract, eng="pool"); tt(prv, prv, MU[i], ALU.mult, eng="pool"); tt(cur, cur, prv, ALU.add, eng="pool")
            put(prv); xs.append(cur)
        R, K, V = xs
        for sidx, (src, p0, npart) in enumerate(lsrc):
            if c == 0:
                P.op("pool", lambda e, sidx=sidx: e.memset(lraw[:, sidx, 0:1], 0.0), writes=["rw_lraw"])
                P.dma("sp", lraw[:npart, sidx, 1:65], src[p0:p0 + npart, 0:64], writes=["rw_lraw"])
            else:
                P.dma("sp", lraw[:npart, sidx, :], src[p0:p0 + npart, t0 - 1:t0 + 64], writes=["rw_lraw"])
            P.op("dve", lambda e, sidx=sidx, npart=npart: e.tensor_tensor(out=ld[:npart, sidx, :], in0=lraw[:npart, sidx, 0:64], in1=lraw[:npart, sidx, 1:65], op=ALU.subtract),
                 reads=["rw_lraw"], writes=["rw_ld"])
            P.op("dve", lambda e, sidx=sidx, npart=npart: e.scalar_tensor_tensor(out=lxs[:npart, sidx, :], in0=ld[:npart, sidx, :], scalar=mul[:npart, sidx:sidx + 1],
                                                                                 in1=lraw[:npart, sidx, 1:65], op0=ALU.mult, op1=ALU.add),
                 reads=["rw_ld", "rw_lraw", "rw_mul"], writes=["rw_lxs"])
            if sidx != 1:
                P.op("act", lambda e, sidx=sidx, npart=npart: e.activation(out=lxs[:npart, sidx, :], in_=lxs[:npart, sidx, :], func=(AF.Tanh if sidx == 0 else AF.Sigmoid)),
                     reads=["rw_lxs"], writes=["rw_lxs"])

        def lora(parts, wsb_aps, wkey):
            ps, pk = nextpb()
            n = len(parts)
            for blk in range(2):
                for i, ((sidx, npart), wap) in enumerate(zip(parts, wsb_aps)):
                    P.op("pe", lambda e, blk=blk, sidx=sidx, npart=npart, wap=wap, i=i: e.matmul(ps[:, blk * 512:(blk + 1) * 512], lhsT=lxs[:npart, sidx, :],
                                                                                              rhs=wap[:npart, blk * 512:(blk + 1) * 512], start=(i == 0), stop=(i == n - 1)),
                         reads=["rw_lxs", wkey], writes=[pk])
            return ps, pk
        SW = get(); AA = get(); GG = get()
        ps, pk = lora([(0, 96)], [w2sb[:, :]], "rw_w2sb"); tt_ps(SW, ps, pk, W0, ALU.add); actf(SW, SW.ap, SW.key, AF.Sigmoid)
        ps, pk = lora([(1, 96)], [a2sb[:, :]], "rw_a2sb"); tt_ps(AA, ps, pk, A0, ALU.add); actf(AA, AA.ap, AA.key, AF.Sigmoid)
        ps, pk = lora([(2, 128), (3, 128)], [g2sb[:, 0, :], g2sb[:, 1, :]], "rw_g2sb"); cp_ps(GG, ps, pk)
        KK = get(); TMP = get(); KP = get(); BB = get()
        tt(KK, K, KKG, ALU.mult); tt(TMP, KK, KK, ALU.mult)
        rsum(st16[0][:], "rw_st0", TMP)
        P.op("act", lambda e: e.activation(out=st16[0][:], in_=st16[0][:], func=AF.Sqrt), reads=["rw_st0"], writes=["rw_st0"])
        P.op("dve", lambda e: e.tensor_scalar(out=st16[0][:], in0=st16[0][:], scalar1=1e-12, scalar2=None, op0=ALU.max), reads=["rw_st0"], writes=["rw_st0"])
        P.op("dve", lambda e: e.reciprocal(out=st16[0][:], in_=st16[0][:]), reads=["rw_st0"], writes=["rw_st0"])
        tt_bc(KK, KK, st16[0][:], "rw_st0", ALU.mult)
        P.op("dve", lambda e, TMP=TMP, AA=AA: e.scalar_tensor_tensor(out=TMP.ap, in0=AA.ap, scalar=-1.0, in1=KA_.ap, op0=ALU.add, op1=ALU.mult),
             reads=[AA.key, KA_.key], writes=[TMP.key])
        P.op("dve", lambda e, TMP=TMP, K=K, KP=KP: e.scalar_tensor_tensor(out=KP.ap, in0=TMP.ap, scalar=1.0, in1=K.ap, op0=ALU.add, op1=ALU.mult),
             reads=[TMP.key, K.key], writes=[KP.key])
        tt(BB, KK, AA, ALU.mult)
        put(K, AA)
        cps, cpk = nextpb(); tps, tpk = nextpb()
        for blk in range(2):
            sl = slice(blk * 512, (blk + 1) * 512)
            P.op("pe", lambda e, sl=sl, SW=SW, cps=cps: e.matmul(cps[:, sl], lhsT=MUI.t[:, 0:64], rhs=SW.t[:, sl], start=True, stop=True), reads=[MUI.key, SW.key], writes=[cpk])
            P.op("pe", lambda e, sl=sl, SW=SW, tps=tps: e.matmul(tps[:, sl], lhsT=sc.ones[:64, :64], rhs=SW.t[:, sl], start=True, stop=True), reads=["ones", SW.key], writes=[tpk])
        RHO = get(); AL = get(); BE = get(); KA = get(); BEP = get(); KAP = get(); DG = get(); T2 = get()
        actf(TMP, cps[:], cpk, AF.Exp, scale=C_DEC); tt(RHO, R, TMP, ALU.mult)
        tt_ps(T2, cps, cpk, SW, ALU.subtract); actf(TMP, T2.ap, T2.key, AF.Exp, scale=C_DEC); tt(AL, KK, TMP, ALU.mult)
        actf(TMP, cps[:], cpk, AF.Exp, scale=-C_DEC); tt(BE, BB, TMP, ALU.mult); tt(KA, KP, TMP, ALU.mult)
        P.op("dve", lambda e, T2=T2, tps=tps, cps=cps: e.tensor_copy(out=T2.ap, in_=cps[:]), reads=[cpk], writes=[T2.key])
        tt_ps(T2, tps, tpk, T2, ALU.subtract); actf(TMP, T2.ap, T2.key, AF.Exp, scale=C_DEC); tt(BEP, BB, TMP, ALU.mult); tt(KAP, KP, TMP, ALU.mult)
        actf(TMP, tps[:], tpk, AF.Exp, scale=C_DEC); tt(DG, IDN, TMP, ALU.mult)
        put(SW, KK, BB, T2, TMP)
        ALT = get(); BET = get(); KAT = get(); RHT = get()
        tr16(ALT, AL); tr16(BET, BE); tr16(KAT, KA); tr16(RHT, RHO)
        put(BE, KA, RHO)
        NN = get(); NT = get(); MAKT = get(); PRBT = get(); PRKT = get(); X = get(); W = get()
        ps, pk = mm16([(BET, ALT)]); tt_ps(NN, ps, pk, MSU, ALU.mult)
        ps, pk = mm16([(ALT, BET)]); tt_ps(NT, ps, pk, MSL, ALU.mult)
        ps, pk = mm16([(KAT, ALT)]); tt_ps(MAKT, ps, pk, MSU, ALU.mult)
        ps, pk = mm16([(BET, RHT)]); tt_ps(PRBT, ps, pk, MUI, ALU.mult)
        ps, pk = mm16([(KAT, RHT)]); tt_ps(PRKT, ps, pk, MUI, ALU.mult)
        put(ALT, BET, KAT)
        tt(X, IDN, NN, ALU.subtract); tt(W, IDN, NT, ALU.subtract, eng="pool")
        for k in range(5):
            last = (k == 4)
            psP, pkP = mm16([(NT, NN)])
            if not last:
                psQ, pkQ = mm16([(NN, NT)])
            cp_ps(NN, psP, pkP)
            if not last:
                cp_ps(NT, psQ, pkQ, eng="dve")
            psX, pkX = mm16([(W, NN)])
            if not last:
                psW, pkW = mm16([(NN, W)])
            tt_ps(X, psX, pkX, X, ALU.add)
            if not last:
                tt_ps(W, psW, pkW, W, ALU.add)
        put(NN, NT, W)
        MV = get(); AT = get(); NU0 = get(); GM = get(); RTT = get()
        ps, pk = mm16([(MAKT, V)]); cp_ps(MV, ps, pk)
        ps, pk = mm16([(X, AL)]); cp_ps(AT, ps, pk)
        ps, pk = mm16([(X, MV)]); cp_ps(NU0, ps, pk, scale=-1.0)
        put(MAKT, X, MV, AL)
        ps, pk = mm16([(AT, BEP)]); tt_ps(GM, ps, pk, DG, ALU.subtract)
        P.op("dve", lambda e, GM=GM: e.tensor_scalar(out=GM.ap, in0=GM.ap, scalar1=-1.0, scalar2=None, op0=ALU.mult), reads=[GM.key], writes=[GM.key])
        ps, pk = mm16([(AT, PRBT)]); tt_ps(RTT, ps, pk, RHT, ALU.subtract)
        P.op("dve", lambda e, RTT=RTT: e.tensor_scalar(out=RTT.ap, in0=RTT.ap, scalar1=-1.0, scalar2=None, op0=ALU.mult), reads=[RTT.key], writes=[RTT.key])
        put(AT, DG, RHT)
        psY, pkY = mm16([(RTT, ST), (PRKT, V), (PRBT, NU0)])
        psS, pkS = mm16([(GM, ST), (KAP, V), (BEP, NU0)])
        YS = get(); SQ = get()
        cp_ps(YS, psY, pkY)
        cp_ps(ST, psS, pkS, eng="dve")
        put(RTT, PRKT, PRBT, NU0, GM, KAP, BEP)
        rsum(st16[1][:], "rw_st1", YS)
        P.op("dve", lambda e: e.tensor_scalar(out=st16[1][:], in0=st16[1][:], scalar1=1.0 / 64, scalar2=None, op0=ALU.mult), reads=["rw_st1"], writes=["rw_st1"])
        tt_bc(YS, YS, st16[1][:], "rw_st1", ALU.subtract)
        tt(SQ, YS, YS, ALU.mult)
        rsum(st16[2][:], "rw_st2", SQ)
        P.op("dve", lambda e: e.tensor_scalar(out=st16[2][:], in0=st16[2][:], scalar1=1.0 / 64, scalar2=GN_EPS, op0=ALU.mult, op1=ALU.add), reads=["rw_st2"], writes=["rw_st2"])
        P.op("act", lambda e: e.activation(out=st16[2][:], in_=st16[2][:], func=AF.Sqrt), reads=["rw_st2"], writes=["rw_st2"])
        P.op("dve", lambda e: e.reciprocal(out=st16[2][:], in_=st16[2][:]), reads=["rw_st2"], writes=["rw_st2"])
        tt_bc(YS, YS, st16[2][:], "rw_st2", ALU.mult)
        tt(YS, YS, GNG, ALU.mult); tt(YS, YS, GNB, ALU.add)
        tt(SQ, R, KP, ALU.mult); tt(SQ, SQ, RK, ALU.mult)
        rsum(st16[3][:], "rw_st3", SQ)
        tt_bc(SQ, V, st16[3][:], "rw_st3", ALU.mult)
        tt(YS, YS, SQ, ALU.add); tt(YS, YS, GG, ALU.mult)
        P.dma("sp", o_b[t0:t0 + 64, :], YS.ap, reads=[YS.key], writes=["o_b"])
        put(YS, SQ, R, KP, V, GG)
        assert len(pool) == 25, len(pool)


def build_b0(T=4096, do_nsa=True, do_rwkv=True):
    nc = new_nc(); P = Prog(nc); sc = SeqCommon(P, nc, T, nps=3 if do_nsa else 2)
    if do_nsa:
        pos = sc.din("pos", [128, T], I32)
        C, S = sc.rope_tables(pos, T, "k")
        at = Attn(sc)
        emit_nsa(P, nc, sc, at, T, C, S)
    if do_rwkv:
        emit_rwkv(P, nc, sc, T)
    P.finish()
    return nc, P


def _rep(a, n=128):
    return np.ascontiguousarray(np.broadcast_to(a[None], (n,) + a.shape))


def _consts():
    return {"RT_d": const_RT(), "inv_d": const_inv(), "ident_d": np.eye(128, dtype=np.float32)}


def nsa_inputs(inp, pa, pos, r, T=SEQ):
    q = pa[:, :2048].reshape(T, 16, 128)
    sec = lambda i: pa[:, 2048 + 512 * i:2048 + 512 * (i + 1)].reshape(T, 4, 128)
    kc, vc, ks, vs, kw, vw = [sec(i) for i in range(6)]
    gates = pa[:, 5120:5168]
    fm = lambda a: np.ascontiguousarray(a[:, 2 * r:2 * r + 2].transpose(1, 2, 0))
    tm = lambda a: np.ascontiguousarray(a[:, 2 * r:2 * r + 2].transpose(1, 0, 2))
    cpos = np.zeros(256, np.int32); cpos[:255] = pos[16 * np.arange(255) + 31]
    d = {"n_qT": np.ascontiguousarray(q[:, 8 * r:8 * r + 8].transpose(1, 2, 0)), "n_kcT": fm(kc), "n_vcT": fm(vc), "n_ksT": fm(ks), "n_kwT": fm(kw),
         "n_vs": tm(vs), "n_vw": tm(vw), "n_gates": np.ascontiguousarray(gates[:, 24 * r:24 * r + 24]),
         "n_cpos": _rep(cpos),
         "n_q_norm": inp['nsa_q_norm'], "n_kc_norm": inp['nsa_kc_norm'], "n_ks_norm": inp['nsa_ks_norm'], "n_kw_norm": inp['nsa_kw_norm'],
         "n_peT_k": np.ascontiguousarray(inp['nsa_pe_k'].T), "n_peT_v": np.ascontiguousarray(inp['nsa_pe_v'].T),
         "n_w1_k": inp['nsa_ck_w1'], "n_w1_v": inp['nsa_cv_w1'], "n_w2_k": inp['nsa_ck_w2'], "n_w2_v": inp['nsa_cv_w2'],
         "pos": _rep(pos.astype(np.int32))}
    d.update(_consts())
    return d


_NSA_CONSTS = None
_RWKV_HOOK = None


def _launch(nc, in_maps):
    t0 = time.time()
    res = run_bass_kernel_spmd(nc, in_maps, core_ids=list(range(8)))
    print(f"[kernel] launch took {time.time() - t0:.1f}s", flush=True)
    return res.results


def kernel(**inp):
    global _NSA_CONSTS
    f32 = lambda a: np.ascontiguousarray(np.asarray(a, dtype=np.float32))
    inp = {k: np.asarray(v) for k, v in inp.items()}
    x = inp["x"]; mem = inp["mem"]; pos = inp["positions"].astype(np.int32)
    B, T, D = x.shape
    cores = [(c // 2, c % 2) for c in range(8)]
    tok = lambda r: slice(r * 2048, (r + 1) * 2048)
    nc, _ = build_token_local(False, True, EVEN_IN, EVEN_SECTIONS)
    res = _launch(nc, [{"xT": f32(x[b, tok(r)].T), "mix_norm": inp["l0_mix_norm"], "w_in": inp["l0_w_in"]} for b, r in cores])
    proj0 = np.empty((B, T, EVEN_IN), np.float32)
    for (b, r), o in zip(cores, res):
        proj0[b, tok(r)] = o["projT"].T
    if _NSA_CONSTS is None:
        _NSA_CONSTS = {"n_" + k: v for k, v in nsa_consts(T).items()}
    nc, _ = build_b0(T, do_nsa=True, do_rwkv=False)
    ins = []
    for b, r in cores:
        d = nsa_inputs(inp, proj0[b, :, :NSA_COLS], pos[b], r, T); d.update(_NSA_CONSTS); ins.append(d)
    res = _launch(nc, ins)
    omix = np.zeros((B, T, 4096), np.float32)
    for (b, r), o in zip(cores, res):
        omix[b, :, r * 1024:(r + 1) * 1024] = o["o_a"]
    if _RWKV_HOOK is not None:
        for b in range(B):
            omix[b, :, 2048:] = _RWKV_HOOK(b)
    else:
        nc, _ = build_b0(T, do_nsa=False, do_rwkv=True)
        res = _launch(nc, [rwkv_inputs(inp, proj0[b, :, NSA_COLS:], r, T) for b, r in cores])
        for (b, r), o in zip(cores, res):
            omix[b, :, 2048 + r * 1024:2048 + (r + 1) * 1024] = o["o_b"]
    def c_inputs(l, xT, oT, b):
        return {"xT": xT, "oT": oT, "w_out": inp[f"l{l}_w_out"], "memT": f32(mem[b].T), "mem_norm": inp["mem_norm"], "mem_w_kv": inp["mem_w_kv"],
                "mem_k_norm": inp["mem_k_norm"], "xattn_norm": inp[f"l{l}_xattn_norm"], "wq": inp[f"l{l}_mem_wq"], "q_norm": inp[f"l{l}_mem_q_norm"],
                "wo": inp[f"l{l}_mem_wo"], "ffn_norm": inp[f"l{l}_ffn_norm"], "w1": inp[f"l{l}_w1"], "w3": inp[f"l{l}_w3"], "w2": inp[f"l{l}_w2"],
                "ident_d": np.eye(128, dtype=np.float32)}
    nc, _ = build_token_local(True, True, ODD_IN, ODD_SECTIONS)
    ins = []
    for b, r in cores:
        d = c_inputs(0, f32(x[b, tok(r)].T), f32(omix[b, tok(r)].T), b)
        d.update({"mix_norm": inp["l1_mix_norm"], "w_in": inp["l1_w_in"]}); ins.append(d)
    res = _launch(nc, ins)
    x3T = [o["x3T"] for o in res]
    proj1 = np.empty((B, T, ODD_IN), np.float32)
    for (b, r), o in zip(cores, res):
        proj1[b, tok(r)] = o["projT"].T
    nc, _ = build_dsa_index()
    ins = []
    for b, r in cores:
        po = proj1[b]; tq = np.arange(r, T, 2)
        qi = po[:, 5120:9216].reshape(T, 32, 128)
        d = {"qiT": np.ascontiguousarray(qi[tq].transpose(1, 2, 0)), "wi": np.ascontiguousarray(po[tq, 9344:9376]), "posq": _rep(pos[b][tq]), "posk": _rep(pos[b]),
             "kiT": np.ascontiguousarray(po[:, 9216:9344].T), "ki_norm": inp["dsa_ki_norm"], "cbias": dsa_index_consts(r)}
        d.update(_consts()); ins.append(d)
    res = _launch(nc, ins)
    bias = np.empty((B, T, T), np.float32)
    for (b, r), o in zip(cores, res):
        bias[b, r::2] = o["biasQ"]
    nc, _ = build_dsa_attn(T)
    ins = []
    for b, r in cores:
        po = proj1[b]
        q = po[:, :4096].reshape(T, 32, 128); k = po[:, 4096:4608].reshape(T, 4, 128); v = po[:, 4608:5120].reshape(T, 4, 128)
        d = {"qT": np.ascontiguousarray(q[:, 16 * r:16 * r + 16].transpose(1, 2, 0)), "kT": np.ascontiguousarray(k[:, 2 * r:2 * r + 2].transpose(1, 2, 0)),
             "v": np.ascontiguousarray(v[:, 2 * r:2 * r + 2].transpose(1, 0, 2)), "biasT": np.ascontiguousarray(bias[b].T), "pos": _rep(pos[b]),
             "q_norm": inp["dsa_q_norm"], "k_norm": inp["dsa_k_norm"]}
        d.update(_consts()); ins.append(d)
    res = _launch(nc, ins)
    od = np.empty((B, T, 4096), np.float32)
    for (b, r), o in zip(cores, res):
        od[b, :, r * 2048:(r + 1) * 2048] = o["o"]
    nc, _ = build_token_local(True, False, 0, None)
    res = _launch(nc, [c_inputs(1, x3T[c], f32(od[b, tok(r)].T), b) for c, (b, r) in enumerate(cores)])
    out = np.empty((B, T, D), np.float32)
    for (b, r), o in zip(cores, res):
        out[b, tok(r)] = o["x3T"].T
    return out
```

```python
import contextlib, time, math
import numpy as np
import concourse.bass as bass
import concourse.mybir as mybir
from concourse.bass_utils import run_bass_kernel_spmd

F32 = mybir.dt.float32; BF16 = mybir.dt.bfloat16; I32 = mybir.dt.int32; U32 = mybir.dt.uint32
AF = mybir.ActivationFunctionType; ALU = mybir.AluOpType; AX = mybir.AxisListType


class Prog:
    COMPUTE = ("pe", "act", "dve", "pool")
    NDMA = 8

    def __init__(self, nc):
        self.nc = nc
        self.es = contextlib.ExitStack()
        self.lists = {e: [] for e in ("pe", "act", "dve", "pool", "sp")}
        self.cnt = {e: 0 for e in self.COMPUTE}
        self.sem = {e: self.es.enter_context(nc.semaphore("s_" + e)) for e in self.COMPUTE}
        self.dsem = {q: [self.es.enter_context(nc.semaphore(f"d_{q}{i}")) for i in range(self.NDMA)]
                     for q in ("sp", "pool", "act")}
        self.dval = {q: [0] * self.NDMA for q in ("sp", "pool", "act")}
        self.dnext = {q: 0 for q in ("sp", "pool", "act")}
        self.waited = {e: {} for e in self.lists}
        self.lastw = {}
        self.readers = {}
        self.semobj = {}
        self.ninst = 0

    def sbuf(self, name, shape, dt):
        return self.es.enter_context(self.nc.sbuf_tensor(name, list(shape), dt))

    def psum(self, name, shape, dt=F32):
        return self.es.enter_context(self.nc.psum_tensor(name, list(shape), dt))

    def dram(self, name, shape, dt, kind="Internal"):
        return self.nc.dram_tensor(name, list(shape), dt, kind=kind).ap()

    def _key(self, sem):
        k = id(sem)
        self.semobj[k] = sem
        return k

    def _deps(self, eng, reads, writes):
        deps = {}
        def add(tok):
            if tok is None:
                return
            k, v, src = tok
            if src == "pe" and eng == "pe":
                return
            if deps.get(k, 0) < v:
                deps[k] = v
        for r in reads:
            for tok in self.lastw.get(r, {}).values():
                add(tok)
        for w in writes:
            for tok in self.lastw.get(w, {}).values():
                add(tok)
            for tok in self.readers.get(w, {}).values():
                add(tok)
        out = []
        wd = self.waited[eng]
        for k, v in deps.items():
            if wd.get(k, 0) < v:
                wd[k] = v
                out.append((self.semobj[k], v))
        return out

    def _commit(self, tok, reads, writes):
        for w in writes:
            self.lastw.setdefault(w, {})[tok[0]] = tok
            self.readers[w] = {}
        for r in reads:
            d = self.readers.setdefault(r, {})
            if tok[0] not in d or d[tok[0]][1] < tok[1]:
                d[tok[0]] = tok

    def op(self, eng, fn, reads=(), writes=()):
        waits = self._deps(eng, reads, writes)
        sem = self.sem[eng]
        self.cnt[eng] += 1
        tok = (self._key(sem), self.cnt[eng], eng)
        def emit(e, fn=fn, waits=waits, sem=sem):
            for s, v in waits:
                e.wait_ge(s, v)
            fn(e).then_inc(sem, 1)
        self.lists[eng].append(emit)
        self._commit(tok, reads, writes)
        self.ninst += 1

    def dma(self, q, out, in_, reads=(), writes=(), **kw):
        eng = q
        waits = self._deps(eng, reads, writes)
        i = self.dnext[q]
        self.dnext[q] = (i + 1) % self.NDMA
        sem = self.dsem[q][i]
        k = self._key(sem)
        prev = self.dval[q][i]
        if prev and self.waited[eng].get(k, 0) < prev:
            self.waited[eng][k] = prev
            waits = waits + [(sem, prev)]
        self.dval[q][i] = prev + 16
        tok = (k, prev + 16, "dma")
        def emit(e, waits=waits, sem=sem):
            for s, v in waits:
                e.wait_ge(s, v)
            e.dma_start(out=out, in_=in_, **kw).then_inc(sem, 16)
        self.lists[eng].append(emit)
        self._commit(tok, reads, writes)
        self.ninst += 1

    def finish(self):
        waits = []
        for q in self.dsem:
            for i, s in enumerate(self.dsem[q]):
                if self.dval[q][i]:
                    waits.append((s, self.dval[q][i]))
        for e in self.COMPUTE:
            if self.cnt[e]:
                waits.append((self.sem[e], self.cnt[e]))
        def emit(e):
            for s, v in waits:
                e.wait_ge(s, v)
        self.lists["sp"].append(emit)
        L = self.lists
        with self.nc.Block() as block:
            @block.sync
            def _(e):
                for f in L["sp"]:
                    f(e)
            @block.tensor
            def _(e):
                for f in L["pe"]:
                    f(e)
            @block.scalar
            def _(e):
                for f in L["act"]:
                    f(e)
            @block.vector
            def _(e):
                for f in L["dve"]:
                    f(e)
            @block.gpsimd
            def _(e):
                for f in L["pool"]:
                    f(e)
        self.es.close()


def new_nc():
    return bass.Bass("TRN2", target_bir_lowering=False)

D_MODEL = 4096; BATCH = 4; SEQ = 4096; HD = 128
NSA_HEADS = 16; NSA_KVH = 4; RW = 2048; RH = 64; RHEADS = 32
L_DEC = 96; L_AAA = 96; L_GATE = 256
D_FF = 11008
NSA_Q = 2048; NSA_KV = 512
NSA_COLS = 2048 + 6 * 512 + 48
RWKV_COLS = 3 * 2048 + 96 + 96 + 256
EVEN_IN = NSA_COLS + RWKV_COLS
ODD_IN = 4096 + 512 + 512 + 4096 + 128 + 32
EPS = 1e-6
NEG = -30000.0


def col_tiles(sections):
    out = []
    for s, n in sections:
        o = 0
        while o < n:
            w = min(128, n - o)
            out.append((s + o, w))
            o += w
    return out


EVEN_SECTIONS = [(0, 2048)] + [(2048 + 512 * i, 512) for i in range(6)] + [(5120, 48)] + \
    [(5168 + 2048 * i, 2048) for i in range(3)] + [(11312, 96), (11408, 96), (11504, 256)]
ODD_SECTIONS = [(0, 4096), (4096, 512), (4608, 512), (5120, 4096), (9216, 128), (9344, 32)]


class TL:
    TG = 512

    def __init__(self, P):
        self.P = P
        self.actbuf = P.sbuf("actbuf", [128, 86 * 512], BF16)
        self.wbuf = [P.sbuf(f"wbuf{i}", [128, 86 * 128], BF16) for i in range(2)]
        self.ps = [P.psum(f"ps{i}", [128, 512]) for i in range(8)]
        self.xf = [P.sbuf(f"xf{i}", [128, 1024], F32) for i in range(2)]
        self.sq = [P.sbuf(f"sq{i}", [128, 1024], F32) for i in range(2)]
        self.rstd = P.sbuf("rstd", [128, 1024], F32)
        self.ob = [P.sbuf(f"ob{i}", [128, 512], F32) for i in range(4)]
        self.rb = [P.sbuf(f"rb{i}", [128, 512], F32) for i in range(4)]
        self.ones = P.sbuf("ones_f", [128, 128], F32)
        self.onesb = P.sbuf("ones_b", [128, 128], BF16)
        P.op("dve", lambda e: e.memset(self.ones[:], 1.0), writes=["ones"])
        P.op("dve", lambda e: e.memset(self.onesb[:], 1.0), writes=["onesb"])
        self.wi = 0; self.pi = 0; self.oi = 0; self.ri = 0

    def act_view(self, KC, TG):
        return self.actbuf[:, :KC * TG].rearrange("p (c t) -> p c t", c=KC)

    def next_ps(self):
        i = self.pi % 8; self.pi += 1
        return self.ps[i], ("ps", i)

    def load_gain(self, name, gvec, K):
        P = self.P
        g = P.sbuf("g_" + name, [128, K // 128], F32)
        P.dma("sp", g[:], gvec.rearrange("(c p) -> p c", p=128), writes=["g_" + name], allow_slow_non_contiguous=True)
        return g

    def norm_act(self, xT, xkey, t0, TG, gain, gkey, K=4096):
        P = self.P; KC = K // 128
        act = self.act_view(KC, TG)
        nb = (TG + 511) // 512
        bw = min(TG, 512)
        spl = [self.next_ps() for _ in range(nb)]
        for c in range(KC):
            s = c % 2
            P.dma("sp", self.xf[s][:, :TG], xT[c * 128:(c + 1) * 128, t0:t0 + TG], reads=[xkey], writes=[("xf", s)])
            P.op("act", lambda e, s=s: e.activation(out=self.sq[s][:, :TG], in_=self.xf[s][:, :TG], func=AF.Square),
                 reads=[("xf", s)], writes=[("sq", s)])
            for b in range(nb):
                sps, skey = spl[b]
                P.op("pe", lambda e, s=s, c=c, b=b, sps=sps: e.matmul(sps[:, :bw], lhsT=self.ones[:], rhs=self.sq[s][:, b * 512:b * 512 + bw],
                                                                      start=(c == 0), stop=(c == KC - 1)),
                     reads=[("sq", s), "ones"], writes=[skey])
        for b in range(nb):
            sps, skey = spl[b]
            P.op("dve", lambda e, b=b, sps=sps: e.tensor_scalar(out=self.rstd[:, b * 512:b * 512 + bw], in0=sps[:, :bw], scalar1=1.0 / K, scalar2=EPS,
                                                              op0=ALU.mult, op1=ALU.add), reads=[skey], writes=["rstd"])
        P.op("act", lambda e: e.activation(out=self.rstd[:, :TG], in_=self.rstd[:, :TG], func=AF.Sqrt), reads=["rstd"], writes=["rstd"])
        P.op("dve", lambda e: e.reciprocal(out=self.rstd[:, :TG], in_=self.rstd[:, :TG]), reads=["rstd"], writes=["rstd"])
        for c in range(KC):
            s = c % 2
            P.dma("sp", self.xf[s][:, :TG], xT[c * 128:(c + 1) * 128, t0:t0 + TG], reads=[xkey], writes=[("xf", s)])
            P.op("dve", lambda e, s=s, c=c: e.scalar_tensor_tensor(out=act[:, c, :], in0=self.xf[s][:, :TG], scalar=gain[:, c:c + 1],
                                                                  in1=self.rstd[:, :TG], op0=ALU.mult, op1=ALU.mult),
                 reads=[("xf", s), gkey, "rstd"], writes=["actbuf"])

    def cast_act(self, srcT, skey, t0, TG, K):
        KC = K // 128
        act = self.act_view(KC, TG)
        self.P.dma("pool", act, srcT[:, t0:t0 + TG].rearrange("(c p) t -> p c t", p=128), reads=[skey], writes=["actbuf"])

    def gemm(self, jobs, K, TG, evac, act=None, akey="actbuf"):
        P = self.P; KC = K // 128
        if act is None:
            act = self.act_view(KC, TG)
        WG = 256 if KC <= 32 else 128
        groups = []
        for (W, c0, w) in jobs:
            if groups and groups[-1][0] is W and groups[-1][1] + groups[-1][2] == c0 and groups[-1][2] + w <= WG:
                groups[-1][2] += w; groups[-1][3].append((c0, w))
            else:
                groups.append([W, c0, w, [(c0, w)]])
        ji = 0
        for (W, g0, gw, tl) in groups:
            s = self.wi % 2; self.wi += 1
            wkey = ("wbuf", s)
            wv = self.wbuf[s][:, :KC * gw].rearrange("p (c n) -> p c n", c=KC)
            P.dma("pool", wv, W[:, g0:g0 + gw].rearrange("(c p) n -> p c n", p=128), writes=[wkey])
            for (c0, w) in tl:
                nb = (TG + 511) // 512
                bw = min(TG, 512)
                for tb in range(nb):
                    ps, pkey = self.next_ps()
                    for c in range(KC):
                        P.op("pe", lambda e, c=c, wv=wv, o=c0 - g0, w=w, ps=ps, tb=tb: e.matmul(
                            ps[:w, :bw], lhsT=wv[:, c, o:o + w], rhs=act[:, c, tb * 512:tb * 512 + bw], start=(c == 0), stop=(c == KC - 1)),
                            reads=[wkey, akey], writes=[pkey])
                    evac(ji, c0, w, tb, ps, pkey)
                ji += 1

    def evac_store(self, dstT, dkey, t0, TG, resT=None, rkey=None):
        P = self.P
        bw = min(TG, 512)
        def evac(ji, c0, w, tb, ps, pkey):
            o = self.oi % 4; self.oi += 1
            tt = t0 + tb * 512
            if resT is None:
                if o % 2:
                    P.op("act", lambda e: e.copy(out=self.ob[o][:w, :bw], in_=ps[:w, :bw]), reads=[pkey], writes=[("ob", o)])
                else:
                    P.op("dve", lambda e: e.tensor_copy(out=self.ob[o][:w, :bw], in_=ps[:w, :bw]), reads=[pkey], writes=[("ob", o)])
            else:
                r = self.ri % 4; self.ri += 1
                P.dma("sp", self.rb[r][:w, :bw], resT[c0:c0 + w, tt:tt + bw], reads=[rkey], writes=[("rb", r)])
                P.op("dve", lambda e: e.tensor_tensor(out=self.ob[o][:w, :bw], in0=ps[:w, :bw], in1=self.rb[r][:w, :bw], op=ALU.add),
                     reads=[pkey, ("rb", r)], writes=[("ob", o)])
            P.dma("sp", dstT[c0:c0 + w, tt:tt + bw], self.ob[o][:w, :bw], reads=[("ob", o)], writes=[dkey])
        return evac


def build_token_local(do_c, do_a, n_in, sections, TN=2048):
    nc = new_nc()
    P = Prog(nc)
    tl = TL(P)
    TG = 1024
    din = lambda name, shape, dt=F32: nc.dram_tensor(name, list(shape), dt, kind="ExternalInput").ap()
    dout = lambda name, shape, dt=F32: nc.dram_tensor(name, list(shape), dt, kind="ExternalOutput").ap()
    xT = din("xT", [4096, TN])
    scale = HD ** -0.5
    if do_c:
        oT = din("oT", [4096, TN]); w_out = din("w_out", [4096, 4096])
        memT = din("memT", [4096, 256]); mem_norm = din("mem_norm", [4096]); mem_w_kv = din("mem_w_kv", [4096, 1024])
        mem_k_norm = din("mem_k_norm", [128]); xattn_norm = din("xattn_norm", [4096]); wq = din("wq", [4096, 512])
        q_norm = din("q_norm", [128]); wo = din("wo", [512, 4096]); ffn_norm = din("ffn_norm", [4096])
        w1 = din("w1", [4096, D_FF]); w3 = din("w3", [4096, D_FF]); w2 = din("w2", [D_FF, 4096])
        ident_d = din("ident_d", [128, 128])
        x1T = P.dram("x1T", [4096, TN], F32); x2T = P.dram("x2T", [4096, TN], F32); uT = P.dram("uT", [D_FF, TN], BF16)
        x3T = dout("x3T", [4096, TN])
        ident = P.sbuf("ident", [128, 128], F32)
        P.dma("sp", ident[:], ident_d, writes=["ident"])
        gk = P.sbuf("gk", [128, 1], F32); gq = P.sbuf("gq", [128, 1], F32)
        P.dma("sp", gk[:], mem_k_norm.rearrange("(p o) -> p o", o=1), writes=["gk"])
        P.dma("sp", gq[:], q_norm.rearrange("(p o) -> p o", o=1), writes=["gq"])
        g_mem = tl.load_gain("mem", mem_norm, 4096); g_xa = tl.load_gain("xa", xattn_norm, 4096); g_ffn = tl.load_gain("ffn", ffn_norm, 4096)
        mkT = P.sbuf("mkT", [128, 4, 256], BF16); mv = P.sbuf("mv", [128, 2, 4, 128], BF16)
        qf = P.sbuf("qf", [128, 512], F32); qn = P.sbuf("qn", [128, 512], BF16); rq = P.sbuf("rq", [128, 512], F32)
        Eb = P.sbuf("Eb", [128, 2, 512], BF16); rinv = P.sbuf("rinv", [128, 512], F32)
        o_act = P.sbuf("o_act", [128, 4, 1024], BF16)
        sil = [P.sbuf(f"sil{i}", [128, 512], F32) for i in range(2)]
        ub = [P.sbuf(f"ub{i}", [128, 512], BF16) for i in range(2)]

        def head_rms(src_ps, pkey, ncol, gcol, gkey, dst, dkey):
            P.op("dve", lambda e: e.tensor_copy(out=qf[:, :ncol], in_=src_ps[:, :ncol]), reads=[pkey], writes=["qf"])
            P.op("act", lambda e: e.activation(out=rq[:, :ncol], in_=qf[:, :ncol], func=AF.Square), reads=["qf"], writes=["rq"])
            sp2, sk2 = tl.next_ps()
            P.op("pe", lambda e: e.matmul(sp2[:, :ncol], lhsT=tl.ones[:], rhs=rq[:, :ncol], start=True, stop=True), reads=["rq", "ones"], writes=[sk2])
            P.op("dve", lambda e: e.tensor_scalar(out=rq[:, :ncol], in0=sp2[:, :ncol], scalar1=1.0 / 128, scalar2=EPS, op0=ALU.mult, op1=ALU.add),
                 reads=[sk2], writes=["rq"])
            P.op("act", lambda e: e.activation(out=rq[:, :ncol], in_=rq[:, :ncol], func=AF.Sqrt), reads=["rq"], writes=["rq"])
            P.op("dve", lambda e: e.reciprocal(out=rq[:, :ncol], in_=rq[:, :ncol]), reads=["rq"], writes=["rq"])
            P.op("dve", lambda e: e.scalar_tensor_tensor(out=dst, in0=qf[:, :ncol], scalar=gcol[:, 0:1], in1=rq[:, :ncol], op0=ALU.mult, op1=ALU.mult),
                 reads=["qf", "rq", gkey], writes=[dkey])

        tl.norm_act(memT, "memT", 0, 256, g_mem, "g_mem")
        def evac_mem(ji, c0, w, tb, ps, pkey):
            h = ji % 4
            if ji < 4:
                head_rms(ps, pkey, 256, gk, "gk", mkT[:, h, :], "mkT")
            else:
                P.op("dve", lambda e: e.tensor_copy(out=qf[:, :256], in_=ps[:, :256]), reads=[pkey], writes=["qf"])
                for mt in range(2):
                    tp, tk = tl.next_ps()
                    P.op("pe", lambda e, mt=mt, tp=tp: e.transpose(out=tp[:, :128], in_=qf[:, mt * 128:(mt + 1) * 128], identity=ident[:]),
                         reads=["qf", "ident"], writes=[tk])
                    P.op("act", lambda e, mt=mt, tp=tp, h=h: e.copy(out=mv[:, mt, h, :], in_=tp[:, :128]), reads=[tk], writes=["mv"])
        tl.gemm([(mem_w_kv, c0, w) for (c0, w) in col_tiles([(0, 1024)])], 4096, 256, evac_mem)

        for t0 in range(0, TN, TG):
            tl.cast_act(oT, "oT", t0, TG, 4096)
            tl.gemm([(w_out, c0, w) for (c0, w) in col_tiles([(0, 4096)])], 4096, TG, tl.evac_store(x1T, "x1T", t0, TG, xT, "xT"))
            tl.norm_act(x1T, "x1T", t0, TG, g_xa, "g_xa")
            def evac_q(ji, c0, w, tb, ps, pkey):
                h = ji
                tsl = slice(tb * 512, (tb + 1) * 512)
                head_rms(ps, pkey, 512, gq, "gq", qn[:], "qn")
                for mt in range(2):
                    sp_, sk_ = tl.next_ps()
                    P.op("pe", lambda e, mt=mt, sp_=sp_: e.matmul(sp_[:], lhsT=mkT[:, h, mt * 128:(mt + 1) * 128], rhs=qn[:], start=True, stop=True),
                         reads=["mkT", "qn"], writes=[sk_])
                    P.op("act", lambda e, mt=mt, sp_=sp_: e.activation(out=Eb[:, mt, :], in_=sp_[:], func=AF.Exp, scale=scale), reads=[sk_], writes=[("Eb", mt)])
                op_, ok_ = tl.next_ps(); rp_, rk_ = tl.next_ps()
                for mt in range(2):
                    P.op("pe", lambda e, mt=mt: e.matmul(op_[:], lhsT=mv[:, mt, h, :], rhs=Eb[:, mt, :], start=(mt == 0), stop=(mt == 1)),
                         reads=["mv", ("Eb", mt)], writes=[ok_])
                for mt in range(2):
                    P.op("pe", lambda e, mt=mt: e.matmul(rp_[:], lhsT=tl.onesb[:], rhs=Eb[:, mt, :], start=(mt == 0), stop=(mt == 1)),
                         reads=["onesb", ("Eb", mt)], writes=[rk_])
                P.op("dve", lambda e: e.reciprocal(out=rinv[:], in_=rp_[:]), reads=[rk_], writes=["rinv"])
                P.op("dve", lambda e: e.tensor_tensor(out=o_act[:, h, tsl], in0=op_[:], in1=rinv[:], op=ALU.mult), reads=[ok_, "rinv"], writes=["o_act"])
            tl.gemm([(wq, c0, w) for (c0, w) in col_tiles([(0, 512)])], 4096, TG, evac_q)
            tl.gemm([(wo, c0, w) for (c0, w) in col_tiles([(0, 4096)])], 512, TG, tl.evac_store(x2T, "x2T", t0, TG, x1T, "x1T"),
                    act=o_act, akey="o_act")
            tl.norm_act(x2T, "x2T", t0, TG, g_ffn, "g_ffn")
            jobs = []
            ft = col_tiles([(0, D_FF)])
            for i in range(0, len(ft), 2):
                pair = ft[i:i + 2]
                jobs += [(w1, c0, w) for (c0, w) in pair] + [(w3, c0, w) for (c0, w) in pair]
            held = {}
            sctr = [0]
            def evac_ffn(ji, c0, w, tb, ps, pkey, t0=t0, held=held, sctr=sctr):
                r = ji % 4
                if r < 2:
                    held[(r, tb)] = (ps, pkey)
                    return
                pa, ka = held[(r - 2, tb)]
                s = sctr[0] % 2; sctr[0] += 1
                tt = t0 + tb * 512
                P.op("act", lambda e: e.activation(out=sil[s][:w, :], in_=pa[:w, :], func=AF.Silu), reads=[ka], writes=[("sil", s)])
                P.op("dve", lambda e: e.tensor_tensor(out=ub[s][:w, :], in0=ps[:w, :], in1=sil[s][:w, :], op=ALU.mult),
                     reads=[pkey, ("sil", s)], writes=[("ub", s)])
                P.dma("sp", uT[c0:c0 + w, tt:tt + 512], ub[s][:w, :], reads=[("ub", s)], writes=["uT"])
            tl.gemm(jobs, 4096, TG, evac_ffn)
            for hf in range(TG // 512):
                tt = t0 + hf * 512
                P.dma("sp", tl.act_view(86, 512), uT[:, tt:tt + 512].rearrange("(c p) t -> p c t", p=128), reads=["uT"], writes=["actbuf"])
                tl.gemm([(w2, c0, w) for (c0, w) in col_tiles([(0, 4096)])], D_FF, 512, tl.evac_store(x3T, "x3T", tt, 512, x2T, "x2T"))
        xn, xnkey = x3T, "x3T"
    else:
        xn, xnkey = xT, "xT"
    if do_a:
        mix_norm = din("mix_norm", [4096]); w_in = din("w_in", [4096, n_in])
        projT = dout("projT", [n_in, TN])
        g_mix = tl.load_gain("mix", mix_norm, 4096)
        for t0 in range(0, TN, TG):
            tl.norm_act(xn, xnkey, t0, TG, g_mix, "g_mix")
            tl.gemm([(w_in, c0, w) for (c0, w) in col_tiles(sections)], 4096, TG, tl.evac_store(projT, "projT", t0, TG))
    P.finish()
    return nc, P


def const_RT():
    RT = np.zeros((128, 128), np.float32)
    for m in range(16):
        RT[m + 16, m] = -1.0
        RT[m, m + 16] = 1.0
    return RT


def const_inv():
    inv = np.zeros((128, 1), np.float32)
    inv[:32, 0] = np.tile((500000.0 ** (-np.arange(0, 32, 2, dtype=np.float32) / 32)).astype(np.float32), 2)
    return inv


class SeqCommon:
    def __init__(self, P, nc, T, nps=3):
        self.P = P; self.nc = nc; self.T = T; self.nps = nps
        din = lambda name, shape, dt=F32: nc.dram_tensor(name, list(shape), dt, kind="ExternalInput").ap()
        self.din = din
        self.ones = P.sbuf("ones_f", [128, 128], F32)
        self.onesb = P.sbuf("ones_b", [128, 128], BF16)
        P.op("dve", lambda e: e.memset(self.ones[:], 1.0), writes=["ones"])
        P.op("dve", lambda e: e.memset(self.onesb[:], 1.0), writes=["onesb"])
        self.RT = P.sbuf("RT", [128, 128], F32); self.inv = P.sbuf("inv", [128, 1], F32); self.ident = P.sbuf("ident", [128, 128], F32)
        P.dma("sp", self.RT[:], din("RT_d", [128, 128]), writes=["RT"])
        P.dma("sp", self.inv[:], din("inv_d", [128, 1]), writes=["inv"])
        P.dma("sp", self.ident[:], din("ident_d", [128, 128]), writes=["ident"])
        self.identb = P.sbuf("identb", [128, 128], BF16)
        P.op("dve", lambda e: e.tensor_copy(out=self.identb[:], in_=self.ident[:]), reads=["ident"], writes=["identb"])
        self.ps = [P.psum(f"ps{i}", [128, 512]) for i in range(nps)]
        self.pi = 0
        self.xf = [P.sbuf(f"pxf{i}", [128, 512], F32) for i in range(2)]
        self.t1 = P.sbuf("pt1", [128, 512], F32); self.t2 = P.sbuf("pt2", [128, 512], F32); self.t3 = P.sbuf("pt3", [128, 512], F32)
        self.xi = 0

    def next_ps(self):
        i = self.pi % self.nps; self.pi += 1
        return self.ps[i], ("ps", i)

    def rope_tables(self, pos_rep, n, name):
        P = self.P
        C = P.sbuf("C_" + name, [128, n], F32); S = P.sbuf("S_" + name, [128, n], F32)
        if not hasattr(self, "rt_pi"):
            self.rt_pi = P.sbuf("rt_pi", [128, 512], I32); self.rt_pf = P.sbuf("rt_pf", [128, 512], F32); self.rt_tf = P.sbuf("rt_tf", [128, 512], F32)
        pi_, pf, tf = self.rt_pi, self.rt_pf, self.rt_tf
        for b0 in range(0, n, 512):
            w = min(512, n - b0)
            P.dma("sp", pi_[:, :w], pos_rep[:, b0:b0 + w], writes=["rt_pi"])
            P.op("dve", lambda e, w=w: e.tensor_copy(out=pf[:, :w], in_=pi_[:, :w]), reads=["rt_pi"], writes=["rt_pf"])
            for X, k, ph in ((C, "C" + name, 0.5 * math.pi), (S, "S" + name, 0.0)):
                Xs = X[:, b0:b0 + w]
                P.op("dve", lambda e, Xs=Xs, ph=ph, w=w: e.tensor_scalar(out=Xs, in0=pf[:, :w], scalar1=self.inv[:, 0:1], scalar2=ph, op0=ALU.mult, op1=ALU.add),
                     reads=["rt_pf", "inv"], writes=[k])
                P.op("dve", lambda e, Xs=Xs, w=w: e.tensor_scalar(out=pi_[:, :w], in0=Xs, scalar1=1.0 / (2 * math.pi), scalar2=None, op0=ALU.mult),
                     reads=[k], writes=["rt_pi"])
                P.op("dve", lambda e, w=w: e.tensor_copy(out=tf[:, :w], in_=pi_[:, :w]), reads=["rt_pi"], writes=["rt_tf"])
                P.op("dve", lambda e, Xs=Xs, w=w: e.scalar_tensor_tensor(out=Xs, in0=tf[:, :w], scalar=-2 * math.pi, in1=Xs, op0=ALU.mult, op1=ALU.add),
                     reads=["rt_tf", k], writes=[k])
                P.op("act", lambda e, Xs=Xs: e.activation(out=Xs, in_=Xs, func=AF.Sin), reads=[k], writes=[k])
        return C, S

    def prep_block(self, src, skey, ncol, dst, dkey, gcol=None, gkey=None, C=None, S=None, ckey=None, norm=True, src_sbuf=False):
        P = self.P
        if src_sbuf:
            xf, xkey = src, skey
        else:
            s = self.xi % 2; self.xi += 1
            xf, xkey = self.xf[s][:, :ncol], ("pxf", s)
            P.dma("sp", xf, src, reads=[skey], writes=[xkey])
        t1, t2, t3 = self.t1[:, :ncol], self.t2[:, :ncol], self.t3[:, :ncol]
        cur, ckey_cur = xf, xkey
        if norm:
            P.op("act", lambda e: e.activation(out=t1, in_=xf, func=AF.Square), reads=[xkey], writes=["pt1"])
            sp_, sk_ = self.next_ps()
            P.op("pe", lambda e: e.matmul(sp_[:, :ncol], lhsT=self.ones[:], rhs=t1, start=True, stop=True), reads=["pt1", "ones"], writes=[sk_])
            P.op("dve", lambda e: e.tensor_scalar(out=t1, in0=sp_[:, :ncol], scalar1=1.0 / 128, scalar2=EPS, op0=ALU.mult, op1=ALU.add), reads=[sk_], writes=["pt1"])
            P.op("act", lambda e: e.activation(out=t1, in_=t1, func=AF.Sqrt), reads=["pt1"], writes=["pt1"])
            P.op("dve", lambda e: e.reciprocal(out=t1, in_=t1), reads=["pt1"], writes=["pt1"])
            P.op("dve", lambda e: e.scalar_tensor_tensor(out=t2, in0=xf, scalar=gcol[:, 0:1], in1=t1, op0=ALU.mult, op1=ALU.mult),
                 reads=[xkey, "pt1", gkey], writes=["pt2"])
            cur, ckey_cur = t2, "pt2"
        if C is None:
            P.op("dve", lambda e: e.tensor_copy(out=dst, in_=cur), reads=[ckey_cur], writes=[dkey])
            return
        rp_, rk_ = self.next_ps()
        P.op("pe", lambda e: e.matmul(rp_[:, :ncol], lhsT=self.RT[:], rhs=cur, start=True, stop=True), reads=[ckey_cur, "RT"], writes=[rk_])
        P.op("dve", lambda e: e.tensor_tensor(out=t3, in0=rp_[:, :ncol], in1=S, op=ALU.mult), reads=[rk_, ckey[1]], writes=["pt3"])
        P.op("dve", lambda e: e.tensor_tensor(out=t1, in0=cur, in1=C, op=ALU.mult), reads=[ckey_cur, ckey[0]], writes=["pt1"])
        P.op("dve", lambda e: e.tensor_tensor(out=dst, in0=t1, in1=t3, op=ALU.add), reads=["pt1", "pt3"], writes=[dkey])

    def load_col(self, name, vec):
        g = self.P.sbuf("gc_" + name, [128, 1], F32)
        self.P.dma("sp", g[:], vec.rearrange("(p o) -> p o", o=1), writes=["gc_" + name])
        return g, "gc_" + name


def build_dsa_index(NQ=2048, T=4096, TOPK=256):
    nc = new_nc(); P = Prog(nc); sc = SeqCommon(P, nc, T); din = sc.din
    qiT = din("qiT", [32, 128, NQ]); wi = din("wi", [NQ, 32]); posq = din("posq", [128, NQ], I32); posk = din("posk", [128, T], I32)
    kiT = din("kiT", [128, T]); ki_norm = din("ki_norm", [128]); cbias_d = din("cbias", [2, 128, 512])
    biasQ = nc.dram_tensor("biasQ", [NQ, T], F32, kind="ExternalOutput").ap()
    Ck, Sk = sc.rope_tables(posk, T, "k"); Cq, Sq = sc.rope_tables(posq, NQ, "q")
    gki, gkey = sc.load_col("ki", ki_norm)
    kib = P.sbuf("kib", [128, T], BF16)
    for b in range(T // 512):
        sl = slice(b * 512, (b + 1) * 512)
        sc.prep_block(kiT[:, sl], "kiT", 512, kib[:, sl], "kib", gki, gkey, Ck[:, sl], Sk[:, sl], ("Ck", "Sk"))
    cb = P.sbuf("cb", [128, 2, 512], F32)
    P.dma("sp", cb[:], cbias_d.rearrange("a p s -> p a s"), writes=["cb"])
    qib = P.sbuf("qib", [128, 32, 128], BF16)
    wt = P.sbuf("wt", [128, 32], F32); wabs = P.sbuf("wabs", [128, 32], F32); wsgn = P.sbuf("wsgn", [128, 32], F32)
    acc = P.sbuf("acc", [128, T], F32); wk = P.sbuf("wk", [128, T], F32)
    tmp = [P.sbuf(f"itmp{i}", [128, 512], F32) for i in range(3)]
    m8 = P.sbuf("m8", [128, TOPK], F32); thr = P.sbuf("thr", [128, 1], F32)
    ti = 0
    for i in range(NQ // 128):
        qs = slice(i * 128, (i + 1) * 128)
        P.dma("sp", wt[:], wi[qs, :], writes=["wt"])
        P.op("act", lambda e: e.activation(out=wabs[:], in_=wt[:], func=AF.Abs), reads=["wt"], writes=["wabs"])
        P.op("act", lambda e: e.activation(out=wsgn[:], in_=wt[:], func=AF.Sign), reads=["wt"], writes=["wsgn"])
        for h in range(32):
            sc.prep_block(qiT[h, :, qs], "qiT", 128, qib[:, h, :], "qib", None, None, Cq[:, qs], Sq[:, qs], ("Cq", "Sq"), norm=False)
        nkb = i // 2 + 1
        n = nkb * 512
        for kb in range(nkb):
            ks = slice(kb * 512, (kb + 1) * 512)
            for h in range(32):
                ps, pk = sc.next_ps()
                P.op("pe", lambda e, h=h, ps=ps, ks=ks: e.matmul(ps[:], lhsT=qib[:, h, :], rhs=kib[:, ks], start=True, stop=True),
                     reads=["qib", "kib"], writes=[pk])
                tt = ti % 3; ti += 1
                P.op("act", lambda e, h=h, ps=ps, tt=tt: e.activation(out=tmp[tt][:], in_=ps[:], func=AF.Relu, scale=wabs[:, h:h + 1]),
                     reads=[pk, "wabs"], writes=[("itmp", tt)])
                if h == 0:
                    P.op("dve", lambda e, tt=tt, ks=ks: e.tensor_scalar(out=acc[:, ks], in0=tmp[tt][:], scalar1=wsgn[:, 0:1], scalar2=None, op0=ALU.mult),
                         reads=[("itmp", tt), "wsgn"], writes=["acc"])
                else:
                    P.op("dve", lambda e, tt=tt, ks=ks, h=h: e.scalar_tensor_tensor(out=acc[:, ks], in0=tmp[tt][:], scalar=wsgn[:, h:h + 1], in1=acc[:, ks],
                                                                                 op0=ALU.mult, op1=ALU.add),
                         reads=[("itmp", tt), "wsgn", "acc"], writes=["acc"])
        ls = slice(n - 512, n)
        P.op("dve", lambda e, ls=ls, i=i: e.tensor_tensor(out=acc[:, ls], in0=acc[:, ls], in1=cb[:, i % 2, :], op=ALU.add), reads=["acc", "cb"], writes=["acc"])
        P.op("act", lambda e, n=n: e.copy(out=wk[:, :n], in_=acc[:, :n]), reads=["acc"], writes=["wk"])
        for r in range(TOPK // 8):
            P.op("dve", lambda e, r=r, n=n: e.max(out=m8[:, r * 8:(r + 1) * 8], in_=wk[:, :n]), reads=["wk"], writes=["m8"])
            if r < TOPK // 8 - 1:
                P.op("dve", lambda e, r=r, n=n: e.match_replace(out=wk[:, :n], in_to_replace=m8[:, r * 8:(r + 1) * 8], in_values=wk[:, :n], imm_value=-1e30),
                     reads=["wk", "m8"], writes=["wk"])
        P.op("dve", lambda e: e.tensor_reduce(out=thr[:], in_=m8[:, TOPK - 8:TOPK], axis=AX.X, op=ALU.min), reads=["m8"], writes=["thr"])
        P.op("dve", lambda e: e.tensor_scalar(out=thr[:], in0=thr[:], scalar1=-1e29, scalar2=None, op0=ALU.max), reads=["thr"], writes=["thr"])
        P.op("dve", lambda e, n=n: e.tensor_scalar(out=wk[:, :n], in0=acc[:, :n], scalar1=thr[:, 0:1], scalar2=None, op0=ALU.is_ge), reads=["acc", "thr"], writes=["wk"])
        P.op("dve", lambda e, n=n: e.tensor_scalar(out=wk[:, :n], in0=wk[:, :n], scalar1=-1.0, scalar2=-NEG, op0=ALU.add, op1=ALU.mult), reads=["wk"], writes=["wk"])
        if n < T:
            P.op("pool", lambda e, n=n: e.memset(wk[:, n:], NEG), reads=[], writes=["wk"])
        P.dma("sp", biasQ[qs, :], wk[:], reads=["wk"], writes=["biasQ"])
    P.finish()
    return nc, P


def dsa_index_consts(r):
    cb = np.zeros((2, 128, 512), np.float32)
    c = np.arange(512)[None, :]; p = np.arange(128)[:, None]
    for par in range(2):
        cb[par] = np.where(c <= 256 * par + 2 * p + r, 0.0, -1e30)
    return cb


class Attn:
    def __init__(self, sc):
        self.sc = sc; P = sc.P
        self.E = [P.sbuf(f"attE{i}", [128, 512], BF16) for i in range(3)]
        self.ei = 0
        self.Ops = [P.psum(f"attO{i}", [128, 2, 256]) for i in range(2)]
        self.Sps = [P.psum(f"attS{i}", [128, 512]) for i in range(2)]
        self.si = 0
        self.rinv = P.sbuf("att_rinv", [128, 4], F32)

    def run(self, q4, qkey, kts, scale, on_E=None):
        P = self.sc.P
        n = len(kts)
        for i, kt in enumerate(kts):
            ns = kt["ns"]
            si = self.si % 2; self.si += 1
            Sp = self.Sps[si]; skey = ("attS", si)
            extra = kt.get("bias", [])
            P.op("pe", lambda e, Sp=Sp, kt=kt, ns=ns, extra=extra: e.matmul(Sp[:ns, :].rearrange("p (h t) -> p h t", h=4), lhsT=kt["k"], rhs=q4,
                                                                       start=True, stop=(len(extra) == 0)),
                 reads=[kt["kkey"], qkey], writes=[skey])
            for j, (bl, br, bkeys) in enumerate(extra):
                P.op("pe", lambda e, Sp=Sp, bl=bl, br=br, ns=ns, j=j, extra=extra: e.matmul(Sp[:ns, :], lhsT=bl, rhs=br, start=False, stop=(j == len(extra) - 1)),
                     reads=list(bkeys), writes=[skey])
            ei = self.ei % 3; self.ei += 1
            E = self.E[ei]; ekey = ("attE", ei)
            P.op("act", lambda e, E=E, Sp=Sp, ns=ns: e.activation(out=E[:ns, :], in_=Sp[:ns, :], func=AF.Exp, scale=scale), reads=[skey], writes=[ekey])
            for h in range(4):
                P.op("pe", lambda e, E=E, h=h, kt=kt, ns=ns, i=i: e.matmul(self.Ops[h // 2][:, h % 2, 0:129], lhsT=E[:ns, h * 128:(h + 1) * 128], rhs=kt["v"],
                                                                          start=(i == 0 and h % 2 == 0), stop=(i == n - 1), skip_group_check=True),
                     reads=[ekey, kt["vkey"]], writes=[("attO", h // 2)])
            if on_E is not None:
                on_E(i, E, ekey, ns)

    def finish(self, h):
        P = self.sc.P
        O = self.Ops[h // 2]
        P.op("dve", lambda e: e.tensor_scalar(out=self.rinv[:, h:h + 1], in0=O[:, h % 2, 128:129], scalar1=1e-30, scalar2=None, op0=ALU.add),
             reads=[("attO", h // 2)], writes=["att_rinv"])
        P.op("dve", lambda e: e.reciprocal(out=self.rinv[:, h:h + 1], in_=self.rinv[:, h:h + 1]), reads=["att_rinv"], writes=["att_rinv"])
        return O[:, h % 2, 0:128], self.rinv[:, h:h + 1], ("attO", h // 2)


def load_v_aug(P, v_dram, vkey, T, name):
    va = P.sbuf("vaug_" + name, [128, T // 128, 129], BF16)
    P.op("pool", lambda e: e.memset(va[:, :, 128:129], 1.0), writes=["vaug_" + name])
    P.dma("pool", va[:, :, 0:128], v_dram.rearrange("(n p) d -> p n d", p=128), reads=[vkey], writes=["vaug_" + name])
    return va


def build_dsa_attn(T=4096):
    nc = new_nc(); P = Prog(nc); sc = SeqCommon(P, nc, T); din = sc.din
    qT = din("qT", [16, 128, T]); kT = din("kT", [2, 128, T]); v = din("v", [2, T, 128]); biasT = din("biasT", [T, T])
    pos = din("pos", [128, T], I32); q_norm = din("q_norm", [128]); k_norm = din("k_norm", [128])
    o = nc.dram_tensor("o", [T, 2048], F32, kind="ExternalOutput").ap()
    C, S = sc.rope_tables(pos, T, "k")
    gq, gqk = sc.load_col("q", q_norm); gk, gkk = sc.load_col("k", k_norm)
    at = Attn(sc)
    kb = P.sbuf("kb", [128, T], BF16); qb = P.sbuf("qb", [128, 4, T], BF16)
    NT = T // 128
    bst = [P.sbuf(f"bst{i}", [128, NT, 128], F32) for i in range(2)]
    b16 = [P.sbuf(f"b16_{i}", [128, NT, 128], BF16) for i in range(2)]
    osb = [P.sbuf(f"osb{i}", [128, 4, 128], F32) for i in range(2)]
    scale = HD ** -0.5
    bi = 0; oi = 0
    for kh in range(2):
        for b in range(T // 512):
            sl = slice(b * 512, (b + 1) * 512)
            sc.prep_block(kT[kh, :, sl], "kT", 512, kb[:, sl], "kb", gk, gkk, C[:, sl], S[:, sl], ("Ck", "Sk"))
        va = load_v_aug(P, v[kh], "v", T, f"{kh}")
        for g in range(2):
            for j in range(4):
                h = kh * 8 + g * 4 + j
                for b in range(T // 512):
                    sl = slice(b * 512, (b + 1) * 512)
                    sc.prep_block(qT[h, :, sl], "qT", 512, qb[:, j, sl], "qb", gq, gqk, C[:, sl], S[:, sl], ("Ck", "Sk"))
            for qi in range(NT):
                qs = slice(qi * 128, (qi + 1) * 128)
                s = bi % 2; bi += 1
                nk = qi + 1
                P.dma("sp", bst[s][:, :nk, :], biasT[0:nk * 128, qs].rearrange("(n p) t -> p n t", p=128), writes=[("bst", s)])
                P.op("dve", lambda e, s=s, nk=nk: e.tensor_copy(out=b16[s][:, :nk, :], in_=bst[s][:, :nk, :]), reads=[("bst", s)], writes=[("b16", s)])
                for kt in range(nk):
                    ks = slice(kt * 128, (kt + 1) * 128)
                    bias = [(sc.identb[:], b16[s][:, kt, :], ("identb", ("b16", s)), j) for j in range(4)]
                    at_run_tile(at, qb[:, :, qs], "qb", dict(k=kb[:, ks], kkey="kb", v=va[:, kt, :], vkey=f"vaug_{kh}", ns=128, bias=bias), scale,
                                first=(kt == 0), last=(kt == qi))
                so = oi % 2; oi += 1
                for j in range(4):
                    O, rinv, okey = at.finish(j)
                    P.op("dve", lambda e, O=O, rinv=rinv, j=j, so=so: e.tensor_scalar(out=osb[so][:, j, :], in0=O, scalar1=rinv, scalar2=None, op0=ALU.mult),
                         reads=[okey, "att_rinv"], writes=[("osb", so)])
                c0 = (kh * 8 + g * 4) * 128
                P.dma("sp", o[qs, c0:c0 + 512], osb[so][:].rearrange("p h d -> p (h d)"), reads=[("osb", so)], writes=["o"])
    P.finish()
    return nc, P


def at_run_tile(at, q4, qkey, kt, scale, first, last, on_E=None):
    P = at.sc.P
    ns = kt["ns"]
    si = at.si % 2; at.si += 1
    Sp = at.Sps[si]; skey = ("attS", si)
    extra = kt.get("bias", [])
    P.op("pe", lambda e: e.matmul(Sp[:ns, :].rearrange("p (h t) -> p h t", h=4), lhsT=kt["k"], rhs=q4, start=True, stop=(len(extra) == 0)),
         reads=[kt["kkey"], qkey], writes=[skey])
    for j, ex in enumerate(extra):
        bl, br, bkeys = ex[0], ex[1], ex[2]
        dst = Sp[:ns, :] if len(ex) == 3 else Sp[:ns, ex[3] * 128:(ex[3] + 1) * 128]
        P.op("pe", lambda e, bl=bl, br=br, j=j, dst=dst: e.matmul(dst, lhsT=bl, rhs=br, start=False, stop=(j == len(extra) - 1)),
             reads=list(bkeys), writes=[skey])
    ei = at.ei % 3; at.ei += 1
    E = at.E[ei]; ekey = ("attE", ei)
    P.op("act", lambda e: e.activation(out=E[:ns, :], in_=Sp[:ns, :], func=AF.Exp, scale=scale), reads=[skey], writes=[ekey])
    for h in range(4):
        P.op("pe", lambda e, h=h: e.matmul(at.Ops[h // 2][:, h % 2, 0:129], lhsT=E[:ns, h * 128:(h + 1) * 128], rhs=kt["v"], start=(first and h % 2 == 0), stop=last, skip_group_check=True),
             reads=[ekey, kt["vkey"]], writes=[("attO", h // 2)])
    if on_E is not None:
        on_E(E, ekey, ns)


def nsa_consts(T=4096):
    n_cmp = T // 16 - 1; n_sel = T // 64; NT = T // 128
    cs = 16 * np.arange(n_cmp)[:, None]; ss = 64 * np.arange(n_sel)[None, :]
    ov = np.clip(np.minimum(cs + 32, ss + 64) - np.maximum(cs, ss), 0, None) / 32.0
    msel = np.zeros((256, 64), np.float32); msel[:n_cmp] = ov
    c = np.arange(256)[:, None]; t = np.arange(T)[None, :]
    cmpb = np.where((16 * c + 31 <= t) & (c < n_cmp), 0.0, NEG).astype(np.float32)
    keep = np.ones((NT, 128, 64), np.float32); addc = np.zeros((NT, 128, 64), np.float32)
    tt = np.arange(T).reshape(NT, 128, 1); jt = tt // 64; jj = np.arange(64)[None, None, :]
    for cond, val in (((jj == 0), 1.0e4), ((jj == jt), 1.1e4), ((jj == jt - 1), 1.2e4), ((jj > jt), -1e30)):
        cond = np.broadcast_to(cond, keep.shape)
        keep = np.where(cond, 0.0, keep); addc = np.where(cond, val, addc)
    esel = np.zeros((NT, 64, 128), np.float32)
    for kt in range(NT):
        esel[kt, 2 * kt, :64] = 1.0; esel[kt, 2 * kt + 1, 64:] = 1.0
    s = np.arange(128)[:, None]; tl = np.arange(128)[None, :]
    tri = np.where(s <= tl, 0.0, NEG).astype(np.float32); atri = np.where(s > tl, 0.0, NEG).astype(np.float32)
    return dict(msel=msel.reshape(2, 128, 64), cmpb=cmpb.reshape(2, 128, T), keep=keep.astype(np.float32), addc=addc.astype(np.float32), esel=esel,
                tri4=np.tile(tri, (1, 4)), atri4=np.tile(atri, (1, 4)))


NSA_BR = (0, 1, 2)


def emit_nsa(P, nc, sc, at, T, C, S):
    din = sc.din
    NT = T // 128; NC = T // 16 - 1
    qT = din("n_qT", [8, 128, T]); kcT = din("n_kcT", [2, 128, T]); vcT = din("n_vcT", [2, 128, T])
    ksT = din("n_ksT", [2, 128, T]); kwT = din("n_kwT", [2, 128, T]); vs = din("n_vs", [2, T, 128]); vw = din("n_vw", [2, T, 128])
    gates = din("n_gates", [T, 24]); cpos = din("n_cpos", [128, 256], I32)
    norms = {k: sc.load_col("n" + k, din("n_" + k + "_norm", [128])) for k in ("q", "kc", "ks", "kw")}
    peT = {k: din("n_peT_" + k, [128, 32]) for k in "kv"}; w1 = {k: din("n_w1_" + k, [4096, 128]) for k in "kv"}; w2 = {k: din("n_w2_" + k, [128, 128]) for k in "kv"}
    msel_d = din("n_msel", [2, 128, 64]); cmpb_d = din("n_cmpb", [2, 128, T]); keep_d = din("n_keep", [NT, 128, 64]); addc_d = din("n_addc", [NT, 128, 64])
    esel_d = din("n_esel", [NT, 64, 128]); tri4_d = din("n_tri4", [128, 512]); atri4_d = din("n_atri4", [128, 512])
    o_a = nc.dram_tensor("o_a", [T, 1024], F32, kind="ExternalOutput").ap()
    Cc, Sc = sc.rope_tables(cpos, 256, "c")
    mselb = P.sbuf("mselb", [128, 2, 64], BF16); P.dma("pool", mselb[:], msel_d.rearrange("a p j -> p a j"), writes=["mselb"])
    keep = P.sbuf("keep", [128, NT, 64], F32); P.dma("sp", keep[:], keep_d.rearrange("n p j -> p n j"), writes=["keep"])
    addc = P.sbuf("addc", [128, NT, 64], F32); P.dma("sp", addc[:], addc_d.rearrange("n p j -> p n j"), writes=["addc"])
    eselb = P.sbuf("eselb", [64, NT, 128], BF16); P.dma("pool", eselb[:], esel_d.rearrange("n j s -> j n s"), writes=["eselb"])
    tri4 = P.sbuf("tri4", [128, 512], BF16); P.dma("pool", tri4[:], tri4_d, writes=["tri4"])
    atri4 = P.sbuf("atri4", [128, 512], BF16); P.dma("pool", atri4[:], atri4_d, writes=["atri4"])
    w1b = P.sbuf("w1b", [128, 32, 128], BF16); w2b = P.sbuf("w2b", [128, 128], BF16); peb = P.sbuf("peb", [128, 32], BF16)
    pecol = P.sbuf("pecol", [128, 1], F32)
    xcb = P.sbuf("xcb", [128, T], BF16)
    gx = P.sbuf("gx", [128, 256], F32); gt = P.sbuf("gt", [128, 256], F32); gT = P.sbuf("gT", [128, 256], BF16)
    kc2 = P.sbuf("kc2", [128, 256], F32)
    kcc = P.sbuf("kcc", [128, 256], BF16); vca = P.sbuf("vca", [128, 2, 129], BF16)
    ksb = P.sbuf("ksb", [128, T], BF16); kwb = P.sbuf("kwb", [128, T], BF16); qb = P.sbuf("qb", [128, 4, T], BF16)
    cb4 = [P.sbuf(f"cb4_{i}", [128, 4, 128], BF16) for i in range(2)]
    gsb = P.sbuf("gsb", [128, 24], F32); fac = P.sbuf("fac", [128, 4], F32)
    imps = P.psum("imps", [128, 4, 64])
    imp = P.sbuf("imp", [128, 64], F32); wk = P.sbuf("nwk", [128, 64], F32); m8 = P.sbuf("nm8", [128, 16], F32); thr = P.sbuf("nthr", [128, 1], F32)
    selbT4 = P.sbuf("selbT4", [64, 4, 128], BF16)
    oacc = [P.sbuf(f"oacc{i}", [128, 4, 128], F32) for i in range(2)]
    scale = HD ** -0.5
    GC = 2 * math.sqrt(2 / math.pi)
    oi = 0; ci = 0
    for kh in range(2):
        for kind, srcT in (("k", kcT), ("v", vcT)):
            P.dma("pool", w1b[:], w1[kind].rearrange("(p d) m -> d p m", d=128), writes=["w1b"])
            P.dma("pool", w2b[:], w2[kind], writes=["w2b"])
            P.dma("pool", peb[:], peT[kind], writes=["peb"])
            P.dma("pool", xcb[:], srcT[kh], writes=["xcb"])
            hp, hk = sc.next_ps(); bp, bk = sc.next_ps()
            for p in range(32):
                P.op("pe", lambda e, p=p, hp=hp: e.matmul(hp[:, :NC], lhsT=w1b[:, p, :], rhs=xcb[:, p:p + 16 * (NC - 1) + 1:16], start=(p == 0), stop=(p == 31)),
                     reads=["w1b", "xcb"], writes=[hk])
            for p in range(32):
                P.op("pe", lambda e, p=p, bp=bp: e.matmul(bp[:, 0:1], lhsT=w1b[:, p, :], rhs=peb[:, p:p + 1], start=(p == 0), stop=(p == 31)),
                     reads=["w1b", "peb"], writes=[bk])
            P.op("dve", lambda e, bp=bp: e.tensor_copy(out=pecol[:], in_=bp[:, 0:1]), reads=[bk], writes=["pecol"])
            P.op("dve", lambda e: e.memset(gT[:], 0.0), writes=["gT"])
            P.op("dve", lambda e, hp=hp: e.tensor_scalar(out=gx[:, :NC], in0=hp[:, :NC], scalar1=pecol[:, 0:1], scalar2=None, op0=ALU.add), reads=[hk, "pecol"], writes=["gx"])
            P.op("dve", lambda e: e.tensor_tensor(out=gt[:, :NC], in0=gx[:, :NC], in1=gx[:, :NC], op=ALU.mult), reads=["gx"], writes=["gt"])
            P.op("dve", lambda e: e.tensor_scalar(out=gt[:, :NC], in0=gt[:, :NC], scalar1=0.044715, scalar2=1.0, op0=ALU.mult, op1=ALU.add), reads=["gt"], writes=["gt"])
            P.op("dve", lambda e: e.tensor_tensor(out=gt[:, :NC], in0=gt[:, :NC], in1=gx[:, :NC], op=ALU.mult), reads=["gt", "gx"], writes=["gt"])
            P.op("act", lambda e: e.activation(out=gt[:, :NC], in_=gt[:, :NC], func=AF.Sigmoid, scale=GC), reads=["gt"], writes=["gt"])
            P.op("dve", lambda e: e.tensor_tensor(out=gT[:, :NC], in0=gt[:, :NC], in1=gx[:, :NC], op=ALU.mult), reads=["gt", "gx"], writes=["gT"])
            if kind == "k":
                op_, ok_ = sc.next_ps()
                P.op("pe", lambda e, op_=op_: e.matmul(op_[:, :256], lhsT=w2b[:], rhs=gT[:], start=True, stop=True), reads=["w2b", "gT"], writes=[ok_])
                P.op("dve", lambda e, op_=op_: e.tensor_copy(out=kc2[:], in_=op_[:, :256]), reads=[ok_], writes=["kc2"])
                g_, gk_ = norms["kc"]
                sc.prep_block(kc2[:], "kc2", 256, kcc[:], "kcc", g_, gk_, Cc[:], Sc[:], ("Cc", "Sc"), src_sbuf=True)
            else:
                P.op("pool", lambda e: e.memset(vca[:, :, 128:129], 1.0), writes=["vca"])
                for ct in range(2):
                    op_, ok_ = sc.next_ps()
                    P.op("pe", lambda e, ct=ct, op_=op_: e.matmul(op_[:, :128], lhsT=gT[:, ct * 128:(ct + 1) * 128], rhs=w2b[:], start=True, stop=True),
                         reads=["w2b", "gT"], writes=[ok_])
                    P.op("dve", lambda e, ct=ct, op_=op_: e.tensor_copy(out=vca[:, ct, 0:128], in_=op_[:, :128]), reads=[ok_], writes=["vca"])
        for b in range(T // 512):
            sl = slice(b * 512, (b + 1) * 512)
            sc.prep_block(ksT[kh, :, sl], "ksT", 512, ksb[:, sl], "ksb", *norms["ks"], C[:, sl], S[:, sl], ("Ck", "Sk"))
            sc.prep_block(kwT[kh, :, sl], "kwT", 512, kwb[:, sl], "kwb", *norms["kw"], C[:, sl], S[:, sl], ("Ck", "Sk"))
        vsa = load_v_aug(P, vs[kh], "vs", T, f"s{kh}"); vwa = load_v_aug(P, vw[kh], "vw", T, f"w{kh}")
        for j in range(4):
            for b in range(T // 512):
                sl = slice(b * 512, (b + 1) * 512)
                sc.prep_block(qT[kh * 4 + j, :, sl], "qT", 512, qb[:, j, sl], "qb", *norms["q"], C[:, sl], S[:, sl], ("Ck", "Sk"))
        for qi in range(NT):
            qs = slice(qi * 128, (qi + 1) * 128)
            q4 = qb[:, :, qs]
            so = oi % 2; oi += 1
            P.dma("sp", gsb[:, 0:12], gates[qs, kh * 12:(kh + 1) * 12], writes=["gsb"])
            P.op("act", lambda e: e.activation(out=gsb[:, 0:12], in_=gsb[:, 0:12], func=AF.Sigmoid), reads=["gsb"], writes=["gsb"])

            def combine(br, first, so=so):
                if br not in NSA_BR:
                    return
                first = (br == min(NSA_BR))
                for g in range(4):
                    O, rinv, okey = at.finish(g)
                    P.op("dve", lambda e, g=g, rinv=rinv: e.tensor_tensor(out=fac[:, g:g + 1], in0=rinv, in1=gsb[:, 3 * g + br:3 * g + br + 1], op=ALU.mult),
                         reads=["att_rinv", "gsb"], writes=["fac"])
                    if first:
                        P.op("dve", lambda e, g=g, O=O: e.tensor_scalar(out=oacc[so][:, g, :], in0=O, scalar1=fac[:, g:g + 1], scalar2=None, op0=ALU.mult),
                             reads=[okey, "fac"], writes=[("oacc", so)])
                    else:
                        P.op("dve", lambda e, g=g, O=O: e.scalar_tensor_tensor(out=oacc[so][:, g, :], in0=O, scalar=fac[:, g:g + 1], in1=oacc[so][:, g, :],
                                                                             op0=ALU.mult, op1=ALU.add),
                             reads=[okey, "fac", ("oacc", so)], writes=[("oacc", so)])

            nct = 1 if 8 * qi + 7 <= 128 else 2
            for ct in range(nct):
                s = ci % 2; ci += 1
                P.dma("pool", cb4[s][:, 0, :], cmpb_d[ct, :, qs], writes=[("cb4", s)])
                P.op("pool", lambda e, s=s: e.tensor_copy(out=cb4[s][:, 1, :], in_=cb4[s][:, 0, :]), reads=[("cb4", s)], writes=[("cb4", s)])
                P.op("pool", lambda e, s=s: e.tensor_copy(out=cb4[s][:, 2:4, :], in_=cb4[s][:, 0:2, :]), reads=[("cb4", s)], writes=[("cb4", s)])
                def on_E(E, ekey, ns, ct=ct):
                    for g in range(4):
                        P.op("pe", lambda e, g=g: e.matmul(imps[:, g, :], lhsT=E[:, g * 128:(g + 1) * 128], rhs=mselb[:, ct, :],
                                                           start=(ct == 0 and g == 0), stop=True, skip_group_check=True),
                             reads=[ekey, "mselb"], writes=["imps"])
                at_run_tile(at, q4, "qb", dict(k=kcc[:, ct * 128:(ct + 1) * 128], kkey="kcc", v=vca[:, ct, :], vkey="vca", ns=128,
                                               bias=[(sc.identb[:], cb4[s][:].rearrange("p h t -> p (h t)"), ("identb", ("cb4", s)))]),
                            scale, first=(ct == 0), last=(ct == nct - 1), on_E=on_E)
            combine(0, True)
            for g in range(4):
                if g == 0:
                    P.op("dve", lambda e: e.tensor_scalar(out=imp[:], in0=imps[:, 0, :], scalar1=at.rinv[:, 0:1], scalar2=None, op0=ALU.mult),
                         reads=["imps", "att_rinv"], writes=["imp"])
                else:
                    P.op("dve", lambda e, g=g: e.scalar_tensor_tensor(out=imp[:], in0=imps[:, g, :], scalar=at.rinv[:, g:g + 1], in1=imp[:], op0=ALU.mult, op1=ALU.add),
                         reads=["imps", "att_rinv", "imp"], writes=["imp"])
            P.op("dve", lambda e, qi=qi: e.tensor_tensor(out=imp[:], in0=imp[:], in1=keep[:, qi, :], op=ALU.mult), reads=["imp", "keep"], writes=["imp"])
            P.op("dve", lambda e, qi=qi: e.tensor_tensor(out=imp[:], in0=imp[:], in1=addc[:, qi, :], op=ALU.add), reads=["imp", "addc"], writes=["imp"])
            P.op("dve", lambda e: e.max(out=m8[:, 0:8], in_=imp[:]), reads=["imp"], writes=["nm8"])
            P.op("dve", lambda e: e.match_replace(out=wk[:], in_to_replace=m8[:, 0:8], in_values=imp[:], imm_value=-1e30), reads=["imp", "nm8"], writes=["nwk"])
            P.op("dve", lambda e: e.max(out=m8[:, 8:16], in_=wk[:]), reads=["nwk"], writes=["nm8"])
            P.op("dve", lambda e: e.tensor_reduce(out=thr[:], in_=m8[:, 8:16], axis=AX.X, op=ALU.min), reads=["nm8"], writes=["nthr"])
            P.op("dve", lambda e: e.tensor_scalar(out=wk[:], in0=imp[:], scalar1=thr[:, 0:1], scalar2=None, op0=ALU.is_ge), reads=["imp", "nthr"], writes=["nwk"])
            P.op("dve", lambda e: e.tensor_scalar(out=wk[:], in0=wk[:], scalar1=-1.0, scalar2=-NEG, op0=ALU.add, op1=ALU.mult), reads=["nwk"], writes=["nwk"])
            tp, tk = sc.next_ps()
            P.op("pe", lambda e, tp=tp: e.transpose(out=tp[:64, :128], in_=wk[:], identity=sc.ident[:]), reads=["nwk", "ident"], writes=[tk])
            P.op("dve", lambda e, tp=tp: e.tensor_copy(out=selbT4[:, 0, :], in_=tp[:64, :128]), reads=[tk], writes=["selbT4"])
            P.op("dve", lambda e: e.tensor_copy(out=selbT4[:, 1, :], in_=selbT4[:, 0, :]), reads=["selbT4"], writes=["selbT4"])
            P.op("dve", lambda e: e.tensor_copy(out=selbT4[:, 2:4, :], in_=selbT4[:, 0:2, :]), reads=["selbT4"], writes=["selbT4"])
            for kt in range(qi + 1):
                ks_ = slice(kt * 128, (kt + 1) * 128)
                bias = [(eselb[:, kt, :], selbT4[:].rearrange("p h t -> p (h t)"), ("eselb", "selbT4"))]
                if kt == qi:
                    bias.append((sc.identb[:], tri4[:], ("identb", "tri4")))
                at_run_tile(at, q4, "qb", dict(k=ksb[:, ks_], kkey="ksb", v=vsa[:, kt, :], vkey=f"vaug_s{kh}", ns=128, bias=bias), scale,
                            first=(kt == 0), last=(kt == qi))
            combine(1, False)
            k0 = max(0, qi - 4)
            for kt in range(k0, qi + 1):
                ks_ = slice(kt * 128, (kt + 1) * 128)
                bias = []
                if kt == qi:
                    bias.append((sc.identb[:], tri4[:], ("identb", "tri4")))
                if kt == qi - 4:
                    bias.append((sc.identb[:], atri4[:], ("identb", "atri4")))
                at_run_tile(at, q4, "qb", dict(k=kwb[:, ks_], kkey="kwb", v=vwa[:, kt, :], vkey=f"vaug_w{kh}", ns=128, bias=bias), scale,
                            first=(kt == k0), last=(kt == qi))
            combine(2, False)
            P.dma("sp", o_a[qs, kh * 512:(kh + 1) * 512], oacc[so][:].rearrange("p h d -> p (h d)"), reads=[("oacc", so)], writes=["o_a"])


GN_EPS = 64e-5
C_DEC = -math.exp(-0.5)


def rwkv_consts():
    s = np.arange(64)[:, None]; t = np.arange(64)[None, :]
    msu = (s < t).astype(np.float32); msl = (t < s).astype(np.float32); mui = (s <= t).astype(np.float32); idn = np.eye(64, dtype=np.float32)
    return np.stack([np.tile(m, (1, 16)) for m in (msu, msl, mui, idn)]).astype(np.float32)


def rwkv_inputs(inp, pb, r, T=SEQ):
    fs = slice(r * 1024, (r + 1) * 1024)
    mu = inp["rwkv_mu"]
    rkv = np.stack([np.ascontiguousarray(pb[:, i * 2048:(i + 1) * 2048][:, fs]) for i in range(3)])
    mu_rkv = np.stack([_rep(mu[i * 2048:(i + 1) * 2048][fs], 64) for i in range(3)])
    mu_l = np.zeros((128, 4), np.float32)
    mu_l[:96, 0] = mu[6144:6240]; mu_l[:96, 1] = mu[6240:6336]; mu_l[:, 2] = mu[6336:6464]; mu_l[:, 3] = mu[6464:6592]
    rep = np.stack([_rep(inp[k].reshape(-1)[fs], 64) for k in ("rwkv_w0", "rwkv_a0", "rwkv_kk", "rwkv_ka", "rwkv_gn_g", "rwkv_gn_b", "rwkv_rk")])
    d = {"rw_rkv": rkv, "rw_wdT": np.ascontiguousarray(pb[:, 6144:6240].T), "rw_adT": np.ascontiguousarray(pb[:, 6240:6336].T),
         "rw_gdT": np.ascontiguousarray(pb[:, 6336:6592].T), "rw_mu_rkv": mu_rkv, "rw_mu_l": mu_l, "rw_rep": rep,
         "rw_w2": np.ascontiguousarray(inp["rwkv_w2"][:, fs]), "rw_a2": np.ascontiguousarray(inp["rwkv_a2"][:, fs]),
         "rw_g2": np.ascontiguousarray(inp["rwkv_g2"][:, fs]), "rw_cst": rwkv_consts()}
    d.update(_consts())
    return d


class WT:
    def __init__(self, P, name, shape=(64, 1024)):
        self.t = P.sbuf(name, list(shape), F32); self.key = name
        self.ap = self.t[:]
        self.v3 = self.t[:].rearrange("p (h j) -> p h j", h=16)


def emit_rwkv(P, nc, sc, T):
    din = sc.din
    NCH = T // 64
    rkv = din("rw_rkv", [3, T, 1024]); wdT = din("rw_wdT", [96, T]); adT = din("rw_adT", [96, T]); gdT = din("rw_gdT", [256, T])
    mu_rkv_d = din("rw_mu_rkv", [3, 64, 1024]); mu_l_d = din("rw_mu_l", [128, 4]); rep_d = din("rw_rep", [7, 64, 1024])
    w2d = din("rw_w2", [96, 1024]); a2d = din("rw_a2", [96, 1024]); g2d = din("rw_g2", [256, 1024]); cst_d = din("rw_cst", [4, 64, 1024])
    o_b = nc.dram_tensor("o_b", [T, 1024], F32, kind="ExternalOutput").ap()

    def persistent(name, src):
        w = WT(P, name)
        P.dma("sp", w.ap, src, writes=[w.key])
        return w
    MU = [persistent(f"rw_mu{i}", mu_rkv_d[i]) for i in range(3)]
    W0, A0, KKG, KA_, GNG, GNB, RK = [persistent(f"rw_rep{i}", rep_d[i]) for i in range(7)]
    MSU, MSL, MUI, IDN = [persistent(f"rw_c{i}", cst_d[i]) for i in range(4)]
    mul = P.sbuf("rw_mul", [128, 4], F32); P.dma("sp", mul[:], mu_l_d, writes=["rw_mul"])
    w2sb = P.sbuf("rw_w2sb", [128, 1024], F32); P.dma("sp", w2sb[:96, :], w2d, writes=["rw_w2sb"])
    a2sb = P.sbuf("rw_a2sb", [128, 1024], F32); P.dma("sp", a2sb[:96, :], a2d, writes=["rw_a2sb"])
    g2sb = P.sbuf("rw_g2sb", [128, 2, 1024], F32); P.dma("sp", g2sb[:], g2d.rearrange("(c p) f -> p c f", p=128), writes=["rw_g2sb"])
    ST = WT(P, "rw_ST")
    P.op("dve", lambda e: e.memset(ST.ap, 0.0), writes=[ST.key])
    lraw = P.sbuf("rw_lraw", [128, 4, 65], F32); lxs = P.sbuf("rw_lxs", [128, 4, 64], F32); ld = P.sbuf("rw_ld", [128, 4, 64], F32)
    st16 = [P.sbuf(f"rw_st{i}", [64, 16], F32) for i in range(4)]
    pool = [WT(P, f"rwt{i}") for i in range(25)]
    PB = [P.psum(f"rw_pb{i}", [64, 1024]) for i in range(3)]
    pbi = [0]

    def get():
        return pool.pop(0)

    def put(*ws):
        for w in ws:
            pool.append(w)

    def nextpb():
        i = pbi[0] % 3; pbi[0] += 1
        return PB[i], ("rw_pb", i)

    def tt(o, a, b, op, eng="dve", b_ap=None, bkey=None):
        bap = b.ap if b_ap is None else b_ap
        P.op(eng, lambda e: e.tensor_tensor(out=o.ap, in0=a.ap, in1=bap, op=op), reads=[a.key, bkey or b.key], writes=[o.key])

    def tt_ps(o, ps, pk, b, op):
        P.op("dve", lambda e: e.tensor_tensor(out=o.ap, in0=ps[:], in1=b.ap, op=op), reads=[pk, b.key], writes=[o.key])

    def cp_ps(o, ps, pk, scale=1.0, eng="act"):
        if eng == "act":
            P.op("act", lambda e: e.activation(out=o.ap, in_=ps[:], func=AF.Copy, scale=scale), reads=[pk], writes=[o.key])
        else:
            P.op("dve", lambda e: e.tensor_scalar(out=o.ap, in0=ps[:], scalar1=scale, scalar2=None, op0=ALU.mult), reads=[pk], writes=[o.key])

    def actf(o, a_ap, akey, func, scale=1.0):
        P.op("act", lambda e: e.activation(out=o.ap, in_=a_ap, func=func, scale=scale), reads=[akey], writes=[o.key])

    def bc(ap16):
        return ap16.unsqueeze(2).broadcast_to([64, 16, 64])

    def tt_bc(o, a, s16, skey, op):
        P.op("dve", lambda e: e.tensor_tensor(out=o.v3, in0=a.v3, in1=bc(s16), op=op), reads=[a.key, skey], writes=[o.key])

    def rsum(s16, skey, a):
        P.op("dve", lambda e: e.tensor_reduce(out=s16, in_=a.v3, axis=AX.X, op=ALU.add), reads=[a.key], writes=[skey])

    def mm16(terms):
        ps, pk = nextpb()
        pv = ps[:].rearrange("p (h j) -> p h j", h=16)
        n = len(terms)
        for h in range(16):
            for ti, (L, R) in enumerate(terms):
                P.op("pe", lambda e, h=h, L=L, R=R, ti=ti: e.matmul(pv[:, h, :], lhsT=L.v3[:, h, :], rhs=R.v3[:, h, :], start=(ti == 0), stop=(ti == n - 1),
                                                                    skip_group_check=True),
                     reads=[L.key, R.key], writes=[pk])
        return ps, pk

    def tr16(o, a):
        ps, pk = nextpb()
        pv = ps[:].rearrange("p (h j) -> p h j", h=16)
        for h in range(16):
            P.op("pe", lambda e, h=h: e.transpose(out=pv[:, h, :], in_=a.v3[:, h, :], identity=sc.ident[:64, :64]), reads=[a.key, "ident"], writes=[pk])
        cp_ps(o, ps, pk)

    lsrc = [(wdT, 0, 96), (adT, 0, 96), (gdT, 0, 128), (gdT, 128, 128)]
    for c in range(NCH):
        t0 = c * 64
        xs = []
        for i in range(3):
            cur = get(); prv = get()
            P.dma("sp", cur.ap, rkv[i, t0:t0 + 64, :], writes=[cur.key])
            if c == 0:
                P.op("pool", lambda e, prv=prv: e.memset(prv.t[0:1, :], 0.0), writes=[prv.key])
                P.dma("sp", prv.t[1:64, :], rkv[i, 0:63, :], writes=[prv.key])
            else:
                P.dma("sp", prv.ap, rkv[i, t0 - 1:t0 + 63, :], writes=[prv.key])
            tt(prv, prv, cur, ALU.subtract, eng="pool"); tt(prv, prv, MU[i], ALU.mult, eng="pool"); tt(cur, cur, prv, ALU.add, eng="pool")
            put(prv); xs.append(cur)
        R, K, V = xs
        for sidx, (src, p0, npart) in enumerate(lsrc):
            if c == 0:
                P.op("pool", lambda e, sidx=sidx: e.memset(lraw[:, sidx, 0:1], 0.0), writes=["rw_lraw"])
                P.dma("sp", lraw[:npart, sidx, 1:65], src[p0:p0 + npart, 0:64], writes=["rw_lraw"])
            else:
                P.dma("sp", lraw[:npart, sidx, :], src[p0:p0 + npart, t0 - 1:t0 + 64], writes=["rw_lraw"])
            P.op("dve", lambda e, sidx=sidx, npart=npart: e.tensor_tensor(out=ld[:npart, sidx, :], in0=lraw[:npart, sidx, 0:64], in1=lraw[:npart, sidx, 1:65], op=ALU.subtract),
                 reads=["rw_lraw"], writes=["rw_ld"])
            P.op("dve", lambda e, sidx=sidx, npart=npart: e.scalar_tensor_tensor(out=lxs[:npart, sidx, :], in0=ld[:npart, sidx, :], scalar=mul[:npart, sidx:sidx + 1],
                                                                                 in1=lraw[:npart, sidx, 1:65], op0=ALU.mult, op1=ALU.add),
                 reads=["rw_ld", "rw_lraw", "rw_mul"], writes=["rw_lxs"])
            if sidx != 1:
                P.op("act", lambda e, sidx=sidx, npart=npart: e.activation(out=lxs[:npart, sidx, :], in_=lxs[:npart, sidx, :], func=(AF.Tanh if sidx == 0 else AF.Sigmoid)),
                     reads=["rw_lxs"], writes=["rw_lxs"])

        def lora(parts, wsb_aps, wkey):
            ps, pk = nextpb()
            n = len(parts)
            for blk in range(2):
                for i, ((sidx, npart), wap) in enumerate(zip(parts, wsb_aps)):
                    P.op("pe", lambda e, blk=blk, sidx=sidx, npart=npart, wap=wap, i=i: e.matmul(ps[:, blk * 512:(blk + 1) * 512], lhsT=lxs[:npart, sidx, :],
                                                                                              rhs=wap[:npart, blk * 512:(blk + 1) * 512], start=(i == 0), stop=(i == n - 1)),
                         reads=["rw_lxs", wkey], writes=[pk])
            return ps, pk
        SW = get(); AA = get(); GG = get()
        ps, pk = lora([(0, 96)], [w2sb[:, :]], "rw_w2sb"); tt_ps(SW, ps, pk, W0, ALU.add); actf(SW, SW.ap, SW.key, AF.Sigmoid)
        ps, pk = lora([(1, 96)], [a2sb[:, :]], "rw_a2sb"); tt_ps(AA, ps, pk, A0, ALU.add); actf(AA, AA.ap, AA.key, AF.Sigmoid)
        ps, pk = lora([(2, 128), (3, 128)], [g2sb[:, 0, :], g2sb[:, 1, :]], "rw_g2sb"); cp_ps(GG, ps, pk)
        KK = get(); TMP = get(); KP = get(); BB = get()
        tt(KK, K, KKG, ALU.mult); tt(TMP, KK, KK, ALU.mult)
        rsum(st16[0][:], "rw_st0", TMP)
        P.op("act", lambda e: e.activation(out=st16[0][:], in_=st16[0][:], func=AF.Sqrt), reads=["rw_st0"], writes=["rw_st0"])
        P.op("dve", lambda e: e.tensor_scalar(out=st16[0][:], in0=st16[0][:], scalar1=1e-12, scalar2=None, op0=ALU.max), reads=["rw_st0"], writes=["rw_st0"])
        P.op("dve", lambda e: e.reciprocal(out=st16[0][:], in_=st16[0][:]), reads=["rw_st0"], writes=["rw_st0"])
        tt_bc(KK, KK, st16[0][:], "rw_st0", ALU.mult)
        P.op("dve", lambda e, TMP=TMP, AA=AA: e.scalar_tensor_tensor(out=TMP.ap, in0=AA.ap, scalar=-1.0, in1=KA_.ap, op0=ALU.add, op1=ALU.mult),
             reads=[AA.key, KA_.key], writes=[TMP.key])
        P.op("dve", lambda e, TMP=TMP, K=K, KP=KP: e.scalar_tensor_tensor(out=KP.ap, in0=TMP.ap, scalar=1.0, in1=K.ap, op0=ALU.add, op1=ALU.mult),
             reads=[TMP.key, K.key], writes=[KP.key])
        tt(BB, KK, AA, ALU.mult)
        put(K, AA)
        cps, cpk = nextpb(); tps, tpk = nextpb()
        for blk in range(2):
            sl = slice(blk * 512, (blk + 1) * 512)
            P.op("pe", lambda e, sl=sl, SW=SW, cps=cps: e.matmul(cps[:, sl], lhsT=MUI.t[:, 0:64], rhs=SW.t[:, sl], start=True, stop=True), reads=[MUI.key, SW.key], writes=[cpk])
            P.op("pe", lambda e, sl=sl, SW=SW, tps=tps: e.matmul(tps[:, sl], lhsT=sc.ones[:64, :64], rhs=SW.t[:, sl], start=True, stop=True), reads=["ones", SW.key], writes=[tpk])
        RHO = get(); AL = get(); BE = get(); KA = get(); BEP = get(); KAP = get(); DG = get(); T2 = get()
        actf(TMP, cps[:], cpk, AF.Exp, scale=C_DEC); tt(RHO, R, TMP, ALU.mult)
        tt_ps(T2, cps, cpk, SW, ALU.subtract); actf(TMP, T2.ap, T2.key, AF.Exp, scale=C_DEC); tt(AL, KK, TMP, ALU.mult)
        actf(TMP, cps[:], cpk, AF.Exp, scale=-C_DEC); tt(BE, BB, TMP, ALU.mult); tt(KA, KP, TMP, ALU.mult)
        P.op("dve", lambda e, T2=T2, tps=tps, cps=cps: e.tensor_copy(out=T2.ap, in_=cps[:]), reads=[cpk], writes=[T2.key])
        tt_ps(T2, tps, tpk, T2, ALU.subtract); actf(TMP, T2.ap, T2.key, AF.Exp, scale=C_DEC); tt(BEP, BB, TMP, ALU.mult); tt(KAP, KP, TMP, ALU.mult)
        actf(TMP, tps[:], tpk, AF.Exp, scale=C_DEC); tt(DG, IDN, TMP, ALU.mult)
        put(SW, KK, BB, T2, TMP)
        ALT = get(); BET = get(); KAT = get(); RHT = get()
        tr16(ALT, AL); tr16(BET, BE); tr16(KAT, KA); tr16(RHT, RHO)
        put(BE, KA, RHO)
        NN = get(); NT = get(); MAKT = get(); PRBT = get(); PRKT = get(); X = get(); W = get()
        ps, pk = mm16([(BET, ALT)]); tt_ps(NN, ps, pk, MSU, ALU.mult)
        ps, pk = mm16([(ALT, BET)]); tt_ps(NT, ps, pk, MSL, ALU.mult)
        ps, pk = mm16([(KAT, ALT)]); tt_ps(MAKT, ps, pk, MSU, ALU.mult)
        ps, pk = mm16([(BET, RHT)]); tt_ps(PRBT, ps, pk, MUI, ALU.mult)
        ps, pk = mm16([(KAT, RHT)]); tt_ps(PRKT, ps, pk, MUI, ALU.mult)
        put(ALT, BET, KAT)
        tt(X, IDN, NN, ALU.subtract); tt(W, IDN, NT, ALU.subtract, eng="pool")
        for k in range(5):
            last = (k == 4)
            psP, pkP = mm16([(NT, NN)])
            if not last:
                psQ, pkQ = mm16([(NN, NT)])
            cp_ps(NN, psP, pkP)
            if not last:
                cp_ps(NT, psQ, pkQ, eng="dve")
            psX, pkX = mm16([(W, NN)])
            if not last:
                psW, pkW = mm16([(NN, W)])
            tt_ps(X, psX, pkX, X, ALU.add)
            if not last:
                tt_ps(W, psW, pkW, W, ALU.add)
        put(NN, NT, W)
        MV = get(); AT = get(); NU0 = get(); GM = get(); RTT = get()
        ps, pk = mm16([(MAKT, V)]); cp_ps(MV, ps, pk)
        ps, pk = mm16([(X, AL)]); cp_ps(AT, ps, pk)
        ps, pk = mm16([(X, MV)]); cp_ps(NU0, ps, pk, scale=-1.0)
        put(MAKT, X, MV, AL)
        ps, pk = mm16([(AT, BEP)]); tt_ps(GM, ps, pk, DG, ALU.subtract)
        P.op("dve", lambda e, GM=GM: e.tensor_scalar(out=GM.ap, in0=GM.ap, scalar1=-1.0, scalar2=None, op0=ALU.mult), reads=[GM.key], writes=[GM.key])
        ps, pk = mm16([(AT, PRBT)]); tt_ps(RTT, ps, pk, RHT, ALU.subtract)
        P.op("dve", lambda e, RTT=RTT: e.tensor_scalar(out=RTT.ap, in0=RTT.ap, scalar1=-1.0, scalar2=None, op0=ALU.mult), reads=[RTT.key], writes=[RTT.key])
        put(AT, DG, RHT)
        psY, pkY = mm16([(RTT, ST), (PRKT, V), (PRBT, NU0)])
        psS, pkS = mm16([(GM, ST), (KAP, V), (BEP, NU0)])
        YS = get(); SQ = get()
        cp_ps(YS, psY, pkY)
        cp_ps(ST, psS, pkS, eng="dve")
        put(RTT, PRKT, PRBT, NU0, GM, KAP, BEP)
        rsum(st16[1][:], "rw_st1", YS)
        P.op("dve", lambda e: e.tensor_scalar(out=st16[1][:], in0=st16[1][:], scalar1=1.0 / 64, scalar2=None, op0=ALU.mult), reads=["rw_st1"], writes=["rw_st1"])
        tt_bc(YS, YS, st16[1][:], "rw_st1", ALU.subtract)
        tt(SQ, YS, YS, ALU.mult)
        rsum(st16[2][:], "rw_st2", SQ)
        P.op("dve", lambda e: e.tensor_scalar(out=st16[2][:], in0=st16[2][:], scalar1=1.0 / 64, scalar2=GN_EPS, op0=ALU.mult, op1=ALU.add), reads=["rw_st2"], writes=["rw_st2"])
        P.op("act", lambda e: e.activation(out=st16[2][:], in_=st16[2][:], func=AF.Sqrt), reads=["rw_st2"], writes=["rw_st2"])
        P.op("dve", lambda e: e.reciprocal(out=st16[2][:], in_=st16[2][:]), reads=["rw_st2"], writes=["rw_st2"])
        tt_bc(YS, YS, st16[2][:], "rw_st2", ALU.mult)
        tt(YS, YS, GNG, ALU.mult); tt(YS, YS, GNB, ALU.add)
        tt(SQ, R, KP, ALU.mult); tt(SQ, SQ, RK, ALU.mult)
        rsum(st16[3][:], "rw_st3", SQ)
        tt_bc(SQ, V, st16[3][:], "rw_st3", ALU.mult)
        tt(YS, YS, SQ, ALU.add); tt(YS, YS, GG, ALU.mult)
        P.dma("sp", o_b[t0:t0 + 64, :], YS.ap, reads=[YS.key], writes=["o_b"])
        put(YS, SQ, R, KP, V, GG)
        assert len(pool) == 25, len(pool)


def build_b0(T=4096, do_nsa=True, do_rwkv=True):
    nc = new_nc(); P = Prog(nc); sc = SeqCommon(P, nc, T, nps=3 if do_nsa else 2)
    if do_nsa:
        pos = sc.din("pos", [128, T], I32)
        C, S = sc.rope_tables(pos, T, "k")
        at = Attn(sc)
        emit_nsa(P, nc, sc, at, T, C, S)
    if do_rwkv:
        emit_rwkv(P, nc, sc, T)
    P.finish()
    return nc, P


def _rep(a, n=128):
    return np.ascontiguousarray(np.broadcast_to(a[None], (n,) + a.shape))


def _consts():
    return {"RT_d": const_RT(), "inv_d": const_inv(), "ident_d": np.eye(128, dtype=np.float32)}


def nsa_inputs(inp, pa, pos, r, T=SEQ):
    q = pa[:, :2048].reshape(T, 16, 128)
    sec = lambda i: pa[:, 2048 + 512 * i:2048 + 512 * (i + 1)].reshape(T, 4, 128)
    kc, vc, ks, vs, kw, vw = [sec(i) for i in range(6)]
    gates = pa[:, 5120:5168]
    fm = lambda a: np.ascontiguousarray(a[:, 2 * r:2 * r + 2].transpose(1, 2, 0))
    tm = lambda a: np.ascontiguousarray(a[:, 2 * r:2 * r + 2].transpose(1, 0, 2))
    cpos = np.zeros(256, np.int32); cpos[:255] = pos[16 * np.arange(255) + 31]
    d = {"n_qT": np.ascontiguousarray(q[:, 8 * r:8 * r + 8].transpose(1, 2, 0)), "n_kcT": fm(kc), "n_vcT": fm(vc), "n_ksT": fm(ks), "n_kwT": fm(kw),
         "n_vs": tm(vs), "n_vw": tm(vw), "n_gates": np.ascontiguousarray(gates[:, 24 * r:24 * r + 24]),
         "n_cpos": _rep(cpos),
         "n_q_norm": inp['nsa_q_norm'], "n_kc_norm": inp['nsa_kc_norm'], "n_ks_norm": inp['nsa_ks_norm'], "n_kw_norm": inp['nsa_kw_norm'],
         "n_peT_k": np.ascontiguousarray(inp['nsa_pe_k'].T), "n_peT_v": np.ascontiguousarray(inp['nsa_pe_v'].T),
         "n_w1_k": inp['nsa_ck_w1'], "n_w1_v": inp['nsa_cv_w1'], "n_w2_k": inp['nsa_ck_w2'], "n_w2_v": inp['nsa_cv_w2'],
         "pos": _rep(pos.astype(np.int32))}
    d.update(_consts())
    return d


_NSA_CONSTS = None
_RWKV_HOOK = None


def _launch(nc, in_maps):
    t0 = time.time()
    res = run_bass_kernel_spmd(nc, in_maps, core_ids=list(range(8)))
    print(f"[kernel] launch took {time.time() - t0:.1f}s", flush=True)
    return res.results


def kernel(**inp):
    global _NSA_CONSTS
    f32 = lambda a: np.ascontiguousarray(np.asarray(a, dtype=np.float32))
    inp = {k: np.asarray(v) for k, v in inp.items()}
    x = inp["x"]; mem = inp["mem"]; pos = inp["positions"].astype(np.int32)
    B, T, D = x.shape
    cores = [(c // 2, c % 2) for c in range(8)]
    tok = lambda r: slice(r * 2048, (r + 1) * 2048)
    nc, _ = build_token_local(False, True, EVEN_IN, EVEN_SECTIONS)
    res = _launch(nc, [{"xT": f32(x[b, tok(r)].T), "mix_norm": inp["l0_mix_norm"], "w_in": inp["l0_w_in"]} for b, r in cores])
    proj0 = np.empty((B, T, EVEN_IN), np.float32)
    for (b, r), o in zip(cores, res):
        proj0[b, tok(r)] = o["projT"].T
    if _NSA_CONSTS is None:
        _NSA_CONSTS = {"n_" + k: v for k, v in nsa_consts(T).items()}
    nc, _ = build_b0(T, do_nsa=True, do_rwkv=False)
    ins = []
    for b, r in cores:
        d = nsa_inputs(inp, proj0[b, :, :NSA_COLS], pos[b], r, T); d.update(_NSA_CONSTS); ins.append(d)
    res = _launch(nc, ins)
    omix = np.zeros((B, T, 4096), np.float32)
    for (b, r), o in zip(cores, res):
        omix[b, :, r * 1024:(r + 1) * 1024] = o["o_a"]
    if _RWKV_HOOK is not None:
        for b in range(B):
            omix[b, :, 2048:] = _RWKV_HOOK(b)
    else:
        nc, _ = build_b0(T, do_nsa=False, do_rwkv=True)
        res = _launch(nc, [rwkv_inputs(inp, proj0[b, :, NSA_COLS:], r, T) for b, r in cores])
        for (b, r), o in zip(cores, res):
            omix[b, :, 2048 + r * 1024:2048 + (r + 1) * 1024] = o["o_b"]
    def c_inputs(l, xT, oT, b):
        return {"xT": xT, "oT": oT, "w_out": inp[f"l{l}_w_out"], "memT": f32(mem[b].T), "mem_norm": inp["mem_norm"], "mem_w_kv": inp["mem_w_kv"],
                "mem_k_norm": inp["mem_k_norm"], "xattn_norm": inp[f"l{l}_xattn_norm"], "wq": inp[f"l{l}_mem_wq"], "q_norm": inp[f"l{l}_mem_q_norm"],
                "wo": inp[f"l{l}_mem_wo"], "ffn_norm": inp[f"l{l}_ffn_norm"], "w1": inp[f"l{l}_w1"], "w3": inp[f"l{l}_w3"], "w2": inp[f"l{l}_w2"],
                "ident_d": np.eye(128, dtype=np.float32)}
    nc, _ = build_token_local(True, True, ODD_IN, ODD_SECTIONS)
    ins = []
    for b, r in cores:
        d = c_inputs(0, f32(x[b, tok(r)].T), f32(omix[b, tok(r)].T), b)
        d.update({"mix_norm": inp["l1_mix_norm"], "w_in": inp["l1_w_in"]}); ins.append(d)
    res = _launch(nc, ins)
    x3T = [o["x3T"] for o in res]
    proj1 = np.empty((B, T, ODD_IN), np.float32)
    for (b, r), o in zip(cores, res):
        proj1[b, tok(r)] = o["projT"].T
    nc, _ = build_dsa_index()
    ins = []
    for b, r in cores:
        po = proj1[b]; tq = np.arange(r, T, 2)
        qi = po[:, 5120:9216].reshape(T, 32, 128)
        d = {"qiT": np.ascontiguousarray(qi[tq].transpose(1, 2, 0)), "wi": np.ascontiguousarray(po[tq, 9344:9376]), "posq": _rep(pos[b][tq]), "posk": _rep(pos[b]),
             "kiT": np.ascontiguousarray(po[:, 9216:9344].T), "ki_norm": inp["dsa_ki_norm"], "cbias": dsa_index_consts(r)}
        d.update(_consts()); ins.append(d)
    res = _launch(nc, ins)
    bias = np.empty((B, T, T), np.float32)
    for (b, r), o in zip(cores, res):
        bias[b, r::2] = o["biasQ"]
    nc, _ = build_dsa_attn(T)
    ins = []
    for b, r in cores:
        po = proj1[b]
        q = po[:, :4096].reshape(T, 32, 128); k = po[:, 4096:4608].reshape(T, 4, 128); v = po[:, 4608:5120].reshape(T, 4, 128)
        d = {"qT": np.ascontiguousarray(q[:, 16 * r:16 * r + 16].transpose(1, 2, 0)), "kT": np.ascontiguousarray(k[:, 2 * r:2 * r + 2].transpose(1, 2, 0)),
             "v": np.ascontiguousarray(v[:, 2 * r:2 * r + 2].transpose(1, 0, 2)), "biasT": np.ascontiguousarray(bias[b].T), "pos": _rep(pos[b]),
             "q_norm": inp["dsa_q_norm"], "k_norm": inp["dsa_k_norm"]}
        d.update(_consts()); ins.append(d)
    res = _launch(nc, ins)
    od = np.empty((B, T, 4096), np.float32)
    for (b, r), o in zip(cores, res):
        od[b, :, r * 2048:(r + 1) * 2048] = o["o"]
    nc, _ = build_token_local(True, False, 0, None)
    res = _launch(nc, [c_inputs(1, x3T[c], f32(od[b, tok(r)].T), b) for c, (b, r) in enumerate(cores)])
    out = np.empty((B, T, D), np.float32)
    for (b, r), o in zip(cores, res):
        out[b, tok(r)] = o["x3T"].T
    return out
```
